# Optimizing a Trainium2 kernel written in Bass

```python
import jax
import jax.numpy as jnp
from jax import lax
import numpy as np


D_MODEL = 2048
BATCH = 1
SEQ = 16384
DEPTH = 4

GRID_W = 64
CTX_LEN = 256
HEAD_DIM = 128
CONV_W = D_MODEL // 4
FOURIER_GROUPS = D_MODEL // 512
FOURIER_W = FOURIER_GROUPS * HEAD_DIM
NA_HEADS = D_MODEL // 512
NA_W = NA_HEADS * HEAD_DIM
NA_ROWS = 8
NA_COLS = 16
GQA_Q_HEADS = D_MODEL // 256
GQA_KV_HEADS = GQA_Q_HEADS // 4
GQA_GROUP = GQA_Q_HEADS // GQA_KV_HEADS
GQA_W = GQA_Q_HEADS * HEAD_DIM
GQA_KV_W = GQA_KV_HEADS * HEAD_DIM
Q_BLOCK = 128
N_BRANCH = 4
_SA = 3 * CONV_W
_SF = _SA + FOURIER_W
_SN = _SF + 3 * NA_W
_SQ = _SN + GQA_W
_SK = _SQ + GQA_KV_W
IN_SPLITS = (_SA, _SF, _SN, _SQ, _SK)
N_IN = _SK + GQA_KV_W
D_FF = 11 * D_MODEL // 4
ROPE_THETA = 10000.0
EPS = 1e-6

kernel_name = 'hybrid_parallel_conv_fourier_natten_gqa_dit'


def rms_norm(x, gain):
    xf = x.astype(jnp.float32)
    y = xf * lax.rsqrt(jnp.mean(xf * xf, axis=-1, keepdims=True) + EPS)
    return y.astype(x.dtype) * gain


def modulate(x, shift, scale):
    return x * (1.0 + scale) + shift


def ada_mod(cvec, w, b):
    return jnp.split(jax.nn.silu(cvec) @ w + b, 6, axis=-1)


def dwconv3(x, w):
    xp = jnp.pad(x, ((0, 0), (1, 1), (0, 0)))
    return xp[:, :-2] * w[0] + xp[:, 1:-1] * w[1] + xp[:, 2:] * w[2]


def to_heads(z, n_heads):
    return z.reshape(z.shape[0], z.shape[1], n_heads, HEAD_DIM)


def qk_norm(z, n_heads, gain):
    return rms_norm(to_heads(z, n_heads), gain)


def axial_rope(n_tok, dtype):
    t = jnp.arange(n_tok)
    row = (t // GRID_W).astype(jnp.float32)
    col = (t % GRID_W).astype(jnp.float32)
    n_freq = HEAD_DIM // 4
    inv_freq = ROPE_THETA ** (-jnp.arange(n_freq, dtype=jnp.float32) / n_freq)
    ang = jnp.concatenate([row[:, None] * inv_freq, col[:, None] * inv_freq], axis=-1)
    return jnp.cos(ang).astype(dtype)[None, :, None, :], jnp.sin(ang).astype(dtype)[None, :, None, :]


def apply_rope(x, cos, sin):
    x1, x2 = jnp.split(x, 2, axis=-1)
    return jnp.concatenate([x1 * cos - x2 * sin, x1 * sin + x2 * cos], axis=-1)


def short_conv_mixer(z, w):
    xa, bg, cg = jnp.split(z, 3, axis=-1)
    return bg * dwconv3(cg * xa, w)


def fourier_mixer(z):
    b, t, _ = z.shape
    g = z.reshape(b, t, FOURIER_GROUPS, HEAD_DIM).astype(jnp.float32)
    f = jnp.fft.fftn(g, axes=(1, 3), norm='ortho').real
    return f.reshape(b, t, FOURIER_W).astype(z.dtype)


def dense_attention(q, k, v):
    b, l, hq, dh = q.shape
    hkv = k.shape[2]
    qg = q.reshape(b, l, hkv, hq // hkv, dh)
    s = jnp.einsum('bqhgd,bkhd->bhgqk', qg, k).astype(jnp.float32) * (dh ** -0.5)
    p = jax.nn.softmax(s, axis=-1).astype(v.dtype)
    return jnp.einsum('bhgqk,bkhd->bqhgd', p, v).reshape(b, l, hq * dh)


def gqa_latent_attention(q, k, v, kc, vc):
    b, s_len, hq, dh = q.shape
    k_all = jnp.concatenate([k, kc], axis=1)
    v_all = jnp.concatenate([v, vc], axis=1)
    n_blk = s_len // Q_BLOCK
    qb = q.reshape(b, n_blk, Q_BLOCK, GQA_KV_HEADS, GQA_GROUP, dh).transpose(1, 0, 2, 3, 4, 5)
    scale = dh ** -0.5

    def block(q_blk):
        s = jnp.einsum('bqhgd,bkhd->bhgqk', q_blk, k_all).astype(jnp.float32) * scale
        p = jax.nn.softmax(s, axis=-1).astype(v_all.dtype)
        return jnp.einsum('bhgqk,bkhd->bqhgd', p, v_all)

    o = lax.map(block, qb)
    return o.transpose(1, 0, 2, 3, 4, 5).reshape(b, s_len, hq * dh)


def neighborhood_attention(q, k, v, kc, vc, rpb):
    b, s_len, h, dh = q.shape
    rows = s_len // GRID_W
    kr = min(NA_ROWS, rows)
    qg = q.reshape(b, rows, GRID_W, h, dh)
    kg = k.reshape(b, rows, GRID_W, h, dh)
    vg = v.reshape(b, rows, GRID_W, h, dh)
    col = jnp.arange(GRID_W)
    col_idx = jnp.clip(col - NA_COLS // 2, 0, GRID_W - NA_COLS)[:, None] + jnp.arange(NA_COLS)[None, :]
    dc_idx = col_idx - col[:, None] + NA_COLS - 1
    n_nb = kr * NA_COLS
    scale = dh ** -0.5

    def row_block(r):
        r0 = jnp.clip(r - kr // 2, 0, rows - kr)
        k_nb = lax.dynamic_slice_in_dim(kg, r0, kr, axis=1)[:, :, col_idx]
        v_nb = lax.dynamic_slice_in_dim(vg, r0, kr, axis=1)[:, :, col_idx]
        q_r = lax.dynamic_index_in_dim(qg, r, axis=1, keepdims=False)
        dr_idx = r0 + jnp.arange(kr) - r + NA_ROWS - 1
        bias = rpb[:, dr_idx][:, :, dc_idx].transpose(0, 2, 1, 3)
        s_nb = jnp.einsum('bqhd,brqchd->bhqrc', q_r, k_nb).astype(jnp.float32) * scale + bias.astype(jnp.float32)
        s_ctx = jnp.einsum('bqhd,bkhd->bhqk', q_r, kc).astype(jnp.float32) * scale
        s = jnp.concatenate([s_nb.reshape(b, h, GRID_W, n_nb), s_ctx], axis=-1)
        p = jax.nn.softmax(s, axis=-1).astype(v.dtype)
        p_nb = p[..., :n_nb].reshape(b, h, GRID_W, kr, NA_COLS)
        p_ctx = p[..., n_nb:]
        return (jnp.einsum('bhqrc,brqchd->bqhd', p_nb, v_nb)
                + jnp.einsum('bhqk,bkhd->bqhd', p_ctx, vc))

    o = lax.map(row_block, jnp.arange(rows))
    return o.transpose(1, 0, 2, 3, 4).reshape(b, s_len, h * dh)


def merge_branches(xn, ys, w_outs, w_gate, b_gate, w_o):
    g = jnp.split(jax.nn.sigmoid(xn @ w_gate + b_gate), N_BRANCH, axis=-1)
    merged = (g[0] * (ys[0] @ w_outs[0]) + g[1] * (ys[1] @ w_outs[1])
              + g[2] * (ys[2] @ w_outs[2]) + g[3] * (ys[3] @ w_outs[3]))
    return merged @ w_o


def conv_ffn(xn, w_up, conv_w, w_down):
    u = dwconv3(xn @ w_up, conv_w)
    a, gate = jnp.split(u, 2, axis=-1)
    return (jax.nn.silu(a) * gate) @ w_down


def setup_inputs(seed: int = 0) -> dict:
    key = jax.random.key(seed)
    keys = jax.random.split(key, 32)
    counter = [0]

    def nrm(shape, scale):
        k = keys[counter[0]]
        counter[0] += 1
        return jax.random.normal(k, shape, jnp.float32) * scale

    def gain(shape):
        return 1.0 + nrm(shape, 0.02)

    L = DEPTH
    D = D_MODEL
    return {
        'x': nrm((BATCH, SEQ, D), 1.0),
        'c': nrm((BATCH, D), 1.0),
        'ctx': nrm((BATCH, CTX_LEN, D), 1.0),
        'c_ctx': nrm((D,), 1.0),
        'w_ada': nrm((L, D, 6 * D), 0.5 * D ** -0.5),
        'b_ada': nrm((L, 6 * D), 0.01),
        'norm1': gain((L, D)),
        'w_in': nrm((L, D, N_IN), D ** -0.5),
        'conv_w': nrm((L, 3, CONV_W), 3 ** -0.5),
        'na_q_gain': gain((L, HEAD_DIM)),
        'na_k_gain': gain((L, HEAD_DIM)),
        'na_rpb': nrm((L, NA_HEADS, 2 * NA_ROWS - 1, 2 * NA_COLS - 1), 0.1),
        'gqa_q_gain': gain((L, HEAD_DIM)),
        'gqa_k_gain': gain((L, HEAD_DIM)),
        'w_conv_out': nrm((L, CONV_W, D), CONV_W ** -0.5),
        'w_fourier_out': nrm((L, FOURIER_W, D), FOURIER_W ** -0.5),
        'w_na_out': nrm((L, NA_W, D), NA_W ** -0.5),
        'w_gqa_out': nrm((L, GQA_W, D), GQA_W ** -0.5),
        'w_gate': nrm((L, D, N_BRANCH * D), D ** -0.5),
        'b_gate': nrm((L, N_BRANCH * D), 0.01),
        'w_o': nrm((L, D, D), D ** -0.5),
        'norm2': gain((L, D)),
        'w_up': nrm((L, D, 2 * D_FF), D ** -0.5),
        'ffn_conv_w': nrm((L, 3, 2 * D_FF), 3 ** -0.5),
        'w_down': nrm((L, D_FF, D), D_FF ** -0.5),
    }


def reference(x, c, ctx, c_ctx, w_ada, b_ada, norm1, w_in, conv_w, na_q_gain, na_k_gain, na_rpb,
              gqa_q_gain, gqa_k_gain, w_conv_out, w_fourier_out, w_na_out, w_gqa_out, w_gate, b_gate,
              w_o, norm2, w_up, ffn_conv_w, w_down):
    h = x
    hc = ctx
    cos, sin = axial_rope(x.shape[1], x.dtype)
    c_lat = c[:, None, :]
    for i in range(DEPTH):
        last = i == DEPTH - 1
        sh1, sc1, g1, sh2, sc2, g2 = ada_mod(c_lat, w_ada[i], b_ada[i])
        csh1, csc1, cg1, csh2, csc2, cg2 = ada_mod(c_ctx, w_ada[i], b_ada[i])
        xn = modulate(rms_norm(h, norm1[i]), sh1, sc1)
        xcn = modulate(rms_norm(hc, norm1[i]), csh1, csc1)
        za, zf, zn, zq, zk, zv = jnp.split(xn @ w_in[i], IN_SPLITS, axis=-1)
        cza, czf, czn, czq, czk, czv = jnp.split(xcn @ w_in[i], IN_SPLITS, axis=-1)
        nq_raw, nk_raw, nv_raw = jnp.split(zn, 3, axis=-1)
        cnq_raw, cnk_raw, cnv_raw = jnp.split(czn, 3, axis=-1)
        nq = qk_norm(nq_raw, NA_HEADS, na_q_gain[i])
        nk = qk_norm(nk_raw, NA_HEADS, na_k_gain[i])
        nv = to_heads(nv_raw, NA_HEADS)
        cnk = qk_norm(cnk_raw, NA_HEADS, na_k_gain[i])
        cnv = to_heads(cnv_raw, NA_HEADS)
        gq = apply_rope(qk_norm(zq, GQA_Q_HEADS, gqa_q_gain[i]), cos, sin)
        gk = apply_rope(qk_norm(zk, GQA_KV_HEADS, gqa_k_gain[i]), cos, sin)
        gv = to_heads(zv, GQA_KV_HEADS)
        ck = qk_norm(czk, GQA_KV_HEADS, gqa_k_gain[i])
        cv = to_heads(czv, GQA_KV_HEADS)
        merge_w = (w_conv_out[i], w_fourier_out[i], w_na_out[i], w_gqa_out[i])
        ys = (short_conv_mixer(za, conv_w[i]),
              fourier_mixer(zf),
              neighborhood_attention(nq, nk, nv, cnk, cnv, na_rpb[i]),
              gqa_latent_attention(gq, gk, gv, ck, cv))
        h = h + g1 * merge_branches(xn, ys, merge_w, w_gate[i], b_gate[i], w_o[i])
        h = h + g2 * conv_ffn(modulate(rms_norm(h, norm2[i]), sh2, sc2), w_up[i], ffn_conv_w[i], w_down[i])
        if not last:
            cnq = qk_norm(cnq_raw, NA_HEADS, na_q_gain[i])
            cq = qk_norm(czq, GQA_Q_HEADS, gqa_q_gain[i])
            cys = (short_conv_mixer(cza, conv_w[i]),
                   fourier_mixer(czf),
                   dense_attention(cnq, cnk, cnv),
                   dense_attention(cq, ck, cv))
            hc = hc + cg1 * merge_branches(xcn, cys, merge_w, w_gate[i], b_gate[i], w_o[i])
            hc = hc + cg2 * conv_ffn(modulate(rms_norm(hc, norm2[i]), csh2, csc2), w_up[i], ffn_conv_w[i], w_down[i])
    return h
```

```python
import contextlib
import numpy as np
import math
import ml_dtypes
import concourse.bass as bass
import concourse.mybir as mybir
from concourse.bass_utils import run_bass_kernel_spmd

F32 = mybir.dt.float32
BF16 = mybir.dt.bfloat16
ALU = mybir.AluOpType
AF = mybir.ActivationFunctionType
AX = mybir.AxisListType


class Res:
    __slots__ = ("w", "r", "name")

    def __init__(self, name=""):
        self.w = None
        self.r = {}
        self.name = name


class KB:
    def __init__(self, nc, n_dma_sems=6, same_engine_sync=True):
        self.nc = nc
        self.es = contextlib.ExitStack()
        self.engs = {"pe": nc.tensor, "act": nc.scalar, "dve": nc.vector,
                     "pool": nc.gpsimd, "sp": nc.sync}
        self.semh = {}
        for k in self.engs:
            self.semh[k] = self.es.enter_context(nc.semaphore("s_" + k))
        self.cnt = {k: 0 for k in self.engs}
        self.seen = {k: {} for k in self.engs}
        self.same = same_engine_sync
        self.dq = {}
        for q in ("sp", "pool", "act"):
            sl = []
            for i in range(n_dma_sems):
                key = ("d", q, i)
                self.semh[key] = self.es.enter_context(nc.semaphore("d_%s%d" % (q, i)))
                sl.append(key)
            self.dq[q] = {"keys": sl, "uses": [0] * n_dma_sems, "next": 0}
        self.n_ins = 0

    def close(self):
        self.es.close()

    def sb(self, name, shape, dt):
        return self.es.enter_context(self.nc.sbuf_tensor("sb_" + name, list(shape), dt))

    def ps(self, name, shape, dt=F32):
        return self.es.enter_context(self.nc.psum_tensor("pp_" + name, list(shape), dt))

    def _wait(self, eng, deps):
        e = self.engs[eng]
        seen = self.seen[eng]
        for sk, v in deps:
            if sk == eng and (not self.same or eng == "pe"):
                continue
            if seen.get(sk, 0) >= v:
                continue
            e.wait_ge(self.semh[sk], v)
            seen[sk] = v

    def _deps(self, reads, writes):
        deps = {}
        for r in reads:
            if r.w is not None:
                sk, v = r.w
                if deps.get(sk, 0) < v:
                    deps[sk] = v
        for w in writes:
            if w.w is not None:
                sk, v = w.w
                if deps.get(sk, 0) < v:
                    deps[sk] = v
            for sk, v in w.r.items():
                if deps.get(sk, 0) < v:
                    deps[sk] = v
        return deps.items()

    def _mark(self, ev, reads, writes):
        sk, v = ev
        for r in reads:
            if r.r.get(sk, 0) < v:
                r.r[sk] = v
        for w in writes:
            w.w = ev
            w.r = {}

    def op(self, eng, fn, reads=(), writes=()):
        self._wait(eng, self._deps(reads, writes))
        ins = fn(self.engs[eng])
        self.cnt[eng] += 1
        n = self.cnt[eng]
        ins.then_inc(self.semh[eng], 1)
        self._mark((eng, n), reads, writes)
        self.n_ins += 1
        return ins

    def pe_group(self, fns, reads=(), writes=()):
        self._wait("pe", self._deps(reads, writes))
        ins = None
        for fn in fns:
            ins = fn(self.nc.tensor)
        self.cnt["pe"] += 1
        n = self.cnt["pe"]
        ins.then_inc(self.semh["pe"], 1)
        self._mark(("pe", n), reads, writes)
        self.n_ins += len(fns)

    def dma(self, q, out, in_, reads=(), writes=(), **kw):
        d = self.dq[q]
        i = d["next"]
        d["next"] = (i + 1) % len(d["keys"])
        key = d["keys"][i]
        deps = dict(self._deps(reads, writes))
        if d["uses"][i] > 0:
            deps[key] = 16 * d["uses"][i]
        self._wait(q, deps.items())
        ins = self.engs[q].dma_start(out=out, in_=in_, **kw)
        d["uses"][i] += 1
        v = 16 * d["uses"][i]
        ins.then_inc(self.semh[key], 16)
        self._mark((key, v), reads, writes)
        self.n_ins += 1
        return (key, v)

    def wait_all(self, eng, ress):
        deps = {}
        for r in ress:
            if r.w is not None:
                sk, v = r.w
                if deps.get(sk, 0) < v:
                    deps[sk] = v
        self._wait(eng, deps.items())


def _kb_collective(self, ins_ap, outs_ap, reads=(), writes=()):
    if "cc" not in self.semh:
        self.semh["cc"] = self.es.enter_context(self.nc.semaphore("s_cc"))
        self.ncc = 0
    self._wait("pool", self._deps(reads, writes))
    ins = self.nc.gpsimd.collective_compute(
        "AllGather", ALU.bypass, replica_groups=[list(range(8))], ins=[ins_ap], outs=[outs_ap])
    self.ncc += 1
    ins.then_inc(self.semh["cc"], 1)
    self._mark(("cc", self.ncc), reads, writes)


def _kb_barrier(self):
    targets = []
    for k in self.engs:
        if self.cnt[k] > 0:
            targets.append((k, self.cnt[k]))
    for q, d in self.dq.items():
        for key, u in zip(d["keys"], d["uses"]):
            if u > 0:
                targets.append((key, 16 * u))
    if "cc" in self.semh and self.ncc > 0:
        targets.append(("cc", self.ncc))
    for e in self.engs:
        self._wait(e, [(sk, v) for sk, v in targets])


KB.collective = _kb_collective
KB.barrier = _kb_barrier


BF = ml_dtypes.bfloat16
NCORE = 8
D = 2048
KC = 16
CTX = 256
GRID_W = 64
EPS = 1e-6
SCALE = 128 ** -0.5
NEG = -30000.0


def split_cols(n, mx=512):
    k = (n + mx - 1) // mx
    base, rem = n // k, n % k
    out, o = [], 0
    for i in range(k):
        s = base + (1 if i < rem else 0)
        out.append((o, s))
        o += s
    return out


class Cfg:
    def __init__(self, S, DEPTH):
        self.S, self.L = S, DEPTH
        self.TL = S // NCORE
        self.TT = self.TL + CTX
        self.GL = self.TL // 2
        self.GN = self.GL + CTX
        self.TLR = self.TL // GRID_W
        self.GR = self.GL // GRID_W
        self.ROWS = S // GRID_W
        self.T1 = S // 128
        self.NKEY = S + CTX
        self.NCH = self.NKEY // 128
        self.NW = (self.TLR + 7) * 64


class Seg:
    def __init__(self, lo, tt, n, is_ctx):
        self.lo, self.tt, self.n, self.is_ctx = lo, tt, n, is_ctx

    def tiles(self):
        return [(self.lo + o, self.tt + o, s) for (o, s) in split_cols(self.n)]


def groups(cfg):
    return [[Seg(0, 0, cfg.GL, False), Seg(cfg.GL, cfg.TL, CTX, True)], [Seg(0, cfg.GL, cfg.GL, False)]]


def na_row_specs(cfg):
    TLR = cfg.TLR
    slots = [(0, None, None, 8)]
    rows = {}
    off = 8 * 64
    for lr in range(TLR):
        if lr < 4:
            k0, nk = lr - 4, 12 - lr
        elif lr >= TLR - 3:
            k0, nk = TLR - 8, lr + 4 - (TLR - 8)
        else:
            rows[lr] = (lr - 4, 8, 0)
            continue
        slots.append((len(slots), lr, k0, nk))
        rows[lr] = (k0, nk, off)
        off += nk * 64
    return {"slots": slots, "rows": rows, "ncols": off, "mid_lr": 4}


def MM(out, lhsT, rhs, start, stop):
    return lambda pe: pe.matmul(out, lhsT, rhs, start=start, stop=stop)


class LB:
    def __init__(self):
        self.nc = bass.Bass("TRN2", target_bir_lowering=False)
        self.kb = KB(self.nc)
        self.res = {}
        self.uid = 0
        self.outs = []
        self.bank = 0
        kb = self.kb
        self.ones_bf = kb.sb("ones_bf", [128, 128], BF16)
        self.ps_all = kb.ps("ps_all", [128, 4096], F32)
        self.PS = [self.ps_all[:, i * 512:(i + 1) * 512] for i in range(8)]
        self.RPS = [self.R("ps", i) for i in range(8)]
        self.rc = self.R("consts")
        kb.op("dve", lambda e: e.memset(self.ones_bf[:], 1.0), writes=[self.rc])
        self.epsb = kb.sb("epsb", [128, 1], F32)
        kb.op("dve", lambda e: e.memset(self.epsb[:], EPS), writes=[self.rc])

    def R(self, *key):
        r = self.res.get(key)
        if r is None:
            r = Res(str(key))
            self.res[key] = r
        return r

    def din(self, name, shape, dt=F32):
        return self.nc.dram_tensor(name, list(shape), dt, kind="ExternalInput").ap()

    def dout(self, name, shape, dt=F32):
        self.outs.append(name)
        return self.nc.dram_tensor(name, list(shape), dt, kind="ExternalOutput").ap()

    def dscr(self, name, shape, dt):
        return self.nc.dram_tensor(name, list(shape), dt, kind="Internal").ap()

    def tmp(self, name, shape, dt):
        self.uid += 1
        return self.nc.sbuf_tensor("t%d_%s" % (self.uid, name), list(shape), dt)

    def const(self, name, src, shape, dt=F32):
        t = self.kb.sb(name, shape, dt)
        self.kb.dma("sp", t[:], src, writes=[self.rc])
        return t

    def nb(self, lo=0, n=4):
        b = lo + self.bank % n
        self.bank += 1
        return b

    def finish(self):
        self.kb.barrier()
        self.kb.close()
        return self


def evac(lb, b, n, dst, rdst, idx, scale=None):
    kb = lb.kb
    if idx % 2 == 0:
        if scale is None:
            kb.op("act", lambda e: e.copy(dst, lb.PS[b][:, 0:n]), reads=[lb.RPS[b]], writes=[rdst])
        else:
            kb.op("act", lambda e: e.mul(dst, lb.PS[b][:, 0:n], scale), reads=[lb.RPS[b]], writes=[rdst])
    else:
        if scale is None:
            kb.op("dve", lambda e: e.tensor_copy(dst, lb.PS[b][:, 0:n]), reads=[lb.RPS[b]], writes=[rdst])
        else:
            kb.op("dve", lambda e: e.tensor_scalar_mul(dst, lb.PS[b][:, 0:n], scale), reads=[lb.RPS[b]], writes=[rdst])


def sweep(lb, st, xin, xres_fn, kcin, W, blocks, bw, tiles, epi, wname, m_of=None):
    kb = lb.kb
    wb = [st.enter_context(lb.tmp(wname + str(i), [128, kcin, bw], BF16)) for i in range(2)]
    for bi, (c0, js) in enumerate(blocks):
        i = bi % 2
        rw = lb.R(wname, i)
        kb.dma("pool", wb[i][:], W[:, c0:c0 + bw].rearrange("(k p) m -> p k m", p=128), writes=[rw])
        for (lo, tt, n) in tiles:
            for j in js:
                b = lb.nb(0, 4)
                fns = [MM(lb.PS[b][:, 0:n], wb[i][:, k, j * 128:(j + 1) * 128], xin[:, k, lo:lo + n],
                          k == 0, k == kcin - 1) for k in range(kcin)]
                kb.pe_group(fns, reads=[rw, xres_fn(lo)], writes=[lb.RPS[b]])
                epi((c0 + j * 128) // 128, lo, tt, n, b)


def norm_mod(lb, st, src, segs, xn, A, modsb, sh0, dstT, tag):
    kb, PS, RPS, R = lb.kb, lb.PS, lb.RPS, lb.R
    hb = [st.enter_context(lb.tmp("hb%d" % i, [128, KC, 512], F32)) for i in range(2)]
    sq = [st.enter_context(lb.tmp("sq%d" % i, [128, KC, 512], BF16)) for i in range(2)]
    rstd = [st.enter_context(lb.tmp("rstd%d" % i, [128, 512], F32)) for i in range(2)]
    tf = [st.enter_context(lb.tmp("tf%d" % i, [128, 512], F32)) for i in range(2)]
    ti = 0
    for sg in segs:
        v = 1 if sg.is_ctx else 0
        for (lo, tt, n) in sg.tiles():
            i = ti % 2
            ti += 1
            rh, rs, rr = R(tag, "hb", i), R(tag, "sq", i), R(tag, "rstd", i)
            kb.dma("sp", hb[i][:, :, 0:n], src[:, :, tt:tt + n].rearrange("k p t -> p k t"),
                   reads=[R(tag, "src", tt)], writes=[rh])
            kb.op("act", lambda e, i=i, n=n: e.activation(sq[i][:, :, 0:n], hb[i][:, :, 0:n], AF.Square),
                  reads=[rh], writes=[rs])
            b = 4 + i
            kb.pe_group([MM(PS[b][:, 0:n], lb.ones_bf[:], sq[i][:, k, 0:n], k == 0, k == KC - 1) for k in range(KC)],
                        reads=[rs, lb.rc], writes=[RPS[b]])
            kb.op("act", lambda e, i=i, n=n, b=b: e.activation(rstd[i][:, 0:n], PS[b][:, 0:n], AF.Sqrt, bias=lb.epsb[:, 0:1],
                                                               scale=1.0 / D), reads=[RPS[b], lb.rc], writes=[rr])
            kb.op("dve", lambda e, i=i, n=n: e.reciprocal(rstd[i][:, 0:n], rstd[i][:, 0:n]), reads=[rr], writes=[rr])
            rx = R(tag, "xn", lo)
            for k in range(KC):
                j = k % 2
                rt = R(tag, "tf", j)
                kb.op("dve", lambda e, i=i, n=n, k=k, j=j, v=v: e.scalar_tensor_tensor(
                    tf[j][:, 0:n], hb[i][:, k, 0:n], A[:, k, v:v + 1], rstd[i][:, 0:n], ALU.mult, ALU.mult),
                    reads=[rh, rr, lb.rc], writes=[rt])
                kb.op("act", lambda e, n=n, k=k, j=j, v=v, lo=lo: e.activation(
                    xn[:, k, lo:lo + n], tf[j][:, 0:n], AF.Identity, bias=modsb[:, sh0 + k, v:v + 1], scale=1.0),
                    reads=[rt, lb.rc], writes=[rx])
            if dstT is not None:
                kb.dma("sp", dstT[:, :, tt:tt + n].rearrange("k p t -> p k t"), xn[:, :, lo:lo + n],
                       reads=[rx], writes=[R(tag, "dst", tt)])


def load_mods(lb, mods_in, norm_in, gc_scale):
    kb = lb.kb
    modsb = lb.const("modsb", mods_in, [128, 96, 2])
    nrm = lb.const("nrm", norm_in, [128, KC])
    A = kb.sb("Amod", [128, KC, 2], F32)
    for v in range(2):
        kb.op("dve", lambda e, v=v: e.tensor_scalar(A[:, :, v], modsb[:, gc_scale:gc_scale + KC, v], 1.0, None, ALU.add),
              reads=[lb.rc], writes=[lb.rc])
        kb.op("dve", lambda e, v=v: e.tensor_tensor(A[:, :, v], A[:, :, v], nrm[:], ALU.mult),
              reads=[lb.rc], writes=[lb.rc])
    return modsb, A


def build_M(cfg):
    lb = LB()
    nc, kb, R, PS, RPS = lb.nc, lb.kb, lb.R, lb.PS, lb.RPS
    L = cfg.L
    cvec_in = lb.din("cvec", [128, KC, 2])
    wada_in = lb.din("w_ada", [L, D, 1536])
    bada_in = lb.din("b_ada", [128, L, 12])
    mout = lb.dout("modloc", [128, L * 24])
    with contextlib.ExitStack() as st:
        csil = st.enter_context(lb.tmp("csil", [128, KC, 2], F32))
        craw = st.enter_context(lb.tmp("craw", [128, KC, 2], F32))
        bada = st.enter_context(lb.tmp("bada", [128, L, 12], F32))
        mloc = st.enter_context(lb.tmp("mloc", [128, L, 12, 2], F32))
        wa = [st.enter_context(lb.tmp("wa%d" % i, [128, KC, 512], F32)) for i in range(2)]
        rcs = R("csil")
        kb.dma("sp", craw[:], cvec_in, writes=[rcs])
        kb.dma("sp", bada[:], bada_in, writes=[rcs])
        kb.op("act", lambda e: e.activation(csil[:], craw[:], AF.Silu), reads=[rcs], writes=[rcs])
        it = 0
        for l in range(L):
            for cb in range(3):
                i = it % 2
                it += 1
                rw = R("wa", i)
                kb.dma("sp", wa[i][:], wada_in[l, :, cb * 512:(cb + 1) * 512].rearrange("(k p) m -> p k m", p=128),
                       writes=[rw])
                b = 4 + i
                fns = []
                for j in range(4):
                    for k in range(KC):
                        fns.append(MM(PS[b][:, j * 2:(j + 1) * 2], wa[i][:, k, j * 128:(j + 1) * 128], csil[:, k, :],
                                      k == 0, k == KC - 1))
                kb.pe_group(fns, reads=[rw, rcs], writes=[RPS[b]])
                kb.op("dve", lambda e, l=l, cb=cb, b=b: e.tensor_tensor(
                    mloc[:, l, cb * 4:(cb + 1) * 4, :], PS[b][:, 0:8].rearrange("p (j v) -> p j v", v=2),
                    bada[:, l, cb * 4:(cb + 1) * 4].unsqueeze(2).to_broadcast([128, 4, 2]), ALU.add),
                    reads=[RPS[b], rcs], writes=[R("mloc")])
        kb.dma("sp", mout, mloc[:].rearrange("p l j v -> p (l j v)"), reads=[R("mloc")], writes=[R("mout")])
        kb.barrier()
    return lb.finish()


def build_A(cfg):
    lb = LB()
    nc, kb, R, PS, RPS = lb.nc, lb.kb, lb.R, lb.PS, lb.RPS
    TL, TT, GL, GN = cfg.TL, cfg.TT, cfg.GL, cfg.GN
    hT = lb.din("hT", [KC, 128, TT])
    mods_in = lb.din("mods", [128, 96, 2])
    norm1_in = lb.din("norm1", [128, KC])
    w_in = lb.din("w_in", [D, 5120])
    gains_in = lb.din("gains", [128, 4])
    ropec_in = lb.din("ropec", [128, TL])
    ropes_in = lb.din("ropes", [128, TL])
    rotT_in = lb.din("rotT", [128, 128])
    fcs_in = lb.din("f_cs", [128, 256])
    xnT = lb.dout("xnT", [KC, 128, TT], BF16)
    zc = lb.dout("zc", [12, 128, TT], F32)
    zfc = lb.dout("zfc", [4, 128, CTX], F32)
    qT = lb.dout("qT", [8, 128, TT], BF16)
    nqT = lb.dout("nqT", [4, 128, TT], BF16)
    kT = lb.dout("kT", [2, 128, TT], BF16)
    knT = lb.dout("knT", [4, 128, TT], BF16)
    vg = lb.dout("vg", [TT, 256], BF16)
    vn = lb.dout("vn", [TT, 512], BF16)
    fx = lb.dout("fx", [TL // 128, 1024, 128], F32)

    modsb, A1 = load_mods(lb, mods_in, norm1_in, 16)
    gains = lb.const("gains", gains_in, [128, 4])
    rotT = lb.const("rotT", rotT_in, [128, 128])
    fcs = lb.const("fcs", fcs_in, [128, 256])
    cosF = lb.const("cosF", ropec_in, [128, TL])
    sinF = lb.const("sinF", ropes_in, [128, TL])
    rc = lb.rc

    for gi, segs in enumerate(groups(cfg)):
        with contextlib.ExitStack() as st:
            xn = st.enter_context(lb.tmp("xn", [128, KC, GN], BF16))
            tag = "g%d" % gi
            with contextlib.ExitStack() as st2:
                norm_mod(lb, st2, hT, segs, xn, A1, modsb, 0, xnT, tag)
                kb.barrier()
            tiles = []
            for sg in segs:
                for t in sg.tiles():
                    tiles.append(t + (sg.is_ctx,))
            stg = [st.enter_context(lb.tmp("stg%d" % i, [128, 512], F32)) for i in range(4)]
            y0 = [st.enter_context(lb.tmp("y0%d" % i, [128, 512], F32)) for i in range(2)]
            sqh = [st.enter_context(lb.tmp("sqh%d" % i, [128, 512], BF16)) for i in range(2)]
            rsh = [st.enter_context(lb.tmp("rsh%d" % i, [128, 512], F32)) for i in range(2)]
            yy = [st.enter_context(lb.tmp("yy%d" % i, [128, 512], F32)) for i in range(2)]
            o1 = [st.enter_context(lb.tmp("o1%d" % i, [128, 512], F32)) for i in range(2)]
            o2 = [st.enter_context(lb.tmp("o2%d" % i, [128, 512], F32)) for i in range(2)]
            ob = [st.enter_context(lb.tmp("ob%d" % i, [128, 512], BF16)) for i in range(2)]
            cnt = {"e": 0, "h": 0}
            isctx = {(lo, tt): c for (lo, tt, n, c) in tiles}

            def epi(c, lo, tt, n, b):
                ctx_t = isctx[(lo, tt)]
                e = cnt["e"]
                cnt["e"] += 1
                if c < 12:
                    s = stg[e % 4]
                    rs_ = R(tag, "stg", e % 4)
                    evac(lb, b, n, s[:, 0:n], rs_, e)
                    kb.dma("sp", zc[c, :, tt:tt + n], s[:, 0:n], reads=[rs_], writes=[R(tag, "zc", c, tt)])
                elif c < 16:
                    g = c - 12
                    s = stg[e % 4]
                    rs_ = R(tag, "stg", e % 4)
                    evac(lb, b, n, s[:, 0:n], rs_, e)
                    if ctx_t:
                        kb.dma("sp", zfc[g, :, tt - TL:tt - TL + n], s[:, 0:n], reads=[rs_], writes=[R(tag, "zfc", g)])
                    else:
                        for part in range(2):
                            b2 = 4 + 2 * part + (e % 2)
                            kb.pe_group([MM(PS[b2][:, 0:n], fcs[:, part * 128:(part + 1) * 128], s[:, 0:n], True, True)],
                                        reads=[rs_, rc], writes=[RPS[b2]])
                            o = o1[e % 2] if part == 0 else o2[e % 2]
                            ro = R(tag, "ob%d" % (part + 1), e % 2)
                            evac(lb, b2, n, o[:, 0:n], ro, e + part)
                            col0 = part * 512 + g * 128
                            kb.dma("sp", fx[tt // 128:(tt + n) // 128, col0:col0 + 128, :].rearrange("b m t -> m b t"),
                                   o[:, 0:n].rearrange("m (b t) -> m b t", t=128), reads=[ro], writes=[R(tag, "fx", c, tt, part)])
                else:
                    if c < 20:
                        gi_, dst, rope = 0, nqT[c - 16], False
                    elif c < 24:
                        gi_, dst, rope = 1, knT[c - 20], False
                    elif c < 36:
                        gi_, dst, rope = 2, qT[c - 28], True
                    else:
                        gi_, dst, rope = 3, kT[c - 36], True
                    rope = rope and not ctx_t
                    i = cnt["h"] % 2
                    cnt["h"] += 1
                    ry, rq, rr, ryy, rob = R(tag, "y0", i), R(tag, "sqh", i), R(tag, "rsh", i), R(tag, "yy", i), R(tag, "ob", i)
                    kb.op("act", lambda e_: e_.copy(y0[i][:, 0:n], PS[b][:, 0:n]), reads=[RPS[b]], writes=[ry])
                    kb.op("act", lambda e_: e_.activation(sqh[i][:, 0:n], PS[b][:, 0:n], AF.Square), reads=[RPS[b]], writes=[rq])
                    b2 = 4 + i
                    kb.pe_group([MM(PS[b2][:, 0:n], lb.ones_bf[:], sqh[i][:, 0:n], True, True)], reads=[rq, rc], writes=[RPS[b2]])
                    kb.op("act", lambda e_: e_.activation(rsh[i][:, 0:n], PS[b2][:, 0:n], AF.Sqrt, bias=lb.epsb[:, 0:1],
                                                          scale=1.0 / 128), reads=[RPS[b2], rc], writes=[rr])
                    kb.op("dve", lambda e_: e_.reciprocal(rsh[i][:, 0:n], rsh[i][:, 0:n]), reads=[rr], writes=[rr])
                    kb.op("dve", lambda e_: e_.scalar_tensor_tensor(yy[i][:, 0:n], y0[i][:, 0:n], gains[:, gi_:gi_ + 1],
                                                                     rsh[i][:, 0:n], ALU.mult, ALU.mult),
                          reads=[ry, rr, rc], writes=[ryy])
                    if rope:
                        b3 = 6 + i
                        kb.pe_group([MM(PS[b3][:, 0:n], rotT[:], yy[i][:, 0:n], True, True)], reads=[ryy, rc], writes=[RPS[b3]])
                        r1, r2 = R(tag, "ob1", i), R(tag, "ob2", i)
                        kb.op("pool", lambda e_: e_.tensor_tensor(o1[i][:, 0:n], yy[i][:, 0:n], cosF[:, tt:tt + n], ALU.mult),
                              reads=[ryy, rc], writes=[r1])
                        kb.op("dve", lambda e_: e_.tensor_tensor(o2[i][:, 0:n], PS[b3][:, 0:n], sinF[:, tt:tt + n], ALU.mult),
                              reads=[RPS[b3], rc], writes=[r2])
                        kb.op("pool", lambda e_: e_.tensor_tensor(ob[i][:, 0:n], o1[i][:, 0:n], o2[i][:, 0:n], ALU.add),
                              reads=[r1, r2], writes=[rob])
                    else:
                        kb.op("pool", lambda e_: e_.tensor_copy(ob[i][:, 0:n], yy[i][:, 0:n]), reads=[ryy], writes=[rob])
                    kb.dma("sp", dst[:, tt:tt + n], ob[i][:, 0:n], reads=[rob], writes=[R(tag, "hd", c, tt)])

            blocks = [(0, [0, 1, 2, 3]), (512, [0, 1, 2, 3]), (1024, [0, 1, 2, 3]), (1536, [0, 1, 2, 3]),
                      (2048, [0, 1, 2, 3]), (2560, [0, 1, 2, 3]), (3584, [0, 1, 2, 3]), (4096, [0, 1, 2, 3]),
                      (4608, [0, 1])]
            sweep(lb, st, xn, lambda lo: R(tag, "xn", lo), KC, w_in, blocks, 512, [t[:3] for t in tiles], epi, "wA")
            wv = st.enter_context(lb.tmp("wv", [128, KC, 256], BF16))
            wvn = st.enter_context(lb.tmp("wvn", [128, KC, 512], BF16))
            vst = [st.enter_context(lb.tmp("vst%d" % i, [128, 768], BF16)) for i in range(2)]
            rwv = R(tag, "wv")
            kb.dma("pool", wv[:], w_in[:, 4864:5120].rearrange("(k p) m -> p k m", p=128), writes=[rwv])
            kb.dma("pool", wvn[:], w_in[:, 3072:3584].rearrange("(k p) m -> p k m", p=128), writes=[rwv])
            bi = 0
            for sg in segs:
                for t0 in range(0, sg.n, 128):
                    lo, tt = sg.lo + t0, sg.tt + t0
                    i = bi % 2
                    bi += 1
                    ba, bb = 4 + i, 6 + i
                    rx = R(tag, "xn", sg.lo + (t0 // 512) * 512 if False else [tl for (tl, _, n_) in sg.tiles() if tl <= lo < tl + n_][0])
                    kb.pe_group([MM(PS[ba][:, 0:256], xn[:, k, lo:lo + 128], wv[:, k, :], k == 0, k == KC - 1) for k in range(KC)],
                                reads=[rx, rwv], writes=[RPS[ba]])
                    kb.pe_group([MM(PS[bb][:, 0:512], xn[:, k, lo:lo + 128], wvn[:, k, :], k == 0, k == KC - 1) for k in range(KC)],
                                reads=[rx, rwv], writes=[RPS[bb]])
                    rv = R(tag, "vst", i)
                    kb.op("act", lambda e_, i=i, ba=ba: e_.copy(vst[i][:, 0:256], PS[ba][:, 0:256]), reads=[RPS[ba]], writes=[rv])
                    kb.op("dve", lambda e_, i=i, bb=bb: e_.tensor_copy(vst[i][:, 256:768], PS[bb][:, 0:512]), reads=[RPS[bb]], writes=[rv])
                    kb.dma("sp", vg[tt:tt + 128, :], vst[i][:, 0:256], reads=[rv], writes=[R(tag, "vg", tt)])
                    kb.dma("sp", vn[tt:tt + 128, :], vst[i][:, 256:768], reads=[rv], writes=[R(tag, "vn", tt)])
            kb.barrier()
    return lb.finish()


_PROG = {}


def get_prog(name, cfg, builder):
    key = (name, cfg.S, cfg.L)
    if key not in _PROG:
        _PROG[key] = builder(cfg)
    return _PROG[key]


def launch(lb, maps):
    res = run_bass_kernel_spmd(lb.nc, maps, core_ids=list(range(NCORE)))
    return res.results


def host_consts(cfg):
    f32 = np.float32
    S, T1, TL = cfg.S, cfg.T1, cfg.TL
    c = {}
    c["ident"] = np.eye(128, dtype=f32)
    rotT = np.zeros((128, 128), f32)
    for i in range(64):
        rotT[i + 64, i] = -1.0
        rotT[i, i + 64] = 1.0
    c["rotT"] = rotT
    a = np.arange(128)
    ang = 2 * np.pi * np.outer(a, a) / 128.0
    c["f_cs"] = np.concatenate([np.cos(ang), np.sin(ang)], 1).astype(f32)
    a1 = np.arange(T1)
    ang1 = 2 * np.pi * np.outer(a1, a1) / T1
    c["f_s1"] = np.ascontiguousarray(np.stack([np.cos(ang1), -np.sin(ang1), -np.cos(ang1)], 1).astype(f32))
    angt = 2 * np.pi * np.outer(a1, a) / float(S)
    c["f_tw"] = np.ascontiguousarray(np.stack([np.cos(angt), np.sin(angt)], 1).astype(f32))
    ac = np.arange(256)
    angc = 2 * np.pi * np.outer(ac, ac) / 256.0
    fc = np.stack([np.cos(angc), -np.sin(angc)], 1).astype(f32)
    c["f_ctx"] = np.ascontiguousarray(fc.reshape(2, 128, 2, 256).transpose(1, 0, 2, 3))
    inv_freq = (np.float32(10000.0) ** (-np.arange(32, dtype=f32) / np.float32(32))).astype(f32)
    c["ropec"], c["ropes"] = [], []
    for core in range(NCORE):
        t = np.arange(core * TL, (core + 1) * TL)
        row = (t // GRID_W).astype(f32)
        cl = (t % GRID_W).astype(f32)
        angr = np.concatenate([row[:, None] * inv_freq, cl[:, None] * inv_freq], -1).astype(f32)
        cosv = np.cos(angr).astype(f32).T
        sinv = np.sin(angr).astype(f32).T
        c["ropec"].append(np.ascontiguousarray(np.concatenate([cosv, cosv], 0)))
        c["ropes"].append(np.ascontiguousarray(np.concatenate([sinv, sinv], 0)))
    return c


def pl(a, nch):
    return np.ascontiguousarray(np.asarray(a, np.float32).reshape(nch, 128).T)


def to_fm(a):
    T, F = a.shape
    return np.ascontiguousarray(a.T.reshape(F // 128, 128, T))


def from_fm(a):
    return np.ascontiguousarray(a.reshape(-1, a.shape[2]).T)


def run_M(inp, cfg):
    L = cfg.L
    f32 = np.float32
    cv = np.stack([np.asarray(inp["c"], f32)[0], np.asarray(inp["c_ctx"], f32)], -1)
    cvec = np.ascontiguousarray(cv.reshape(KC, 128, 2).transpose(1, 0, 2))
    maps = []
    for c in range(NCORE):
        ba = np.asarray(inp["b_ada"], f32)[:L, c * 1536:(c + 1) * 1536]
        maps.append({"cvec": cvec,
                     "w_ada": np.ascontiguousarray(np.asarray(inp["w_ada"], f32)[:L, :, c * 1536:(c + 1) * 1536]),
                     "b_ada": np.ascontiguousarray(ba.reshape(L, 12, 128).transpose(2, 0, 1))})
    res = launch(get_prog("M", cfg, build_M), maps)
    allm = np.stack([res[c]["modloc"].reshape(128, L, 12, 2) for c in range(NCORE)], 0)
    mods = [np.ascontiguousarray(allm[:, :, l].transpose(1, 0, 2, 3).reshape(128, 96, 2)) for l in range(L)]
    return mods


def run_A(inp, cfg, l, hT, mods, hc):
    f32 = np.float32
    gains = np.ascontiguousarray(np.stack([np.asarray(inp[k], f32)[l] for k in
                                           ("na_q_gain", "na_k_gain", "gqa_q_gain", "gqa_k_gain")], -1))
    w_in = np.ascontiguousarray(np.asarray(inp["w_in"], f32)[l])
    n1 = pl(inp["norm1"][l], KC)
    maps = []
    for c in range(NCORE):
        maps.append({"hT": hT[c], "mods": mods[l], "norm1": n1, "w_in": w_in, "gains": gains,
                     "ropec": hc["ropec"][c], "ropes": hc["ropes"][c], "rotT": hc["rotT"], "f_cs": hc["f_cs"]})
    return launch(get_prog("A", cfg, build_A), maps)


def build_B(cfg):
    lb = LB()
    nc, kb, R, PS, RPS = lb.nc, lb.kb, lb.R, lb.PS, lb.RPS
    T1, S = cfg.T1, cfg.S
    xs_in = lb.din("xs", [T1, 2, 64, 128])
    fs1_in = lb.din("f_s1", [T1, 3, T1])
    ftw_in = lb.din("f_tw", [T1, 2, 128])
    fcs_in = lb.din("f_cs", [128, 256])
    ident_in = lb.din("ident", [128, 128])
    fy = lb.dout("fy", [128, 64, T1], F32)
    fs1 = lb.const("fs1", fs1_in, [T1, 3, T1])
    ftw = lb.const("ftw", ftw_in, [T1, 2, 128])
    fcs = lb.const("fcs", fcs_in, [128, 256])
    ident = lb.const("ident", ident_in, [128, 128])
    rc = lb.rc
    norm = 1.0 / math.sqrt(float(S) * 128.0)
    with contextlib.ExitStack() as st:
        xs = st.enter_context(lb.tmp("xs", [T1, 2, 64, 128], F32))
        V = st.enter_context(lb.tmp("V", [128, 2, 64, T1], F32))
        Y = st.enter_context(lb.tmp("Y", [128, 64, T1], F32))
        tt_ = [st.enter_context(lb.tmp("tw%d" % i, [T1, 4, 128], F32)) for i in range(4)]
        for blk in range(16):
            for part in range(2):
                kb.dma("sp", xs[:, part, 4 * blk:4 * blk + 4, :], xs_in[:, part, 4 * blk:4 * blk + 4, :],
                       writes=[R("xs", blk)])
        tcb = ftw[:, 0:1, :].to_broadcast([T1, 4, 128])
        tsb = ftw[:, 1:2, :].to_broadcast([T1, 4, 128])
        for blk in range(16):
            rx = R("xs", blk)
            Ab = xs[:, 0, 4 * blk:4 * blk + 4, :].rearrange("p c t -> p (c t)")
            Bb = xs[:, 1, 4 * blk:4 * blk + 4, :].rearrange("p c t -> p (c t)")
            b0, b1 = (blk % 2) * 2, (blk % 2) * 2 + 1
            kb.pe_group([MM(PS[b0][0:T1, :], fs1[:, 0, :], Ab, True, False), MM(PS[b0][0:T1, :], fs1[:, 1, :], Bb, False, True)],
                        reads=[rx, rc], writes=[RPS[b0]])
            kb.pe_group([MM(PS[b1][0:T1, :], fs1[:, 2, :], Bb, True, False), MM(PS[b1][0:T1, :], fs1[:, 1, :], Ab, False, True)],
                        reads=[rx, rc], writes=[RPS[b1]])
            ure = PS[b0][0:T1, :].rearrange("p (c t) -> p c t", c=4)
            uim = PS[b1][0:T1, :].rearrange("p (c t) -> p c t", c=4)
            rt = [R("tw", i) for i in range(4)]
            kb.op("dve", lambda e: e.tensor_tensor(tt_[0][:], ure, tcb, ALU.mult), reads=[RPS[b0], rc], writes=[rt[0]])
            kb.op("dve", lambda e: e.tensor_tensor(tt_[1][:], uim, tsb, ALU.mult), reads=[RPS[b1], rc], writes=[rt[1]])
            kb.op("dve", lambda e: e.tensor_tensor(tt_[2][:], uim, tcb, ALU.mult), reads=[RPS[b1], rc], writes=[rt[2]])
            kb.op("dve", lambda e: e.tensor_tensor(tt_[3][:], ure, tsb, ALU.mult), reads=[RPS[b0], rc], writes=[rt[3]])
            kb.op("pool", lambda e: e.tensor_tensor(xs[:, 0, 4 * blk:4 * blk + 4, :], tt_[0][:], tt_[1][:], ALU.add),
                  reads=[rt[0], rt[1]], writes=[rx])
            kb.op("pool", lambda e: e.tensor_tensor(xs[:, 1, 4 * blk:4 * blk + 4, :], tt_[2][:], tt_[3][:], ALU.subtract),
                  reads=[rt[2], rt[3]], writes=[rx])
        ei = 0
        for part in range(2):
            for cg in range(16):
                b = 4 + (ei % 4)
                fns = [lambda pe, j=j: pe.transpose(PS[b][:, j * T1:(j + 1) * T1], xs[:, part, 4 * cg + j, :], ident[0:T1, 0:T1])
                       for j in range(4)]
                kb.pe_group(fns, reads=[R("xs", cg), rc], writes=[RPS[b]])
                evac(lb, b, 4 * T1, V[:, part, 4 * cg:4 * cg + 4, :].rearrange("p c k -> p (c k)"), R("V", part, cg), ei)
                ei += 1
        cb = 512 // T1
        for blk in range(64 // cb):
            b = blk % 4
            reads = [R("V", p_, cg) for p_ in range(2) for cg in range((blk * cb) // 4, max((blk * cb) // 4 + 1, ((blk + 1) * cb + 3) // 4))]
            vre = V[:, 0, blk * cb:(blk + 1) * cb, :].rearrange("p c k -> p (c k)")
            vim = V[:, 1, blk * cb:(blk + 1) * cb, :].rearrange("p c k -> p (c k)")
            kb.pe_group([MM(PS[b][:, :], fcs[:, 0:128], vre, True, False), MM(PS[b][:, :], fcs[:, 128:256], vim, False, True)],
                        reads=reads + [rc], writes=[RPS[b]])
            evac(lb, b, 512, Y[:, blk * cb:(blk + 1) * cb, :].rearrange("p c k -> p (c k)"), R("Y"), blk, scale=norm)
        kb.dma("sp", fy, Y[:], reads=[R("Y")], writes=[R("fy")])
        kb.barrier()
    return lb.finish()


def run_B(cfg, rA, hc):
    T1 = cfg.T1
    fx_all = np.concatenate([np.asarray(rA[c]["fx"]) for c in range(NCORE)], 0)
    maps = []
    for c in range(NCORE):
        xs = np.stack([fx_all[:, 64 * c:64 * c + 64, :], fx_all[:, 512 + 64 * c:512 + 64 * c + 64, :]], 1)
        maps.append({"xs": np.ascontiguousarray(xs), "f_s1": hc["f_s1"], "f_tw": hc["f_tw"], "f_cs": hc["f_cs"],
                     "ident": hc["ident"]})
    res = launch(get_prog("B", cfg, build_B), maps)
    yall = np.stack([np.asarray(res[c]["fy"]) for c in range(NCORE)], 0)
    ycols = yall.transpose(0, 2, 1, 3).reshape(512, cfg.S)
    return [np.ascontiguousarray(ycols[:, r * cfg.TL:(r + 1) * cfg.TL].reshape(4, 128, cfg.TL)) for r in range(NCORE)]


def build_C(cfg):
    lb = LB()
    nc, kb, R, PS, RPS = lb.nc, lb.kb, lb.R, lb.PS, lb.RPS
    TL, TT, GL, GN, S = cfg.TL, cfg.TT, cfg.GL, cfg.GN, cfg.S
    TLR, GR, NKEY, NCH, NW = cfg.TLR, cfg.GR, cfg.NKEY, cfg.NCH, cfg.NW
    specs = na_row_specs(cfg)
    NTAB = specs["ncols"]
    hT = lb.din("hT", [KC, 128, TT])
    xnT = lb.din("xnT", [KC, 128, TT], BF16)
    zc = lb.din("zc", [12, 128, TT])
    zce = lb.din("zce", [8, 128, TL + 2])
    zfc = lb.din("zfc", [4, 128, CTX])
    qT = lb.din("qT", [8, 128, TT], BF16)
    nqT = lb.din("nqT", [4, 128, TT], BF16)
    kTall = lb.din("kTall", [2, 128, NKEY], BF16)
    vgall = lb.din("vgall", [NKEY, 256], BF16)
    knw = lb.din("knw", [4, 128, NW], BF16)
    vnw = lb.din("vnw", [NW, 512], BF16)
    cnkT = lb.din("cnkT", [4, 128, CTX], BF16)
    cvn = lb.din("cvn", [CTX, 512], BF16)
    natab = lb.din("natab", [64, 4, NTAB])
    fyT = lb.din("fyT", [4, 128, TL])
    mods_in = lb.din("mods", [128, 96, 2])
    norm2_in = lb.din("norm2", [128, KC])
    bgate_in = lb.din("b_gate", [128, 64])
    convw_in = lb.din("conv_w", [128, 3, 4])
    fcs_in = lb.din("f_cs", [128, 256])
    fctx_in = lb.din("f_ctx", [128, 2, 2, 256])
    w_gate = lb.din("w_gate", [D, 8192])
    w_outs = [lb.din("w_conv_out", [512, D]), lb.din("w_fourier_out", [512, D]),
              lb.din("w_na_out", [512, D]), lb.din("w_gqa_out", [1024, D])]
    w_o = lb.din("w_o", [D, D])
    hmT = lb.dout("hmT", [KC, 128, TT], F32)
    xn2T = lb.dout("xn2T", [KC, 128, TT], BF16)
    gT = lb.dscr("gT", [64, 128, TT], BF16)
    ysT = lb.dscr("ysT", [20, 128, TT], BF16)
    mgT = lb.dscr("mgT", [KC, 128, TT], BF16)

    modsb, A2 = load_mods(lb, mods_in, norm2_in, 64)
    bgate = lb.const("bgate", bgate_in, [128, 64])
    convw = lb.const("convw", convw_in, [128, 3, 4])
    fcs = lb.const("fcs", fcs_in, [128, 256])
    fctx = lb.const("fctx", fctx_in, [128, 2, 2, 256])
    rc = lb.rc

    for gi, segs in enumerate(groups(cfg)):
        tag = "g%d" % gi
        tiles = []
        for sg in segs:
            for t in sg.tiles():
                tiles.append(t + (sg.is_ctx,))
        t3 = [t[:3] for t in tiles]
        isctx = {(lo, tt): c for (lo, tt, n, c) in tiles}

        def load_res(dst, src, nch, key):
            for (lo, tt, n) in t3:
                kb.dma("sp", dst[:, 0:nch, lo:lo + n], src[0:nch, :, tt:tt + n].rearrange("k p t -> p k t"),
                       writes=[R(tag, key, lo)])

        with contextlib.ExitStack() as st:
            xn = st.enter_context(lb.tmp("xn", [128, KC, GN], BF16))
            gst = [st.enter_context(lb.tmp("gst%d" % i, [128, 512], BF16)) for i in range(4)]
            load_res(xn, xnT, KC, "xn")
            cnt = {"e": 0}

            def epi_g(c, lo, tt, n, b):
                e = cnt["e"] % 4
                cnt["e"] += 1
                rg = R(tag, "gst", e)
                kb.op("act", lambda e_: e_.activation(gst[e][:, 0:n], PS[b][:, 0:n], AF.Sigmoid, bias=bgate[:, c:c + 1], scale=1.0),
                      reads=[RPS[b], rc], writes=[rg])
                kb.dma("sp", gT[c, :, tt:tt + n], gst[e][:, 0:n], reads=[rg], writes=[R(tag, "gT", c, tt)])

            sweep(lb, st, xn, lambda lo: R(tag, "xn", lo), KC, w_gate, [(512 * i, [0, 1, 2, 3]) for i in range(16)], 512,
                  t3, epi_g, "wG")
            kb.barrier()

        with contextlib.ExitStack() as st:
            NB = GL + 2
            xa = [st.enter_context(lb.tmp("xa%d" % i, [128, NB], F32)) for i in range(2)]
            cg = [st.enter_context(lb.tmp("cg%d" % i, [128, NB], F32)) for i in range(2)]
            bg = [st.enter_context(lb.tmp("bg%d" % i, [128, NB], F32)) for i in range(2)]
            uu = [st.enter_context(lb.tmp("uu%d" % i, [128, NB], F32)) for i in range(2)]
            tc_ = [st.enter_context(lb.tmp("tc%d" % i, [128, NB], F32)) for i in range(2)]
            yb = [st.enter_context(lb.tmp("yb%d" % i, [128, NB], BF16)) for i in range(2)]
            it = 0
            for sg in segs:
                n = sg.n
                for j in range(4):
                    i = it % 2
                    it += 1
                    rin, ru, rt_, ry = R(tag, "cin", i), R(tag, "cu", i), R(tag, "ct", i), R(tag, "cy", i)
                    if not sg.is_ctx:
                        kb.dma("sp", xa[i][:, 0:n + 2], zce[j, :, sg.tt:sg.tt + n + 2], writes=[rin])
                        kb.dma("sp", cg[i][:, 0:n + 2], zce[4 + j, :, sg.tt:sg.tt + n + 2], writes=[rin])
                    else:
                        kb.op("pool", lambda e_: e_.memset(xa[i][:, 0:n + 2], 0.0), writes=[rin])
                        kb.op("pool", lambda e_: e_.memset(cg[i][:, 0:n + 2], 0.0), writes=[rin])
                        kb.dma("sp", xa[i][:, 1:n + 1], zc[j, :, sg.tt:sg.tt + n], writes=[rin])
                        kb.dma("sp", cg[i][:, 1:n + 1], zc[8 + j, :, sg.tt:sg.tt + n], writes=[rin])
                    kb.dma("sp", bg[i][:, 0:n], zc[4 + j, :, sg.tt:sg.tt + n], writes=[rin])
                    kb.op("pool", lambda e_: e_.tensor_tensor(uu[i][:, 0:n + 2], xa[i][:, 0:n + 2], cg[i][:, 0:n + 2], ALU.mult),
                          reads=[rin], writes=[ru])
                    kb.op("dve", lambda e_: e_.tensor_scalar(tc_[i][:, 0:n], uu[i][:, 0:n], convw[:, 0, j:j + 1], None, ALU.mult),
                          reads=[ru, rc], writes=[rt_])
                    kb.op("dve", lambda e_: e_.scalar_tensor_tensor(tc_[i][:, 0:n], uu[i][:, 1:n + 1], convw[:, 1, j:j + 1],
                                                                     tc_[i][:, 0:n], ALU.mult, ALU.add), reads=[ru, rc, rt_], writes=[rt_])
                    kb.op("dve", lambda e_: e_.scalar_tensor_tensor(tc_[i][:, 0:n], uu[i][:, 2:n + 2], convw[:, 2, j:j + 1],
                                                                     tc_[i][:, 0:n], ALU.mult, ALU.add), reads=[ru, rc, rt_], writes=[rt_])
                    kb.op("pool", lambda e_: e_.tensor_tensor(yb[i][:, 0:n], tc_[i][:, 0:n], bg[i][:, 0:n], ALU.mult),
                          reads=[rt_, rin], writes=[ry])
                    kb.dma("sp", ysT[j, :, sg.tt:sg.tt + n], yb[i][:, 0:n], reads=[ry], writes=[R(tag, "ys", j, sg.tt)])
            kb.barrier()

        with contextlib.ExitStack() as st:
            for sg in segs:
                if not sg.is_ctx:
                    for g in range(4):
                        kb.dma("pool", ysT[4 + g, :, sg.tt:sg.tt + sg.n], fyT[g, :, sg.tt:sg.tt + sg.n],
                               writes=[R(tag, "ys", 4 + g, sg.tt)])
                else:
                    zf = st.enter_context(lb.tmp("zf", [128, 4, CTX], F32))
                    xtm = st.enter_context(lb.tmp("xtm", [128, 2, 4, 256], F32))
                    yo = st.enter_context(lb.tmp("yo", [128, 4, CTX], BF16))
                    kb.dma("sp", zf[:], zfc.rearrange("g p t -> p g t"), writes=[R(tag, "zf")])
                    e = 0
                    for blk in range(2):
                        for g in range(4):
                            b = e % 4
                            kb.pe_group([MM(PS[b][:, 0:256], zf[:, g, blk * 128:(blk + 1) * 128], fcs[:, 0:256], True, True)],
                                        reads=[R(tag, "zf"), rc], writes=[RPS[b]])
                            evac(lb, b, 256, xtm[:, blk, g, :], R(tag, "xtm"), e)
                            e += 1
                    for g in range(4):
                        b = 4 + g
                        fns = []
                        for blk in range(2):
                            fns.append(MM(PS[b][:, 0:256], xtm[:, blk, g, 0:128], fctx[:, blk, 0, :], blk == 0, False))
                            fns.append(MM(PS[b][:, 0:256], xtm[:, blk, g, 128:256], fctx[:, blk, 1, :], False, blk == 1))
                        kb.pe_group(fns, reads=[R(tag, "xtm"), rc], writes=[RPS[b]])
                        evac(lb, b, 256, yo[:, g, :], R(tag, "yo"), g, scale=1.0 / math.sqrt(256.0 * 128.0))
                    kb.dma("sp", ysT[4:8, :, TL:TT].rearrange("g p t -> p g t"), yo[:], reads=[R(tag, "yo")],
                           writes=[R(tag, "ys", "fctx")])
            kb.barrier()

        with contextlib.ExitStack() as st:
            lr0 = gi * GR
            kmin = lr0 - 4
            kmax = max([specs["rows"][lr][0] + specs["rows"][lr][1] for lr in range(lr0, lr0 + GR)])
            nrw = kmax - kmin
            sp_rows = [lr for lr in range(lr0, lr0 + GR) if specs["rows"][lr][2] != 0]
            s0 = min([specs["rows"][lr][2] for lr in sp_rows])
            s1 = max([specs["rows"][lr][2] + specs["rows"][lr][1] * 64 for lr in sp_rows])
            KnT = st.enter_context(lb.tmp("KnT", [128, 4, nrw * 64], BF16))
            Vn = st.enter_context(lb.tmp("Vn", [64, nrw, 512], BF16))
            KcT = st.enter_context(lb.tmp("KcT", [128, 4, CTX], BF16))
            Vc = st.enter_context(lb.tmp("Vc", [64, 4, 512], BF16))
            Qn = st.enter_context(lb.tmp("Qn", [128, 4, GN], BF16))
            On = st.enter_context(lb.tmp("On", [128, 4, GN], BF16))
            tabm = st.enter_context(lb.tmp("tabm", [64, 4, 512], F32))
            tabs = st.enter_context(lb.tmp("tabs", [64, 4, s1 - s0], F32))
            sbb = [st.enter_context(lb.tmp("sbb%d" % i, [64, 768], F32)) for i in range(2)]
            Pn = [st.enter_context(lb.tmp("Pn%d" % i, [64, 1024], BF16)) for i in range(2)]
            rsn = [st.enter_context(lb.tmp("rsn%d" % i, [128, 64], F32)) for i in range(2)]
            rk = R(tag, "nak")
            kb.dma("sp", KnT[:], knw[:, :, (kmin + 4) * 64:(kmax + 4) * 64].rearrange("h p t -> p h t"), writes=[rk])
            kb.dma("sp", Vn[:], vnw[(kmin + 4) * 64:(kmax + 4) * 64, :].rearrange("(r c) f -> c r f", c=64), writes=[rk])
            kb.dma("sp", KcT[:], cnkT.rearrange("h p t -> p h t"), writes=[rk])
            kb.dma("sp", Vc[:], cvn.rearrange("(r c) f -> c r f", c=64), writes=[rk])
            kb.dma("sp", tabm[:], natab[:, :, 0:512], writes=[rk])
            kb.dma("sp", tabs[:], natab[:, :, s0:s1], writes=[rk])
            for (lo, tt, n) in t3:
                kb.dma("sp", Qn[:, :, lo:lo + n], nqT[:, :, tt:tt + n].rearrange("h p t -> p h t"), writes=[rk])
            qblocks = []
            for sg in segs:
                for q0 in range(0, sg.n, 64):
                    if sg.is_ctx:
                        qblocks.append((sg.lo + q0, 0, 0, None))
                    else:
                        lr = (sg.tt + q0) // 64
                        k0, nk, off = specs["rows"][lr]
                        qblocks.append((sg.lo + q0, k0, nk, off))
            it = 0
            ron = R(tag, "On")
            for (qlo, k0, nk, off) in qblocks:
                for h in range(4):
                    i = it % 2
                    it += 1
                    nb_ = nk + 4
                    SPv = lb.ps_all[0:64, i * 1024:i * 1024 + nb_ * 64]
                    rsp = [RPS[2 * i], RPS[2 * i + 1]]
                    fns = []
                    for a in range(nk):
                        kk = (k0 + a - kmin) * 64
                        fns.append(MM(SPv[:, a * 64:(a + 1) * 64], KnT[:, h, kk:kk + 64], Qn[:, h, qlo:qlo + 64], True, True))
                    for a in range(4):
                        fns.append(MM(SPv[:, (nk + a) * 64:(nk + a + 1) * 64], KcT[:, h, a * 64:(a + 1) * 64], Qn[:, h, qlo:qlo + 64], True, True))
                    kb.pe_group(fns, reads=[rk], writes=rsp)
                    rP, rsb = R(tag, "Pn", i), R(tag, "sbb", i)
                    if nk > 0:
                        tab = tabm[:, h, 0:512] if off == 0 else tabs[:, h, off - s0:off - s0 + nk * 64]
                        kb.op("dve", lambda e_: e_.scalar_tensor_tensor(sbb[i][:, 0:nk * 64], SPv[:, 0:nk * 64], SCALE, tab,
                                                                         ALU.mult, ALU.add), reads=rsp + [rk], writes=[rsb])
                        kb.op("act", lambda e_: e_.activation(Pn[i][:, 0:nk * 64], sbb[i][:, 0:nk * 64], AF.Exp),
                              reads=[rsb], writes=[rP])
                    kb.op("act", lambda e_: e_.activation(Pn[i][:, nk * 64:nb_ * 64], SPv[:, nk * 64:nb_ * 64], AF.Exp, scale=SCALE),
                          reads=rsp, writes=[rP])
                    ba, bs = 4 + i, 6 + i
                    fns = []
                    for a in range(nk):
                        fns.append(MM(PS[ba][:, 0:64], Vn[:, k0 + a - kmin, h * 128:(h + 1) * 128], Pn[i][:, a * 64:(a + 1) * 64], a == 0, False))
                    for a in range(4):
                        fns.append(MM(PS[ba][:, 0:64], Vc[:, a, h * 128:(h + 1) * 128], Pn[i][:, (nk + a) * 64:(nk + a + 1) * 64],
                                      nk == 0 and a == 0, a == 3))
                    kb.pe_group(fns, reads=[rP, rk], writes=[RPS[ba]])
                    fns = [MM(PS[bs][:, 0:64], lb.ones_bf[0:64, :], Pn[i][:, a * 64:(a + 1) * 64], a == 0, a == nb_ - 1) for a in range(nb_)]
                    kb.pe_group(fns, reads=[rP, rc], writes=[RPS[bs]])
                    rr = R(tag, "rsn", i)
                    kb.op("dve", lambda e_: e_.reciprocal(rsn[i][:], PS[bs][:, 0:64]), reads=[RPS[bs]], writes=[rr])
                    kb.op("dve", lambda e_: e_.tensor_tensor(On[:, h, qlo:qlo + 64], PS[ba][:, 0:64], rsn[i][:], ALU.mult),
                          reads=[RPS[ba], rr], writes=[ron])
            for (lo, tt, n) in t3:
                kb.dma("sp", ysT[8:12, :, tt:tt + n].rearrange("h p t -> p h t"), On[:, :, lo:lo + n], reads=[ron],
                       writes=[R(tag, "ys", "na", tt)])
            kb.barrier()

        for g2 in range(2):
            with contextlib.ExitStack() as st:
                KT = st.enter_context(lb.tmp("KT", [128, NKEY], BF16))
                Vg = st.enter_context(lb.tmp("Vg", [128, NCH, 128], BF16))
                Qg = st.enter_context(lb.tmp("Qg", [128, 4, GN], BF16))
                Og = st.enter_context(lb.tmp("Og", [128, 4, GN], BF16))
                Pb = [st.enter_context(lb.tmp("Pb%d" % i, [128, 512], BF16)) for i in range(3)]
                rsg = [st.enter_context(lb.tmp("rsg%d" % i, [128, 512], F32)) for i in range(2)]
                rk = R(tag, "gk", g2)
                for (o, s_) in split_cols(NKEY, 4096):
                    kb.dma("sp", KT[:, o:o + s_], kTall[g2, :, o:o + s_], writes=[rk])
                for c0 in range(0, NCH, 32):
                    c1 = min(NCH, c0 + 32)
                    kb.dma("sp", Vg[:, c0:c1, :], vgall[c0 * 128:c1 * 128, g2 * 128:(g2 + 1) * 128].rearrange("(c p) d -> p c d", p=128),
                           writes=[rk])
                for (lo, tt, n) in t3:
                    kb.dma("sp", Qg[:, :, lo:lo + n], qT[4 * g2:4 * g2 + 4, :, tt:tt + n].rearrange("h p t -> p h t"), writes=[rk])
                rog = R(tag, "Og", g2)
                it = 0
                sidx = 0
                for j in range(4):
                    for (lo, tt, n) in t3:
                        chunks = list(range(S // 128, NCH)) if isctx[(lo, tt)] else list(range(NCH))
                        ba, bs = 4 + it % 2, 6 + it % 2
                        it += 1

                        def emit_s(c, si):
                            bq = si % 2
                            kb.pe_group([MM(PS[bq][:, 0:n], KT[:, c * 128:(c + 1) * 128], Qg[:, j, lo:lo + n], True, True)],
                                        reads=[rk], writes=[RPS[bq]])

                        emit_s(chunks[0], sidx)
                        for ci, c in enumerate(chunks):
                            if ci + 1 < len(chunks):
                                emit_s(chunks[ci + 1], sidx + ci + 1)
                            bq = (sidx + ci) % 2
                            pi = (sidx + ci) % 3
                            rp = R(tag, "Pb", pi)
                            kb.op("act", lambda e_: e_.activation(Pb[pi][:, 0:n], PS[bq][:, 0:n], AF.Exp, scale=SCALE),
                                  reads=[RPS[bq]], writes=[rp])
                            first, last = ci == 0, ci == len(chunks) - 1
                            kb.pe_group([MM(PS[ba][:, 0:n], Vg[:, c, :], Pb[pi][:, 0:n], first, last),
                                         MM(PS[bs][:, 0:n], lb.ones_bf[:], Pb[pi][:, 0:n], first, last)],
                                        reads=[rp, rk, rc], writes=[RPS[ba], RPS[bs]])
                        sidx += len(chunks)
                        ri = it % 2
                        rr = R(tag, "rsg", ri)
                        kb.op("dve", lambda e_: e_.reciprocal(rsg[ri][:, 0:n], PS[bs][:, 0:n]), reads=[RPS[bs]], writes=[rr])
                        kb.op("dve", lambda e_: e_.tensor_tensor(Og[:, j, lo:lo + n], PS[ba][:, 0:n], rsg[ri][:, 0:n], ALU.mult),
                              reads=[RPS[ba], rr], writes=[rog])
                for (lo, tt, n) in t3:
                    kb.dma("sp", ysT[12 + 4 * g2:16 + 4 * g2, :, tt:tt + n].rearrange("h p t -> p h t"), Og[:, :, lo:lo + n],
                           reads=[rog], writes=[R(tag, "ys", "gqa", g2, tt)])
                kb.barrier()

        with contextlib.ExitStack() as st:
            ys = st.enter_context(lb.tmp("ys", [128, 20, GN], BF16))
            wm = [st.enter_context(lb.tmp("wm%d" % i, [128, 20, 512], BF16)) for i in range(2)]
            G = [st.enter_context(lb.tmp("G%d" % i, [128, 4, 512], BF16)) for i in range(2)]
            mm_ = [st.enter_context(lb.tmp("mm%d" % i, [128, 512], F32)) for i in range(4)]
            ss_ = [st.enter_context(lb.tmp("ss%d" % i, [128, 512], F32)) for i in range(2)]
            mo = [st.enter_context(lb.tmp("mo%d" % i, [128, 512], BF16)) for i in range(2)]
            load_res(ys, ysT, 20, "ys")
            gT4 = gT.rearrange("(b c) p t -> b c p t", b=4)
            kcs = [(0, 4), (4, 8), (8, 12), (12, 20)]
            it = 0
            for mb in range(4):
                wi = mb % 2
                rw = R(tag, "wm", wi)
                for b_, (ka, kb_) in enumerate(kcs):
                    kb.dma("pool", wm[wi][:, ka:kb_, :], w_outs[b_][:, mb * 512:(mb + 1) * 512].rearrange("(k p) m -> p k m", p=128),
                           writes=[rw])
                for (lo, tt, n) in t3:
                    for j in range(4):
                        c = mb * 4 + j
                        i = it % 2
                        it += 1
                        rG = R(tag, "G", i)
                        kb.dma("sp", G[i][:, :, 0:n], gT4[:, c, :, tt:tt + n].rearrange("b p t -> p b t"), writes=[rG])
                        rm = [R(tag, "mm", b_) for b_ in range(4)]
                        for b_, (ka, kb_) in enumerate(kcs):
                            bank = b_ + 4 * i
                            kb.pe_group([MM(PS[bank][:, 0:n], wm[wi][:, k, j * 128:(j + 1) * 128], ys[:, k, lo:lo + n], k == ka, k == kb_ - 1)
                                         for k in range(ka, kb_)], reads=[rw, R(tag, "ys", lo)], writes=[RPS[bank]])
                            kb.op("dve", lambda e_: e_.tensor_tensor(mm_[b_][:, 0:n], PS[bank][:, 0:n], G[i][:, b_, 0:n], ALU.mult),
                                  reads=[RPS[bank], rG], writes=[rm[b_]])
                        rs0, rs1, rmo = R(tag, "ss", 0), R(tag, "ss", 1), R(tag, "mo", i)
                        kb.op("pool", lambda e_: e_.tensor_tensor(ss_[0][:, 0:n], mm_[0][:, 0:n], mm_[1][:, 0:n], ALU.add),
                              reads=[rm[0], rm[1]], writes=[rs0])
                        kb.op("pool", lambda e_: e_.tensor_tensor(ss_[1][:, 0:n], mm_[2][:, 0:n], mm_[3][:, 0:n], ALU.add),
                              reads=[rm[2], rm[3]], writes=[rs1])
                        kb.op("pool", lambda e_: e_.tensor_tensor(mo[i][:, 0:n], ss_[0][:, 0:n], ss_[1][:, 0:n], ALU.add),
                              reads=[rs0, rs1], writes=[rmo])
                        kb.dma("sp", mgT[c, :, tt:tt + n], mo[i][:, 0:n], reads=[rmo], writes=[R(tag, "mg", c, tt)])
            kb.barrier()

        with contextlib.ExitStack() as st:
            mg = st.enter_context(lb.tmp("mg", [128, KC, GN], BF16))
            hbt = [st.enter_context(lb.tmp("hbt%d" % i, [128, 512], F32)) for i in range(4)]
            hot = [st.enter_context(lb.tmp("hot%d" % i, [128, 512], F32)) for i in range(4)]
            load_res(mg, mgT, KC, "mg2")
            cnt = {"e": 0}

            def epi_o(c, lo, tt, n, b):
                e = cnt["e"] % 4
                cnt["e"] += 1
                v = 1 if isctx[(lo, tt)] else 0
                rh, ro = R(tag, "hbt", e), R(tag, "hot", e)
                kb.dma("sp", hbt[e][:, 0:n], hT[c, :, tt:tt + n], writes=[rh])
                kb.op("dve", lambda e_: e_.scalar_tensor_tensor(hot[e][:, 0:n], PS[b][:, 0:n], modsb[:, 32 + c, v:v + 1],
                                                                 hbt[e][:, 0:n], ALU.mult, ALU.add), reads=[RPS[b], rh, rc], writes=[ro])
                kb.dma("sp", hmT[c, :, tt:tt + n], hot[e][:, 0:n], reads=[ro], writes=[R(tag, "hm", c, tt)])

            sweep(lb, st, mg, lambda lo: R(tag, "mg2", lo), KC, w_o, [(512 * i, [0, 1, 2, 3]) for i in range(4)], 512, t3, epi_o, "wO")
            kb.barrier()

        with contextlib.ExitStack() as st:
            xn2 = st.enter_context(lb.tmp("xn2", [128, KC, GN], BF16))
            norm_mod(lb, st, hmT, segs, xn2, A2, modsb, 48, xn2T, tag + "n2")
            kb.barrier()
    return lb.finish()


def na_tables(cfg, rpb_l):
    specs = na_row_specs(cfg)
    ROWS = cfg.ROWS
    col = np.arange(GRID_W)
    c0 = np.clip(col - 8, 0, GRID_W - 16)
    outs = []
    for c in range(NCORE):
        base = c * cfg.TLR
        tabs = []
        for (slot, lr, k0, nk) in specs["slots"]:
            tab = np.full((4, nk, 64, 64), NEG, np.float32)
            lrs = specs["mid_lr"] if lr is None else lr
            kk0 = lrs - 4 if lr is None else k0
            r = base + lrs
            r0 = min(max(r - 4, 0), ROWS - 8)
            for aa in range(nk):
                kr = base + kk0 + aa
                if kr < r0 or kr >= r0 + 8 or kr < 0 or kr >= ROWS:
                    continue
                dr = kr - r + 7
                for qc in range(64):
                    kcs = np.arange(c0[qc], c0[qc] + 16)
                    tab[:, aa, kcs, qc] = rpb_l[:, dr, kcs - qc + 15]
            tabs.append(tab.transpose(2, 0, 1, 3).reshape(64, 4, nk * 64))
        outs.append(np.ascontiguousarray(np.concatenate(tabs, -1)))
    return outs


def run_C(inp, cfg, l, hT, mods, hc, rA, fyT):
    f32 = np.float32
    TL, TT, S, NW = cfg.TL, cfg.TT, cfg.S, cfg.NW
    zc_lat = np.concatenate([np.asarray(rA[c]["zc"])[:, :, :TL] for c in range(NCORE)], 2)
    zpad = np.pad(zc_lat[[0, 1, 2, 3, 8, 9, 10, 11]], ((0, 0), (0, 0), (1, 1)))
    kT_lat = np.concatenate([np.asarray(rA[c]["kT"])[:, :, :TL] for c in range(NCORE)], 2)
    kTall = np.ascontiguousarray(np.concatenate([kT_lat, np.asarray(rA[0]["kT"])[:, :, TL:]], 2))
    vgall = np.ascontiguousarray(np.concatenate([np.asarray(rA[c]["vg"])[:TL] for c in range(NCORE)] + [np.asarray(rA[0]["vg"])[TL:]], 0))
    kn_lat = np.concatenate([np.asarray(rA[c]["knT"])[:, :, :TL] for c in range(NCORE)], 2)
    kn_pad = np.pad(kn_lat, ((0, 0), (0, 0), (256, 192)))
    vn_lat = np.concatenate([np.asarray(rA[c]["vn"])[:TL] for c in range(NCORE)], 0)
    vn_pad = np.pad(vn_lat, ((256, 192), (0, 0)))
    cnkT = np.ascontiguousarray(np.asarray(rA[0]["knT"])[:, :, TL:])
    cvn = np.ascontiguousarray(np.asarray(rA[0]["vn"])[TL:])
    tabs = na_tables(cfg, np.asarray(inp["na_rpb"], f32)[l])
    conv_w = np.ascontiguousarray(np.asarray(inp["conv_w"], f32)[l].reshape(3, 4, 128).transpose(2, 0, 1))
    wts = {k: np.ascontiguousarray(np.asarray(inp[k], f32)[l]) for k in
           ("w_gate", "w_conv_out", "w_fourier_out", "w_na_out", "w_gqa_out", "w_o")}
    n2 = pl(inp["norm2"][l], KC)
    bg = pl(inp["b_gate"][l], 64)
    maps = []
    for c in range(NCORE):
        m = {"hT": hT[c], "xnT": np.asarray(rA[c]["xnT"]), "zc": np.asarray(rA[c]["zc"]),
             "zce": np.ascontiguousarray(zpad[:, :, c * TL:c * TL + TL + 2]), "zfc": np.asarray(rA[c]["zfc"]),
             "qT": np.asarray(rA[c]["qT"]), "nqT": np.asarray(rA[c]["nqT"]), "kTall": kTall, "vgall": vgall,
             "knw": np.ascontiguousarray(kn_pad[:, :, c * TL:c * TL + NW]),
             "vnw": np.ascontiguousarray(vn_pad[c * TL:c * TL + NW]), "cnkT": cnkT, "cvn": cvn, "natab": tabs[c],
             "fyT": fyT[c], "mods": mods[l], "norm2": n2, "b_gate": bg, "conv_w": conv_w, "f_cs": hc["f_cs"],
             "f_ctx": hc["f_ctx"]}
        m.update(wts)
        maps.append(m)
    return launch(get_prog("C", cfg, build_C), maps)


def build_D(cfg):
    lb = LB()
    nc, kb, R, PS, RPS = lb.nc, lb.kb, lb.R, lb.PS, lb.RPS
    TL, TT, GL, GN = cfg.TL, cfg.TT, cfg.GL, cfg.GN
    NX0, NX1 = GL + 2 + CTX + 2, GL + 2
    hmT = lb.din("hmT", [KC, 128, TT])
    xg = [lb.din("xg0", [KC, 128, NX0], BF16), lb.din("xg1", [KC, 128, NX1], BF16)]
    fcw_in = lb.din("ffn_cw", [128, 3, 88])
    mods_in = lb.din("mods", [128, 96, 2])
    w_up = lb.din("w_up", [D, 11264])
    w_down = lb.din("w_down", [5632, D])
    hTo = lb.dout("hTo", [KC, 128, TT], F32)
    actT = lb.dscr("actT", [44, 128, TT], BF16)
    modsb = lb.const("modsb", mods_in, [128, 96, 2])
    fcw = lb.const("fcw", fcw_in, [128, 3, 88])
    rc = lb.rc
    for gi, segs in enumerate(groups(cfg)):
        tag = "g%d" % gi
        NX = NX0 if gi == 0 else NX1
        segD = []
        off = 0
        for sg in segs:
            segD.append((off, sg.n, sg.tt, sg.is_ctx))
            off += sg.n + 2
        ctiles = []
        for (o, n, tt, c_) in segD:
            for (a, s_) in split_cols(n + 2):
                ctiles.append((o + a, s_))
        with contextlib.ExitStack() as st:
            xin = st.enter_context(lb.tmp("xin", [128, KC, NX], BF16))
            ua = [st.enter_context(lb.tmp("ua%d" % i, [128, NX], F32)) for i in range(2)]
            ug = [st.enter_context(lb.tmp("ug%d" % i, [128, NX], F32)) for i in range(2)]
            ca = [st.enter_context(lb.tmp("ca%d" % i, [128, GL], F32)) for i in range(2)]
            cg = [st.enter_context(lb.tmp("cg%d" % i, [128, GL], F32)) for i in range(2)]
            sil = st.enter_context(lb.tmp("sil", [128, GL], F32))
            ptmp = st.enter_context(lb.tmp("ptmp", [128, GL], F32))
            ab = [st.enter_context(lb.tmp("ab%d" % i, [128, GL], BF16)) for i in range(2)]
            wblk = [st.enter_context(lb.tmp("wu%d" % i, [128, KC, 2, 256], BF16)) for i in range(2)]
            rx = R(tag, "xin")
            for (o, s_) in split_cols(NX, 512):
                kb.dma("sp", xin[:, :, o:o + s_], xg[gi][:, :, o:o + s_].rearrange("k p t -> p k t"), writes=[rx])
            ei = 0
            si = 0
            for pb in range(22):
                wi = pb % 2
                rw = R(tag, "wu", wi)
                kb.dma("pool", wblk[wi][:, :, 0, :], w_up[:, pb * 256:(pb + 1) * 256].rearrange("(k p) m -> p k m", p=128), writes=[rw])
                kb.dma("pool", wblk[wi][:, :, 1, :], w_up[:, 5632 + pb * 256:5632 + (pb + 1) * 256].rearrange("(k p) m -> p k m", p=128),
                       writes=[rw])
                for jj in range(2):
                    j = 2 * pb + jj
                    ui = j % 2
                    rua, rug = R(tag, "ua", ui), R(tag, "ug", ui)
                    for (co, s_) in ctiles:
                        for part, (dstb, rd) in enumerate(((ua[ui], rua), (ug[ui], rug))):
                            b = lb.nb(0, 8)
                            kb.pe_group([MM(PS[b][:, 0:s_], wblk[wi][:, k, part, jj * 128:(jj + 1) * 128], xin[:, k, co:co + s_],
                                            k == 0, k == KC - 1) for k in range(KC)], reads=[rw, rx], writes=[RPS[b]])
                            evac(lb, b, s_, dstb[:, co:co + s_], rd, ei)
                            ei += 1
                    for (o, n, tt, c_) in segD:
                        i = si % 2
                        si += 1
                        rca, rcg, rsl, rab = R(tag, "ca", i), R(tag, "cg", i), R(tag, "sil"), R(tag, "ab", i)
                        src, dst, rs_, rd, ch = ua[ui], ca[i], rua, rca, j
                        kb.op("dve", lambda e_: e_.tensor_scalar(dst[:, 0:n], src[:, o:o + n], fcw[:, 0, ch:ch + 1], None, ALU.mult),
                              reads=[rs_, rc], writes=[rd])
                        kb.op("dve", lambda e_: e_.scalar_tensor_tensor(dst[:, 0:n], src[:, o + 1:o + 1 + n], fcw[:, 1, ch:ch + 1],
                                                                         dst[:, 0:n], ALU.mult, ALU.add), reads=[rs_, rc, rd], writes=[rd])
                        kb.op("dve", lambda e_: e_.scalar_tensor_tensor(dst[:, 0:n], src[:, o + 2:o + 2 + n], fcw[:, 2, ch:ch + 1],
                                                                         dst[:, 0:n], ALU.mult, ALU.add), reads=[rs_, rc, rd], writes=[rd])
                        src, dst, rs_, rd, ch = ug[ui], cg[i], rug, rcg, 44 + j
                        rtp = R(tag, "ptmp")
                        kb.op("pool", lambda e_: e_.tensor_scalar(dst[:, 0:n], src[:, o:o + n], fcw[:, 0, ch:ch + 1], None, ALU.mult),
                              reads=[rs_, rc], writes=[rd])
                        for tap in (1, 2):
                            kb.op("pool", lambda e_: e_.tensor_scalar(ptmp[:, 0:n], src[:, o + tap:o + tap + n], fcw[:, tap, ch:ch + 1], None, ALU.mult),
                                  reads=[rs_, rc], writes=[rtp])
                            kb.op("pool", lambda e_: e_.tensor_tensor(dst[:, 0:n], dst[:, 0:n], ptmp[:, 0:n], ALU.add),
                                  reads=[rtp, rd], writes=[rd])
                        kb.op("act", lambda e_: e_.activation(sil[:, 0:n], ca[i][:, 0:n], AF.Silu), reads=[rca], writes=[rsl])
                        kb.op("pool", lambda e_: e_.tensor_tensor(ab[i][:, 0:n], sil[:, 0:n], cg[i][:, 0:n], ALU.mult),
                              reads=[rsl, rcg], writes=[rab])
                        kb.dma("sp", actT[j, :, tt:tt + n], ab[i][:, 0:n], reads=[rab], writes=[R(tag, "act", j, tt)])
            kb.barrier()
        tiles = []
        for sg in segs:
            for t in sg.tiles():
                tiles.append(t + (sg.is_ctx,))
        t3 = [t[:3] for t in tiles]
        isctx = {(lo, tt): c for (lo, tt, n, c) in tiles}
        with contextlib.ExitStack() as st:
            act = st.enter_context(lb.tmp("act", [128, 44, GN], BF16))
            hbt = [st.enter_context(lb.tmp("hbt%d" % i, [128, 512], F32)) for i in range(4)]
            hot = [st.enter_context(lb.tmp("hot%d" % i, [128, 512], F32)) for i in range(4)]
            for (lo, tt, n) in t3:
                for k0 in range(0, 44, 11):
                    kb.dma("sp", act[:, k0:k0 + 11, lo:lo + n], actT[k0:k0 + 11, :, tt:tt + n].rearrange("k p t -> p k t"),
                           writes=[R(tag, "actr", lo)])
            cnt = {"e": 0}

            def epi_d(c, lo, tt, n, b):
                e = cnt["e"] % 4
                cnt["e"] += 1
                v = 1 if isctx[(lo, tt)] else 0
                rh, ro = R(tag, "hbt", e), R(tag, "hot", e)
                kb.dma("sp", hbt[e][:, 0:n], hmT[c, :, tt:tt + n], writes=[rh])
                kb.op("dve", lambda e_: e_.scalar_tensor_tensor(hot[e][:, 0:n], PS[b][:, 0:n], modsb[:, 80 + c, v:v + 1],
                                                                 hbt[e][:, 0:n], ALU.mult, ALU.add), reads=[RPS[b], rh, rc], writes=[ro])
                kb.dma("sp", hTo[c, :, tt:tt + n], hot[e][:, 0:n], reads=[ro], writes=[R(tag, "ho", c, tt)])

            sweep(lb, st, act, lambda lo: R(tag, "actr", lo), 44, w_down, [(256 * i, [0, 1]) for i in range(8)], 256, t3, epi_d, "wD")
            kb.barrier()
    return lb.finish()


def run_D(inp, cfg, l, mods, rC):
    f32 = np.float32
    TL, GL = cfg.TL, cfg.GL
    x2_lat = np.concatenate([np.asarray(rC[c]["xn2T"])[:, :, :TL] for c in range(NCORE)], 2)
    x2p = np.pad(x2_lat, ((0, 0), (0, 0), (1, 1)))
    fcw = np.ascontiguousarray(np.asarray(inp["ffn_conv_w"], f32)[l].reshape(3, 88, 128).transpose(2, 0, 1))
    w_up = np.ascontiguousarray(np.asarray(inp["w_up"], f32)[l])
    w_down = np.ascontiguousarray(np.asarray(inp["w_down"], f32)[l])
    maps = []
    for c in range(NCORE):
        cx = np.pad(np.asarray(rC[c]["xn2T"])[:, :, TL:], ((0, 0), (0, 0), (1, 1)))
        xg0 = np.ascontiguousarray(np.concatenate([x2p[:, :, c * TL:c * TL + GL + 2], cx], 2))
        xg1 = np.ascontiguousarray(x2p[:, :, c * TL + GL:c * TL + 2 * GL + 2])
        maps.append({"hmT": np.asarray(rC[c]["hmT"]), "xg0": xg0, "xg1": xg1, "ffn_cw": fcw, "mods": mods[l],
                     "w_up": w_up, "w_down": w_down})
    return launch(get_prog("D", cfg, build_D), maps)


def run_model(inp, cfg, verbose=False):
    import time
    hc = host_consts(cfg)
    TL = cfg.TL
    t0 = time.time()
    mods = run_M(inp, cfg)
    x = np.asarray(inp["x"], np.float32)[0]
    ctx = np.asarray(inp["ctx"], np.float32)[0]
    ctx_fm = to_fm(ctx)
    hT = [np.ascontiguousarray(np.concatenate([to_fm(x[c * TL:(c + 1) * TL]), ctx_fm], 2)) for c in range(NCORE)]
    for l in range(cfg.L):
        rA = run_A(inp, cfg, l, hT, mods, hc)
        if verbose:
            print("layer", l, "A done", time.time() - t0, flush=True)
        fyT = run_B(cfg, rA, hc)
        rC = run_C(inp, cfg, l, hT, mods, hc, rA, fyT)
        if verbose:
            print("layer", l, "C done", time.time() - t0, flush=True)
        del rA
        rD = run_D(inp, cfg, l, mods, rC)
        del rC
        hT = [np.asarray(rD[c]["hTo"]) for c in range(NCORE)]
        if verbose:
            print("layer", l, "D done", time.time() - t0, flush=True)
    out = np.concatenate([from_fm(hT[c][:, :, :TL]) for c in range(NCORE)], 0)
    return np.ascontiguousarray(out[None].astype(np.float32))


def kernel(**inputs):
    cfg = Cfg(16384, 4)
    return run_model(inputs, cfg)
```

```python
import contextlib
import numpy as np
import math
import ml_dtypes
import concourse.bass as bass
import concourse.mybir as mybir
from concourse.bass_utils import run_bass_kernel_spmd

F32 = mybir.dt.float32
BF16 = mybir.dt.bfloat16
ALU = mybir.AluOpType
AF = mybir.ActivationFunctionType
AX = mybir.AxisListType


class Res:
    __slots__ = ("w", "r", "name")

    def __init__(self, name=""):
        self.w = None
        self.r = {}
        self.name = name


class KB:
    def __init__(self, nc, n_dma_sems=6, same_engine_sync=True):
        self.nc = nc
        self.es = contextlib.ExitStack()
        self.engs = {"pe": nc.tensor, "act": nc.scalar, "dve": nc.vector,
                     "pool": nc.gpsimd, "sp": nc.sync}
        self.semh = {}
        for k in self.engs:
            self.semh[k] = self.es.enter_context(nc.semaphore("s_" + k))
        self.cnt = {k: 0 for k in self.engs}
        self.seen = {k: {} for k in self.engs}
        self.same = same_engine_sync
        self.dq = {}
        for q in ("sp", "pool", "act"):
            sl = []
            for i in range(n_dma_sems):
                key = ("d", q, i)
                self.semh[key] = self.es.enter_context(nc.semaphore("d_%s%d" % (q, i)))
                sl.append(key)
            self.dq[q] = {"keys": sl, "uses": [0] * n_dma_sems, "next": 0}
        self.n_ins = 0

    def close(self):
        self.es.close()

    def sb(self, name, shape, dt):
        return self.es.enter_context(self.nc.sbuf_tensor("sb_" + name, list(shape), dt))

    def ps(self, name, shape, dt=F32):
        return self.es.enter_context(self.nc.psum_tensor("pp_" + name, list(shape), dt))

    def _wait(self, eng, deps):
        e = self.engs[eng]
        seen = self.seen[eng]
        for sk, v in deps:
            if sk == eng and (not self.same or eng == "pe"):
                continue
            if seen.get(sk, 0) >= v:
                continue
            e.wait_ge(self.semh[sk], v)
            seen[sk] = v

    def _deps(self, reads, writes):
        deps = {}
        for r in reads:
            if r.w is not None:
                sk, v = r.w
                if deps.get(sk, 0) < v:
                    deps[sk] = v
        for w in writes:
            if w.w is not None:
                sk, v = w.w
                if deps.get(sk, 0) < v:
                    deps[sk] = v
            for sk, v in w.r.items():
                if deps.get(sk, 0) < v:
                    deps[sk] = v
        return deps.items()

    def _mark(self, ev, reads, writes):
        sk, v = ev
        for r in reads:
            if r.r.get(sk, 0) < v:
                r.r[sk] = v
        for w in writes:
            w.w = ev
            w.r = {}

    def op(self, eng, fn, reads=(), writes=()):
        self._wait(eng, self._deps(reads, writes))
        ins = fn(self.engs[eng])
        self.cnt[eng] += 1
        n = self.cnt[eng]
        ins.then_inc(self.semh[eng], 1)
        self._mark((eng, n), reads, writes)
        self.n_ins += 1
        return ins

    def pe_group(self, fns, reads=(), writes=()):
        self._wait("pe", self._deps(reads, writes))
        ins = None
        for fn in fns:
            ins = fn(self.nc.tensor)
        self.cnt["pe"] += 1
        n = self.cnt["pe"]
        ins.then_inc(self.semh["pe"], 1)
        self._mark(("pe", n), reads, writes)
        self.n_ins += len(fns)

    def dma(self, q, out, in_, reads=(), writes=(), **kw):
        d = self.dq[q]
        i = d["next"]
        d["next"] = (i + 1) % len(d["keys"])
        key = d["keys"][i]
        deps = dict(self._deps(reads, writes))
        if d["uses"][i] > 0:
            deps[key] = 16 * d["uses"][i]
        self._wait(q, deps.items())
        ins = self.engs[q].dma_start(out=out, in_=in_, **kw)
        d["uses"][i] += 1
        v = 16 * d["uses"][i]
        ins.then_inc(self.semh[key], 16)
        self._mark((key, v), reads, writes)
        self.n_ins += 1
        return (key, v)

    def wait_all(self, eng, ress):
        deps = {}
        for r in ress:
            if r.w is not None:
                sk, v = r.w
                if deps.get(sk, 0) < v:
                    deps[sk] = v
        self._wait(eng, deps.items())


def _kb_collective(self, ins_ap, outs_ap, reads=(), writes=()):
    if "cc" not in self.semh:
        self.semh["cc"] = self.es.enter_context(self.nc.semaphore("s_cc"))
        self.ncc = 0
    self._wait("pool", self._deps(reads, writes))
    ins = self.nc.gpsimd.collective_compute(
        "AllGather", ALU.bypass, replica_groups=[list(range(8))], ins=[ins_ap], outs=[outs_ap])
    self.ncc += 1
    ins.then_inc(self.semh["cc"], 1)
    self._mark(("cc", self.ncc), reads, writes)


def _kb_barrier(self):
    targets = []
    for k in self.engs:
        if self.cnt[k] > 0:
            targets.append((k, self.cnt[k]))
    for q, d in self.dq.items():
        for key, u in zip(d["keys"], d["uses"]):
            if u > 0:
                targets.append((key, 16 * u))
    if "cc" in self.semh and self.ncc > 0:
        targets.append(("cc", self.ncc))
    for e in self.engs:
        self._wait(e, [(sk, v) for sk, v in targets])


KB.collective = _kb_collective
KB.barrier = _kb_barrier


BF = ml_dtypes.bfloat16
NCORE = 8
D = 2048
KC = 16
CTX = 256
GRID_W = 64
EPS = 1e-6
SCALE = 128 ** -0.5
NEG = -30000.0


def split_cols(n, mx=512):
    k = (n + mx - 1) // mx
    base, rem = n // k, n % k
    out, o = [], 0
    for i in range(k):
        s = base + (1 if i < rem else 0)
        out.append((o, s))
        o += s
    return out


class Cfg:
    def __init__(self, S, DEPTH):
        self.S, self.L = S, DEPTH
        self.TL = S // NCORE
        self.TT = self.TL + CTX
        self.GL = self.TL // 2
        self.GN = self.GL + CTX
        self.TLR = self.TL // GRID_W
        self.GR = self.GL // GRID_W
        self.ROWS = S // GRID_W
        self.T1 = S // 128
        self.NKEY = S + CTX
        self.NCH = self.NKEY // 128
        self.NW = (self.TLR + 7) * 64


class Seg:
    def __init__(self, lo, tt, n, is_ctx):
        self.lo, self.tt, self.n, self.is_ctx = lo, tt, n, is_ctx

    def tiles(self):
        return [(self.lo + o, self.tt + o, s) for (o, s) in split_cols(self.n)]


def groups(cfg):
    return [[Seg(0, 0, cfg.GL, False), Seg(cfg.GL, cfg.TL, CTX, True)], [Seg(0, cfg.GL, cfg.GL, False)]]


def na_row_specs(cfg):
    TLR = cfg.TLR
    slots = [(0, None, None, 8)]
    rows = {}
    off = 8 * 64
    for lr in range(TLR):
        if lr < 4:
            k0, nk = lr - 4, 12 - lr
        elif lr >= TLR - 3:
            k0, nk = TLR - 8, lr + 4 - (TLR - 8)
        else:
            rows[lr] = (lr - 4, 8, 0)
            continue
        slots.append((len(slots), lr, k0, nk))
        rows[lr] = (k0, nk, off)
        off += nk * 64
    return {"slots": slots, "rows": rows, "ncols": off, "mid_lr": 4}


def MM(out, lhsT, rhs, start, stop):
    return lambda pe: pe.matmul(out, lhsT, rhs, start=start, stop=stop)


class LB:
    def __init__(self):
        self.nc = bass.Bass("TRN2", target_bir_lowering=False)
        self.kb = KB(self.nc)
        self.res = {}
        self.uid = 0
        self.outs = []
        self.bank = 0
        kb = self.kb
        self.ones_bf = kb.sb("ones_bf", [128, 128], BF16)
        self.ps_all = kb.ps("ps_all", [128, 4096], F32)
        self.PS = [self.ps_all[:, i * 512:(i + 1) * 512] for i in range(8)]
        self.RPS = [self.R("ps", i) for i in range(8)]
        self.rc = self.R("consts")
        kb.op("dve", lambda e: e.memset(self.ones_bf[:], 1.0), writes=[self.rc])
        self.epsb = kb.sb("epsb", [128, 1], F32)
        kb.op("dve", lambda e: e.memset(self.epsb[:], EPS), writes=[self.rc])

    def R(self, *key):
        r = self.res.get(key)
        if r is None:
            r = Res(str(key))
            self.res[key] = r
        return r

    def din(self, name, shape, dt=F32):
        return self.nc.dram_tensor(name, list(shape), dt, kind="ExternalInput").ap()

    def dout(self, name, shape, dt=F32):
        self.outs.append(name)
        return self.nc.dram_tensor(name, list(shape), dt, kind="ExternalOutput").ap()

    def dscr(self, name, shape, dt):
        return self.nc.dram_tensor(name, list(shape), dt, kind="Internal").ap()

    def tmp(self, name, shape, dt):
        self.uid += 1
        return self.nc.sbuf_tensor("t%d_%s" % (self.uid, name), list(shape), dt)

    def const(self, name, src, shape, dt=F32):
        t = self.kb.sb(name, shape, dt)
        self.kb.dma("sp", t[:], src, writes=[self.rc])
        return t

    def nb(self, lo=0, n=4):
        b = lo + self.bank % n
        self.bank += 1
        return b

    def finish(self):
        self.kb.barrier()
        self.kb.close()
        return self


def evac(lb, b, n, dst, rdst, idx, scale=None):
    kb = lb.kb
    if idx % 2 == 0:
        if scale is None:
            kb.op("act", lambda e: e.copy(dst, lb.PS[b][:, 0:n]), reads=[lb.RPS[b]], writes=[rdst])
        else:
            kb.op("act", lambda e: e.mul(dst, lb.PS[b][:, 0:n], scale), reads=[lb.RPS[b]], writes=[rdst])
    else:
        if scale is None:
            kb.op("dve", lambda e: e.tensor_copy(dst, lb.PS[b][:, 0:n]), reads=[lb.RPS[b]], writes=[rdst])
        else:
            kb.op("dve", lambda e: e.tensor_scalar_mul(dst, lb.PS[b][:, 0:n], scale), reads=[lb.RPS[b]], writes=[rdst])


def conv_weight(lb, name, W_in, K, M):
    Wb = lb.dscr("wb_" + name, [K, M], BF16)
    rp = max(128, ((8 << 20) // (M * 4)) // 128 * 128)
    ress = []
    for r0 in range(0, K, rp):
        r1 = min(K, r0 + rp)
        r = lb.R("wcv", name, r0)
        lb.kb.dma("pool", Wb[r0:r1, :], W_in[r0:r1, :], writes=[r])
        ress.append(r)
    return Wb, ress


def sweep(lb, st, xin, xres_fn, kcin, W, wres, blocks, bw, tiles, epi, wname):
    kb = lb.kb
    wb = [st.enter_context(lb.tmp(wname + str(i), [128, kcin, bw], BF16)) for i in range(2)]

    def issue(bi):
        c0 = blocks[bi][0]
        kb.dma("sp", wb[bi % 2][:], W[:, c0:c0 + bw].rearrange("(k p) m -> p k m", p=128), reads=wres,
               writes=[lb.R(wname, bi % 2)])

    issue(0)
    for bi, (c0, js) in enumerate(blocks):
        i = bi % 2
        rw = lb.R(wname, i)
        if bi + 1 < len(blocks):
            issue(bi + 1)
        for (lo, tt, n) in tiles:
            for j in js:
                b = lb.nb(0, 4)
                fns = [MM(lb.PS[b][:, 0:n], wb[i][:, k, j * 128:(j + 1) * 128], xin[:, k, lo:lo + n],
                          k == 0, k == kcin - 1) for k in range(kcin)]
                kb.pe_group(fns, reads=[rw, xres_fn(lo)], writes=[lb.RPS[b]])
                epi((c0 + j * 128) // 128, lo, tt, n, b)


def norm_mod(lb, st, src, segs, xn, A, modsb, sh0, dstT, tag):
    kb, PS, RPS, R = lb.kb, lb.PS, lb.RPS, lb.R
    hb = [st.enter_context(lb.tmp("hb%d" % i, [128, KC, 512], F32)) for i in range(2)]
    sq = [st.enter_context(lb.tmp("sq%d" % i, [128, KC, 512], BF16)) for i in range(2)]
    rstd = [st.enter_context(lb.tmp("rstd%d" % i, [128, 512], F32)) for i in range(2)]
    tf = [st.enter_context(lb.tmp("tf%d" % i, [128, 512], F32)) for i in range(2)]
    ti = 0
    for sg in segs:
        v = 1 if sg.is_ctx else 0
        for (lo, tt, n) in sg.tiles():
            i = ti % 2
            ti += 1
            rh, rs, rr = R(tag, "hb", i), R(tag, "sq", i), R(tag, "rstd", i)
            kb.dma("sp", hb[i][:, :, 0:n], src[:, :, tt:tt + n].rearrange("k p t -> p k t"),
                   reads=[R(tag, "src", tt)], writes=[rh])
            kb.op("act", lambda e, i=i, n=n: e.activation(sq[i][:, :, 0:n], hb[i][:, :, 0:n], AF.Square),
                  reads=[rh], writes=[rs])
            b = 4 + i
            kb.pe_group([MM(PS[b][:, 0:n], lb.ones_bf[:], sq[i][:, k, 0:n], k == 0, k == KC - 1) for k in range(KC)],
                        reads=[rs, lb.rc], writes=[RPS[b]])
            kb.op("act", lambda e, i=i, n=n, b=b: e.activation(rstd[i][:, 0:n], PS[b][:, 0:n], AF.Sqrt, bias=lb.epsb[:, 0:1],
                                                               scale=1.0 / D), reads=[RPS[b], lb.rc], writes=[rr])
            kb.op("dve", lambda e, i=i, n=n: e.reciprocal(rstd[i][:, 0:n], rstd[i][:, 0:n]), reads=[rr], writes=[rr])
            rx = R(tag, "xn", lo)
            for k in range(KC):
                j = k % 2
                rt = R(tag, "tf", j)
                kb.op("dve", lambda e, i=i, n=n, k=k, j=j, v=v: e.scalar_tensor_tensor(
                    tf[j][:, 0:n], hb[i][:, k, 0:n], A[:, k, v:v + 1], rstd[i][:, 0:n], ALU.mult, ALU.mult),
                    reads=[rh, rr, lb.rc], writes=[rt])
                kb.op("act", lambda e, n=n, k=k, j=j, v=v, lo=lo: e.activation(
                    xn[:, k, lo:lo + n], tf[j][:, 0:n], AF.Identity, bias=modsb[:, sh0 + k, v:v + 1], scale=1.0),
                    reads=[rt, lb.rc], writes=[rx])
            if dstT is not None:
                kb.dma("sp", dstT[:, :, tt:tt + n].rearrange("k p t -> p k t"), xn[:, :, lo:lo + n],
                       reads=[rx], writes=[R(tag, "dst", tt)])


def load_mods(lb, mods_in, norm_in, gc_scale):
    kb = lb.kb
    modsb = lb.const("modsb", mods_in, [128, 96, 2])
    nrm = lb.const("nrm", norm_in, [128, KC])
    A = kb.sb("Amod", [128, KC, 2], F32)
    for v in range(2):
        kb.op("dve", lambda e, v=v: e.tensor_scalar(A[:, :, v], modsb[:, gc_scale:gc_scale + KC, v], 1.0, None, ALU.add),
              reads=[lb.rc], writes=[lb.rc])
        kb.op("dve", lambda e, v=v: e.tensor_tensor(A[:, :, v], A[:, :, v], nrm[:], ALU.mult),
              reads=[lb.rc], writes=[lb.rc])
    return modsb, A


def build_M(cfg):
    lb = LB()
    nc, kb, R, PS, RPS = lb.nc, lb.kb, lb.R, lb.PS, lb.RPS
    L = cfg.L
    cvec_in = lb.din("cvec", [128, KC, 2])
    wada_in = lb.din("w_ada", [L, D, 1536])
    bada_in = lb.din("b_ada", [128, L, 12])
    mout = lb.dout("modloc", [128, L * 24])
    with contextlib.ExitStack() as st:
        csil = st.enter_context(lb.tmp("csil", [128, KC, 2], F32))
        craw = st.enter_context(lb.tmp("craw", [128, KC, 2], F32))
        bada = st.enter_context(lb.tmp("bada", [128, L, 12], F32))
        mloc = st.enter_context(lb.tmp("mloc", [128, L, 12, 2], F32))
        wa = [st.enter_context(lb.tmp("wa%d" % i, [128, KC, 512], F32)) for i in range(2)]
        rcs = R("csil")
        kb.dma("sp", craw[:], cvec_in, writes=[rcs])
        kb.dma("sp", bada[:], bada_in, writes=[rcs])
        kb.op("act", lambda e: e.activation(csil[:], craw[:], AF.Silu), reads=[rcs], writes=[rcs])
        it = 0
        for l in range(L):
            for cb in range(3):
                i = it % 2
                it += 1
                rw = R("wa", i)
                kb.dma("sp", wa[i][:], wada_in[l, :, cb * 512:(cb + 1) * 512].rearrange("(k p) m -> p k m", p=128),
                       writes=[rw])
                b = 4 + i
                fns = []
                for j in range(4):
                    for k in range(KC):
                        fns.append(MM(PS[b][:, j * 2:(j + 1) * 2], wa[i][:, k, j * 128:(j + 1) * 128], csil[:, k, :],
                                      k == 0, k == KC - 1))
                kb.pe_group(fns, reads=[rw, rcs], writes=[RPS[b]])
                kb.op("dve", lambda e, l=l, cb=cb, b=b: e.tensor_tensor(
                    mloc[:, l, cb * 4:(cb + 1) * 4, :], PS[b][:, 0:8].rearrange("p (j v) -> p j v", v=2),
                    bada[:, l, cb * 4:(cb + 1) * 4].unsqueeze(2).to_broadcast([128, 4, 2]), ALU.add),
                    reads=[RPS[b], rcs], writes=[R("mloc")])
        kb.dma("sp", mout, mloc[:].rearrange("p l j v -> p (l j v)"), reads=[R("mloc")], writes=[R("mout")])
        kb.barrier()
    return lb.finish()


def build_A(cfg):
    lb = LB()
    nc, kb, R, PS, RPS = lb.nc, lb.kb, lb.R, lb.PS, lb.RPS
    TL, TT, GL, GN = cfg.TL, cfg.TT, cfg.GL, cfg.GN
    hT = lb.din("hT", [KC, 128, TT])
    mods_in = lb.din("mods", [128, 96, 2])
    norm1_in = lb.din("norm1", [128, KC])
    w_in = lb.din("w_in", [D, 5120])
    gains_in = lb.din("gains", [128, 4])
    ropec_in = lb.din("ropec", [128, TL])
    ropes_in = lb.din("ropes", [128, TL])
    rotT_in = lb.din("rotT", [128, 128])
    fcs_in = lb.din("f_cs", [128, 256])
    xnT = lb.dout("xnT", [KC, 128, TT], BF16)
    zc = lb.dout("zc", [12, 128, TT], F32)
    zfc = lb.dout("zfc", [4, 128, CTX], F32)
    qT = lb.dout("qT", [8, 128, TT], BF16)
    nqT = lb.dout("nqT", [4, 128, TT], BF16)
    kT = lb.dout("kT", [2, 128, TT], BF16)
    knT = lb.dout("knT", [4, 128, TT], BF16)
    vg = lb.dout("vg", [TT, 256], BF16)
    vn = lb.dout("vn", [TT, 512], BF16)
    fx = lb.dout("fx", [TL // 128, 1024, 128], F32)
    w_in_f32 = w_in
    w_in, r_win = conv_weight(lb, "w_in", w_in_f32, D, 5120)

    modsb, A1 = load_mods(lb, mods_in, norm1_in, 16)
    gains = lb.const("gains", gains_in, [128, 4])
    rotT = lb.const("rotT", rotT_in, [128, 128])
    fcs = lb.const("fcs", fcs_in, [128, 256])
    cosF = lb.const("cosF", ropec_in, [128, TL])
    sinF = lb.const("sinF", ropes_in, [128, TL])
    rc = lb.rc

    for gi, segs in enumerate(groups(cfg)):
        with contextlib.ExitStack() as st:
            xn = st.enter_context(lb.tmp("xn", [128, KC, GN], BF16))
            tag = "g%d" % gi
            with contextlib.ExitStack() as st2:
                norm_mod(lb, st2, hT, segs, xn, A1, modsb, 0, xnT, tag)
                kb.barrier()
            tiles = []
            for sg in segs:
                for t in sg.tiles():
                    tiles.append(t + (sg.is_ctx,))
            stg = [st.enter_context(lb.tmp("stg%d" % i, [128, 512], F32)) for i in range(4)]
            y0 = [st.enter_context(lb.tmp("y0%d" % i, [128, 512], F32)) for i in range(2)]
            sqh = [st.enter_context(lb.tmp("sqh%d" % i, [128, 512], BF16)) for i in range(2)]
            rsh = [st.enter_context(lb.tmp("rsh%d" % i, [128, 512], F32)) for i in range(2)]
            yy = [st.enter_context(lb.tmp("yy%d" % i, [128, 512], F32)) for i in range(2)]
            o1 = [st.enter_context(lb.tmp("o1%d" % i, [128, 512], F32)) for i in range(2)]
            o2 = [st.enter_context(lb.tmp("o2%d" % i, [128, 512], F32)) for i in range(2)]
            ob = [st.enter_context(lb.tmp("ob%d" % i, [128, 512], BF16)) for i in range(2)]
            cnt = {"e": 0, "h": 0}
            isctx = {(lo, tt): c for (lo, tt, n, c) in tiles}

            def epi(c, lo, tt, n, b):
                ctx_t = isctx[(lo, tt)]
                e = cnt["e"]
                cnt["e"] += 1
                if c < 12:
                    s = stg[e % 4]
                    rs_ = R(tag, "stg", e % 4)
                    evac(lb, b, n, s[:, 0:n], rs_, e)
                    kb.dma("sp", zc[c, :, tt:tt + n], s[:, 0:n], reads=[rs_], writes=[R(tag, "zc", c, tt)])
                elif c < 16:
                    g = c - 12
                    s = stg[e % 4]
                    rs_ = R(tag, "stg", e % 4)
                    evac(lb, b, n, s[:, 0:n], rs_, e)
                    if ctx_t:
                        kb.dma("sp", zfc[g, :, tt - TL:tt - TL + n], s[:, 0:n], reads=[rs_], writes=[R(tag, "zfc", g)])
                    else:
                        for part in range(2):
                            b2 = 4 + 2 * part + (e % 2)
                            kb.pe_group([MM(PS[b2][:, 0:n], fcs[:, part * 128:(part + 1) * 128], s[:, 0:n], True, True)],
                                        reads=[rs_, rc], writes=[RPS[b2]])
                            o = o1[e % 2] if part == 0 else o2[e % 2]
                            ro = R(tag, "ob%d" % (part + 1), e % 2)
                            evac(lb, b2, n, o[:, 0:n], ro, e + part)
                            col0 = part * 512 + g * 128
                            kb.dma("sp", fx[tt // 128:(tt + n) // 128, col0:col0 + 128, :].rearrange("b m t -> m b t"),
                                   o[:, 0:n].rearrange("m (b t) -> m b t", t=128), reads=[ro], writes=[R(tag, "fx", c, tt, part)])
                else:
                    if c < 20:
                        gi_, dst, rope = 0, nqT[c - 16], False
                    elif c < 24:
                        gi_, dst, rope = 1, knT[c - 20], False
                    elif c < 36:
                        gi_, dst, rope = 2, qT[c - 28], True
                    else:
                        gi_, dst, rope = 3, kT[c - 36], True
                    rope = rope and not ctx_t
                    i = cnt["h"] % 2
                    cnt["h"] += 1
                    ry, rq, rr, ryy, rob = R(tag, "y0", i), R(tag, "sqh", i), R(tag, "rsh", i), R(tag, "yy", i), R(tag, "ob", i)
                    kb.op("act", lambda e_: e_.copy(y0[i][:, 0:n], PS[b][:, 0:n]), reads=[RPS[b]], writes=[ry])
                    kb.op("act", lambda e_: e_.activation(sqh[i][:, 0:n], PS[b][:, 0:n], AF.Square), reads=[RPS[b]], writes=[rq])
                    b2 = 4 + i
                    kb.pe_group([MM(PS[b2][:, 0:n], lb.ones_bf[:], sqh[i][:, 0:n], True, True)], reads=[rq, rc], writes=[RPS[b2]])
                    kb.op("act", lambda e_: e_.activation(rsh[i][:, 0:n], PS[b2][:, 0:n], AF.Sqrt, bias=lb.epsb[:, 0:1],
                                                          scale=1.0 / 128), reads=[RPS[b2], rc], writes=[rr])
                    kb.op("dve", lambda e_: e_.reciprocal(rsh[i][:, 0:n], rsh[i][:, 0:n]), reads=[rr], writes=[rr])
                    kb.op("dve", lambda e_: e_.scalar_tensor_tensor(yy[i][:, 0:n], y0[i][:, 0:n], gains[:, gi_:gi_ + 1],
                                                                     rsh[i][:, 0:n], ALU.mult, ALU.mult),
                          reads=[ry, rr, rc], writes=[ryy])
                    if rope:
                        b3 = 6 + i
                        kb.pe_group([MM(PS[b3][:, 0:n], rotT[:], yy[i][:, 0:n], True, True)], reads=[ryy, rc], writes=[RPS[b3]])
                        r1, r2 = R(tag, "ob1", i), R(tag, "ob2", i)
                        kb.op("pool", lambda e_: e_.tensor_tensor(o1[i][:, 0:n], yy[i][:, 0:n], cosF[:, tt:tt + n], ALU.mult),
                              reads=[ryy, rc], writes=[r1])
                        kb.op("dve", lambda e_: e_.tensor_tensor(o2[i][:, 0:n], PS[b3][:, 0:n], sinF[:, tt:tt + n], ALU.mult),
                              reads=[RPS[b3], rc], writes=[r2])
                        kb.op("pool", lambda e_: e_.tensor_tensor(ob[i][:, 0:n], o1[i][:, 0:n], o2[i][:, 0:n], ALU.add),
                              reads=[r1, r2], writes=[rob])
                    else:
                        kb.op("pool", lambda e_: e_.tensor_copy(ob[i][:, 0:n], yy[i][:, 0:n]), reads=[ryy], writes=[rob])
                    kb.dma("sp", dst[:, tt:tt + n], ob[i][:, 0:n], reads=[rob], writes=[R(tag, "hd", c, tt)])

            blocks = [(0, [0, 1, 2, 3]), (512, [0, 1, 2, 3]), (1024, [0, 1, 2, 3]), (1536, [0, 1, 2, 3]),
                      (2048, [0, 1, 2, 3]), (2560, [0, 1, 2, 3]), (3584, [0, 1, 2, 3]), (4096, [0, 1, 2, 3]),
                      (4608, [0, 1])]
            sweep(lb, st, xn, lambda lo: R(tag, "xn", lo), KC, w_in, r_win, blocks, 512, [t[:3] for t in tiles], epi, "wA")
            wv = st.enter_context(lb.tmp("wv", [128, KC, 256], BF16))
            wvn = st.enter_context(lb.tmp("wvn", [128, KC, 512], BF16))
            vst = [st.enter_context(lb.tmp("vst%d" % i, [128, 768], BF16)) for i in range(2)]
            rwv = R(tag, "wv")
            kb.dma("sp", wv[:], w_in[:, 4864:5120].rearrange("(k p) m -> p k m", p=128), reads=r_win, writes=[rwv])
            kb.dma("sp", wvn[:], w_in[:, 3072:3584].rearrange("(k p) m -> p k m", p=128), reads=r_win, writes=[R(tag, "wvn")])
            bi = 0
            for sg in segs:
                for t0 in range(0, sg.n, 128):
                    lo, tt = sg.lo + t0, sg.tt + t0
                    i = bi % 2
                    bi += 1
                    ba, bb = 4 + i, 6 + i
                    rx = R(tag, "xn", sg.lo + (t0 // 512) * 512 if False else [tl for (tl, _, n_) in sg.tiles() if tl <= lo < tl + n_][0])
                    kb.pe_group([MM(PS[ba][:, 0:256], xn[:, k, lo:lo + 128], wv[:, k, :], k == 0, k == KC - 1) for k in range(KC)],
                                reads=[rx, rwv], writes=[RPS[ba]])
                    kb.pe_group([MM(PS[bb][:, 0:512], xn[:, k, lo:lo + 128], wvn[:, k, :], k == 0, k == KC - 1) for k in range(KC)],
                                reads=[rx, R(tag, "wvn")], writes=[RPS[bb]])
                    rv = R(tag, "vst", i)
                    kb.op("act", lambda e_, i=i, ba=ba: e_.copy(vst[i][:, 0:256], PS[ba][:, 0:256]), reads=[RPS[ba]], writes=[rv])
                    kb.op("dve", lambda e_, i=i, bb=bb: e_.tensor_copy(vst[i][:, 256:768], PS[bb][:, 0:512]), reads=[RPS[bb]], writes=[rv])
                    kb.dma("sp", vg[tt:tt + 128, :], vst[i][:, 0:256], reads=[rv], writes=[R(tag, "vg", tt)])
                    kb.dma("sp", vn[tt:tt + 128, :], vst[i][:, 256:768], reads=[rv], writes=[R(tag, "vn", tt)])
            kb.barrier()
    return lb.finish()


_PROG = {}


def get_prog(name, cfg, builder):
    key = (name, cfg.S, cfg.L)
    if key not in _PROG:
        _PROG[key] = builder(cfg)
    return _PROG[key]


def launch(lb, maps):
    res = run_bass_kernel_spmd(lb.nc, maps, core_ids=list(range(NCORE)))
    return res.results


def host_consts(cfg):
    f32 = np.float32
    S, T1, TL = cfg.S, cfg.T1, cfg.TL
    c = {}
    c["ident"] = np.eye(128, dtype=f32)
    rotT = np.zeros((128, 128), f32)
    for i in range(64):
        rotT[i + 64, i] = -1.0
        rotT[i, i + 64] = 1.0
    c["rotT"] = rotT
    a = np.arange(128)
    ang = 2 * np.pi * np.outer(a, a) / 128.0
    c["f_cs"] = np.concatenate([np.cos(ang), np.sin(ang)], 1).astype(f32)
    a1 = np.arange(T1)
    ang1 = 2 * np.pi * np.outer(a1, a1) / T1
    c["f_s1"] = np.ascontiguousarray(np.stack([np.cos(ang1), -np.sin(ang1), -np.cos(ang1)], 1).astype(f32))
    angt = 2 * np.pi * np.outer(a1, a) / float(S)
    c["f_tw"] = np.ascontiguousarray(np.stack([np.cos(angt), np.sin(angt)], 1).astype(f32))
    ac = np.arange(256)
    angc = 2 * np.pi * np.outer(ac, ac) / 256.0
    fc = np.stack([np.cos(angc), -np.sin(angc)], 1).astype(f32)
    c["f_ctx"] = np.ascontiguousarray(fc.reshape(2, 128, 2, 256).transpose(1, 0, 2, 3))
    inv_freq = (np.float32(10000.0) ** (-np.arange(32, dtype=f32) / np.float32(32))).astype(f32)
    c["ropec"], c["ropes"] = [], []
    for core in range(NCORE):
        t = np.arange(core * TL, (core + 1) * TL)
        row = (t // GRID_W).astype(f32)
        cl = (t % GRID_W).astype(f32)
        angr = np.concatenate([row[:, None] * inv_freq, cl[:, None] * inv_freq], -1).astype(f32)
        cosv = np.cos(angr).astype(f32).T
        sinv = np.sin(angr).astype(f32).T
        c["ropec"].append(np.ascontiguousarray(np.concatenate([cosv, cosv], 0)))
        c["ropes"].append(np.ascontiguousarray(np.concatenate([sinv, sinv], 0)))
    return c


def pl(a, nch):
    return np.ascontiguousarray(np.asarray(a, np.float32).reshape(nch, 128).T)


def to_fm(a):
    T, F = a.shape
    return np.ascontiguousarray(a.T.reshape(F // 128, 128, T))


def from_fm(a):
    return np.ascontiguousarray(a.reshape(-1, a.shape[2]).T)


def run_M(inp, cfg):
    L = cfg.L
    f32 = np.float32
    cv = np.stack([np.asarray(inp["c"], f32)[0], np.asarray(inp["c_ctx"], f32)], -1)
    cvec = np.ascontiguousarray(cv.reshape(KC, 128, 2).transpose(1, 0, 2))
    maps = []
    for c in range(NCORE):
        ba = np.asarray(inp["b_ada"], f32)[:L, c * 1536:(c + 1) * 1536]
        maps.append({"cvec": cvec,
                     "w_ada": np.ascontiguousarray(np.asarray(inp["w_ada"], f32)[:L, :, c * 1536:(c + 1) * 1536]),
                     "b_ada": np.ascontiguousarray(ba.reshape(L, 12, 128).transpose(2, 0, 1))})
    res = launch(get_prog("M", cfg, build_M), maps)
    allm = np.stack([res[c]["modloc"].reshape(128, L, 12, 2) for c in range(NCORE)], 0)
    mods = [np.ascontiguousarray(allm[:, :, l].transpose(1, 0, 2, 3).reshape(128, 96, 2)) for l in range(L)]
    return mods


def run_A(inp, cfg, l, hT, mods, hc):
    f32 = np.float32
    gains = np.ascontiguousarray(np.stack([np.asarray(inp[k], f32)[l] for k in
                                           ("na_q_gain", "na_k_gain", "gqa_q_gain", "gqa_k_gain")], -1))
    w_in = np.ascontiguousarray(np.asarray(inp["w_in"], f32)[l])
    n1 = pl(inp["norm1"][l], KC)
    maps = []
    for c in range(NCORE):
        maps.append({"hT": hT[c], "mods": mods[l], "norm1": n1, "w_in": w_in, "gains": gains,
                     "ropec": hc["ropec"][c], "ropes": hc["ropes"][c], "rotT": hc["rotT"], "f_cs": hc["f_cs"]})
    return launch(get_prog("A", cfg, build_A), maps)


def build_B(cfg):
    lb = LB()
    nc, kb, R, PS, RPS = lb.nc, lb.kb, lb.R, lb.PS, lb.RPS
    T1, S = cfg.T1, cfg.S
    xs_in = lb.din("xs", [T1, 2, 64, 128])
    fs1_in = lb.din("f_s1", [T1, 3, T1])
    ftw_in = lb.din("f_tw", [T1, 2, 128])
    fcs_in = lb.din("f_cs", [128, 256])
    ident_in = lb.din("ident", [128, 128])
    fy = lb.dout("fy", [128, 64, T1], F32)
    fs1 = lb.const("fs1", fs1_in, [T1, 3, T1])
    ftw = lb.const("ftw", ftw_in, [T1, 2, 128])
    fcs = lb.const("fcs", fcs_in, [128, 256])
    ident = lb.const("ident", ident_in, [128, 128])
    rc = lb.rc
    norm = 1.0 / math.sqrt(float(S) * 128.0)
    with contextlib.ExitStack() as st:
        xs = st.enter_context(lb.tmp("xs", [T1, 2, 64, 128], F32))
        V = st.enter_context(lb.tmp("V", [128, 2, 64, T1], F32))
        Y = st.enter_context(lb.tmp("Y", [128, 64, T1], F32))
        tt_ = [st.enter_context(lb.tmp("tw%d" % i, [T1, 4, 128], F32)) for i in range(4)]
        for blk in range(16):
            for part in range(2):
                kb.dma("sp", xs[:, part, 4 * blk:4 * blk + 4, :], xs_in[:, part, 4 * blk:4 * blk + 4, :],
                       writes=[R("xs", blk)])
        tcb = ftw[:, 0:1, :].to_broadcast([T1, 4, 128])
        tsb = ftw[:, 1:2, :].to_broadcast([T1, 4, 128])
        for blk in range(16):
            rx = R("xs", blk)
            Ab = xs[:, 0, 4 * blk:4 * blk + 4, :].rearrange("p c t -> p (c t)")
            Bb = xs[:, 1, 4 * blk:4 * blk + 4, :].rearrange("p c t -> p (c t)")
            b0, b1 = (blk % 2) * 2, (blk % 2) * 2 + 1
            kb.pe_group([MM(PS[b0][0:T1, :], fs1[:, 0, :], Ab, True, False), MM(PS[b0][0:T1, :], fs1[:, 1, :], Bb, False, True)],
                        reads=[rx, rc], writes=[RPS[b0]])
            kb.pe_group([MM(PS[b1][0:T1, :], fs1[:, 2, :], Bb, True, False), MM(PS[b1][0:T1, :], fs1[:, 1, :], Ab, False, True)],
                        reads=[rx, rc], writes=[RPS[b1]])
            ure = PS[b0][0:T1, :].rearrange("p (c t) -> p c t", c=4)
            uim = PS[b1][0:T1, :].rearrange("p (c t) -> p c t", c=4)
            rt = [R("tw", i) for i in range(4)]
            kb.op("dve", lambda e: e.tensor_tensor(tt_[0][:], ure, tcb, ALU.mult), reads=[RPS[b0], rc], writes=[rt[0]])
            kb.op("dve", lambda e: e.tensor_tensor(tt_[1][:], uim, tsb, ALU.mult), reads=[RPS[b1], rc], writes=[rt[1]])
            kb.op("dve", lambda e: e.tensor_tensor(tt_[2][:], uim, tcb, ALU.mult), reads=[RPS[b1], rc], writes=[rt[2]])
            kb.op("dve", lambda e: e.tensor_tensor(tt_[3][:], ure, tsb, ALU.mult), reads=[RPS[b0], rc], writes=[rt[3]])
            kb.op("pool", lambda e: e.tensor_tensor(xs[:, 0, 4 * blk:4 * blk + 4, :], tt_[0][:], tt_[1][:], ALU.add),
                  reads=[rt[0], rt[1]], writes=[rx])
            kb.op("pool", lambda e: e.tensor_tensor(xs[:, 1, 4 * blk:4 * blk + 4, :], tt_[2][:], tt_[3][:], ALU.subtract),
                  reads=[rt[2], rt[3]], writes=[rx])
        ei = 0
        for part in range(2):
            for cg in range(16):
                b = 4 + (ei % 4)
                fns = [lambda pe, j=j: pe.transpose(PS[b][:, j * T1:(j + 1) * T1], xs[:, part, 4 * cg + j, :], ident[0:T1, 0:T1])
                       for j in range(4)]
                kb.pe_group(fns, reads=[R("xs", cg), rc], writes=[RPS[b]])
                evac(lb, b, 4 * T1, V[:, part, 4 * cg:4 * cg + 4, :].rearrange("p c k -> p (c k)"), R("V", part, cg), ei)
                ei += 1
        cb = 512 // T1
        for blk in range(64 // cb):
            b = blk % 4
            reads = [R("V", p_, cg) for p_ in range(2) for cg in range((blk * cb) // 4, max((blk * cb) // 4 + 1, ((blk + 1) * cb + 3) // 4))]
            vre = V[:, 0, blk * cb:(blk + 1) * cb, :].rearrange("p c k -> p (c k)")
            vim = V[:, 1, blk * cb:(blk + 1) * cb, :].rearrange("p c k -> p (c k)")
            kb.pe_group([MM(PS[b][:, :], fcs[:, 0:128], vre, True, False), MM(PS[b][:, :], fcs[:, 128:256], vim, False, True)],
                        reads=reads + [rc], writes=[RPS[b]])
            evac(lb, b, 512, Y[:, blk * cb:(blk + 1) * cb, :].rearrange("p c k -> p (c k)"), R("Y"), blk, scale=norm)
        kb.dma("sp", fy, Y[:], reads=[R("Y")], writes=[R("fy")])
        kb.barrier()
    return lb.finish()


def run_B(cfg, rA, hc):
    T1 = cfg.T1
    fx_all = np.concatenate([np.asarray(rA[c]["fx"]) for c in range(NCORE)], 0)
    maps = []
    for c in range(NCORE):
        xs = np.stack([fx_all[:, 64 * c:64 * c + 64, :], fx_all[:, 512 + 64 * c:512 + 64 * c + 64, :]], 1)
        maps.append({"xs": np.ascontiguousarray(xs), "f_s1": hc["f_s1"], "f_tw": hc["f_tw"], "f_cs": hc["f_cs"],
                     "ident": hc["ident"]})
    res = launch(get_prog("B", cfg, build_B), maps)
    yall = np.stack([np.asarray(res[c]["fy"]) for c in range(NCORE)], 0)
    ycols = yall.transpose(0, 2, 1, 3).reshape(512, cfg.S)
    return [np.ascontiguousarray(ycols[:, r * cfg.TL:(r + 1) * cfg.TL].reshape(4, 128, cfg.TL)) for r in range(NCORE)]


def build_C(cfg):
    lb = LB()
    nc, kb, R, PS, RPS = lb.nc, lb.kb, lb.R, lb.PS, lb.RPS
    TL, TT, GL, GN, S = cfg.TL, cfg.TT, cfg.GL, cfg.GN, cfg.S
    TLR, GR, NKEY, NCH, NW = cfg.TLR, cfg.GR, cfg.NKEY, cfg.NCH, cfg.NW
    specs = na_row_specs(cfg)
    NTAB = specs["ncols"]
    hT = lb.din("hT", [KC, 128, TT])
    xnT = lb.din("xnT", [KC, 128, TT], BF16)
    zc = lb.din("zc", [12, 128, TT])
    zce = lb.din("zce", [8, 128, TL + 2])
    zfc = lb.din("zfc", [4, 128, CTX])
    qT = lb.din("qT", [8, 128, TT], BF16)
    nqT = lb.din("nqT", [4, 128, TT], BF16)
    kTall = lb.din("kTall", [2, 128, NKEY], BF16)
    vgall = lb.din("vgall", [NKEY, 256], BF16)
    knw = lb.din("knw", [4, 128, NW], BF16)
    vnw = lb.din("vnw", [NW, 512], BF16)
    cnkT = lb.din("cnkT", [4, 128, CTX], BF16)
    cvn = lb.din("cvn", [CTX, 512], BF16)
    natab = lb.din("natab", [64, 4, NTAB])
    fyT = lb.din("fyT", [4, 128, TL])
    mods_in = lb.din("mods", [128, 96, 2])
    norm2_in = lb.din("norm2", [128, KC])
    bgate_in = lb.din("b_gate", [128, 64])
    convw_in = lb.din("conv_w", [128, 3, 4])
    fcs_in = lb.din("f_cs", [128, 256])
    fctx_in = lb.din("f_ctx", [128, 2, 2, 256])
    w_gate = lb.din("w_gate", [D, 8192])
    w_outs = [lb.din("w_conv_out", [512, D]), lb.din("w_fourier_out", [512, D]),
              lb.din("w_na_out", [512, D]), lb.din("w_gqa_out", [1024, D])]
    w_o = lb.din("w_o", [D, D])
    hmT = lb.dout("hmT", [KC, 128, TT], F32)
    xn2T = lb.dout("xn2T", [KC, 128, TT], BF16)
    gT = lb.dscr("gT", [64, 128, TT], BF16)
    ysT = lb.dscr("ysT", [20, 128, TT], BF16)
    mgT = lb.dscr("mgT", [KC, 128, TT], BF16)
    w_gate, r_wg = conv_weight(lb, "w_gate", w_gate, D, 8192)
    r_wouts = []
    for b_, (nm, kk) in enumerate((("w_conv_out", 512), ("w_fourier_out", 512), ("w_na_out", 512), ("w_gqa_out", 1024))):
        w_outs[b_], rr_ = conv_weight(lb, nm, w_outs[b_], kk, D)
        r_wouts.append(rr_)
    w_o, r_wo = conv_weight(lb, "w_o", w_o, D, D)

    modsb, A2 = load_mods(lb, mods_in, norm2_in, 64)
    bgate = lb.const("bgate", bgate_in, [128, 64])
    convw = lb.const("convw", convw_in, [128, 3, 4])
    fcs = lb.const("fcs", fcs_in, [128, 256])
    fctx = lb.const("fctx", fctx_in, [128, 2, 2, 256])
    rc = lb.rc

    for gi, segs in enumerate(groups(cfg)):
        tag = "g%d" % gi
        tiles = []
        for sg in segs:
            for t in sg.tiles():
                tiles.append(t + (sg.is_ctx,))
        t3 = [t[:3] for t in tiles]
        isctx = {(lo, tt): c for (lo, tt, n, c) in tiles}

        def load_res(dst, src, nch, key):
            for (lo, tt, n) in t3:
                kb.dma("sp", dst[:, 0:nch, lo:lo + n], src[0:nch, :, tt:tt + n].rearrange("k p t -> p k t"),
                       writes=[R(tag, key, lo)])

        with contextlib.ExitStack() as st:
            xn = st.enter_context(lb.tmp("xn", [128, KC, GN], BF16))
            gst = [st.enter_context(lb.tmp("gst%d" % i, [128, 512], BF16)) for i in range(4)]
            load_res(xn, xnT, KC, "xn")
            cnt = {"e": 0}

            def epi_g(c, lo, tt, n, b):
                e = cnt["e"] % 4
                cnt["e"] += 1
                rg = R(tag, "gst", e)
                kb.op("act", lambda e_: e_.activation(gst[e][:, 0:n], PS[b][:, 0:n], AF.Sigmoid, bias=bgate[:, c:c + 1], scale=1.0),
                      reads=[RPS[b], rc], writes=[rg])
                kb.dma("sp", gT[c, :, tt:tt + n], gst[e][:, 0:n], reads=[rg], writes=[R(tag, "gT", c, tt)])

            sweep(lb, st, xn, lambda lo: R(tag, "xn", lo), KC, w_gate, r_wg, [(512 * i, [0, 1, 2, 3]) for i in range(16)], 512,
                  t3, epi_g, "wG")
            kb.barrier()

        with contextlib.ExitStack() as st:
            NB = GL + 2
            xa = [st.enter_context(lb.tmp("xa%d" % i, [128, NB], F32)) for i in range(2)]
            cg = [st.enter_context(lb.tmp("cg%d" % i, [128, NB], F32)) for i in range(2)]
            bg = [st.enter_context(lb.tmp("bg%d" % i, [128, NB], F32)) for i in range(2)]
            uu = [st.enter_context(lb.tmp("uu%d" % i, [128, NB], F32)) for i in range(2)]
            tc_ = [st.enter_context(lb.tmp("tc%d" % i, [128, NB], F32)) for i in range(2)]
            yb = [st.enter_context(lb.tmp("yb%d" % i, [128, NB], BF16)) for i in range(2)]
            it = 0
            for sg in segs:
                n = sg.n
                for j in range(4):
                    i = it % 2
                    it += 1
                    rin, ru, rt_, ry = R(tag, "cin", i), R(tag, "cu", i), R(tag, "ct", i), R(tag, "cy", i)
                    if not sg.is_ctx:
                        kb.dma("sp", xa[i][:, 0:n + 2], zce[j, :, sg.tt:sg.tt + n + 2], writes=[rin])
                        kb.dma("sp", cg[i][:, 0:n + 2], zce[4 + j, :, sg.tt:sg.tt + n + 2], writes=[rin])
                    else:
                        kb.op("pool", lambda e_: e_.memset(xa[i][:, 0:n + 2], 0.0), writes=[rin])
                        kb.op("pool", lambda e_: e_.memset(cg[i][:, 0:n + 2], 0.0), writes=[rin])
                        kb.dma("sp", xa[i][:, 1:n + 1], zc[j, :, sg.tt:sg.tt + n], writes=[rin])
                        kb.dma("sp", cg[i][:, 1:n + 1], zc[8 + j, :, sg.tt:sg.tt + n], writes=[rin])
                    kb.dma("sp", bg[i][:, 0:n], zc[4 + j, :, sg.tt:sg.tt + n], writes=[rin])
                    kb.op("pool", lambda e_: e_.tensor_tensor(uu[i][:, 0:n + 2], xa[i][:, 0:n + 2], cg[i][:, 0:n + 2], ALU.mult),
                          reads=[rin], writes=[ru])
                    kb.op("dve", lambda e_: e_.tensor_scalar(tc_[i][:, 0:n], uu[i][:, 0:n], convw[:, 0, j:j + 1], None, ALU.mult),
                          reads=[ru, rc], writes=[rt_])
                    kb.op("dve", lambda e_: e_.scalar_tensor_tensor(tc_[i][:, 0:n], uu[i][:, 1:n + 1], convw[:, 1, j:j + 1],
                                                                     tc_[i][:, 0:n], ALU.mult, ALU.add), reads=[ru, rc, rt_], writes=[rt_])
                    kb.op("dve", lambda e_: e_.scalar_tensor_tensor(tc_[i][:, 0:n], uu[i][:, 2:n + 2], convw[:, 2, j:j + 1],
                                                                     tc_[i][:, 0:n], ALU.mult, ALU.add), reads=[ru, rc, rt_], writes=[rt_])
                    kb.op("pool", lambda e_: e_.tensor_tensor(yb[i][:, 0:n], tc_[i][:, 0:n], bg[i][:, 0:n], ALU.mult),
                          reads=[rt_, rin], writes=[ry])
                    kb.dma("sp", ysT[j, :, sg.tt:sg.tt + n], yb[i][:, 0:n], reads=[ry], writes=[R(tag, "ys", j, sg.tt)])
            kb.barrier()

        with contextlib.ExitStack() as st:
            for sg in segs:
                if not sg.is_ctx:
                    for g in range(4):
                        kb.dma("pool", ysT[4 + g, :, sg.tt:sg.tt + sg.n], fyT[g, :, sg.tt:sg.tt + sg.n],
                               writes=[R(tag, "ys", 4 + g, sg.tt)])
                else:
                    zf = st.enter_context(lb.tmp("zf", [128, 4, CTX], F32))
                    xtm = st.enter_context(lb.tmp("xtm", [128, 2, 4, 256], F32))
                    yo = st.enter_context(lb.tmp("yo", [128, 4, CTX], BF16))
                    kb.dma("sp", zf[:], zfc.rearrange("g p t -> p g t"), writes=[R(tag, "zf")])
                    e = 0
                    for blk in range(2):
                        for g in range(4):
                            b = e % 4
                            kb.pe_group([MM(PS[b][:, 0:256], zf[:, g, blk * 128:(blk + 1) * 128], fcs[:, 0:256], True, True)],
                                        reads=[R(tag, "zf"), rc], writes=[RPS[b]])
                            evac(lb, b, 256, xtm[:, blk, g, :], R(tag, "xtm"), e)
                            e += 1
                    for g in range(4):
                        b = 4 + g
                        fns = []
                        for blk in range(2):
                            fns.append(MM(PS[b][:, 0:256], xtm[:, blk, g, 0:128], fctx[:, blk, 0, :], blk == 0, False))
                            fns.append(MM(PS[b][:, 0:256], xtm[:, blk, g, 128:256], fctx[:, blk, 1, :], False, blk == 1))
                        kb.pe_group(fns, reads=[R(tag, "xtm"), rc], writes=[RPS[b]])
                        evac(lb, b, 256, yo[:, g, :], R(tag, "yo"), g, scale=1.0 / math.sqrt(256.0 * 128.0))
                    kb.dma("sp", ysT[4:8, :, TL:TT].rearrange("g p t -> p g t"), yo[:], reads=[R(tag, "yo")],
                           writes=[R(tag, "ys", "fctx")])
            kb.barrier()

        with contextlib.ExitStack() as st:
            lr0 = gi * GR
            kmin = lr0 - 4
            kmax = max([specs["rows"][lr][0] + specs["rows"][lr][1] for lr in range(lr0, lr0 + GR)])
            nrw = kmax - kmin
            sp_rows = [lr for lr in range(lr0, lr0 + GR) if specs["rows"][lr][2] != 0]
            s0 = min([specs["rows"][lr][2] for lr in sp_rows])
            s1 = max([specs["rows"][lr][2] + specs["rows"][lr][1] * 64 for lr in sp_rows])
            KnT = st.enter_context(lb.tmp("KnT", [128, 4, nrw * 64], BF16))
            Vn = st.enter_context(lb.tmp("Vn", [64, nrw, 512], BF16))
            KcT = st.enter_context(lb.tmp("KcT", [128, 4, CTX], BF16))
            Vc = st.enter_context(lb.tmp("Vc", [64, 4, 512], BF16))
            Qn = st.enter_context(lb.tmp("Qn", [128, 4, GN], BF16))
            On = st.enter_context(lb.tmp("On", [128, 4, GN], BF16))
            tabm = st.enter_context(lb.tmp("tabm", [64, 4, 512], F32))
            tabs = st.enter_context(lb.tmp("tabs", [64, 4, s1 - s0], F32))
            sbb = [st.enter_context(lb.tmp("sbb%d" % i, [64, 768], F32)) for i in range(2)]
            Pn = [st.enter_context(lb.tmp("Pn%d" % i, [64, 1024], BF16)) for i in range(2)]
            rsn = [st.enter_context(lb.tmp("rsn%d" % i, [128, 64], F32)) for i in range(2)]
            rk = R(tag, "nak")
            kb.dma("sp", KnT[:], knw[:, :, (kmin + 4) * 64:(kmax + 4) * 64].rearrange("h p t -> p h t"), writes=[rk])
            kb.dma("sp", Vn[:], vnw[(kmin + 4) * 64:(kmax + 4) * 64, :].rearrange("(r c) f -> c r f", c=64), writes=[rk])
            kb.dma("sp", KcT[:], cnkT.rearrange("h p t -> p h t"), writes=[rk])
            kb.dma("sp", Vc[:], cvn.rearrange("(r c) f -> c r f", c=64), writes=[rk])
            kb.dma("sp", tabm[:], natab[:, :, 0:512], writes=[rk])
            kb.dma("sp", tabs[:], natab[:, :, s0:s1], writes=[rk])
            for (lo, tt, n) in t3:
                kb.dma("sp", Qn[:, :, lo:lo + n], nqT[:, :, tt:tt + n].rearrange("h p t -> p h t"), writes=[rk])
            qblocks = []
            for sg in segs:
                for q0 in range(0, sg.n, 64):
                    if sg.is_ctx:
                        qblocks.append((sg.lo + q0, 0, 0, None))
                    else:
                        lr = (sg.tt + q0) // 64
                        k0, nk, off = specs["rows"][lr]
                        qblocks.append((sg.lo + q0, k0, nk, off))
            it = 0
            ron = R(tag, "On")
            for (qlo, k0, nk, off) in qblocks:
                for h in range(4):
                    i = it % 2
                    it += 1
                    nb_ = nk + 4
                    SPv = lb.ps_all[0:64, i * 1024:i * 1024 + nb_ * 64]
                    rsp = [RPS[2 * i], RPS[2 * i + 1]]
                    fns = []
                    for a in range(nk):
                        kk = (k0 + a - kmin) * 64
                        fns.append(MM(SPv[:, a * 64:(a + 1) * 64], KnT[:, h, kk:kk + 64], Qn[:, h, qlo:qlo + 64], True, True))
                    for a in range(4):
                        fns.append(MM(SPv[:, (nk + a) * 64:(nk + a + 1) * 64], KcT[:, h, a * 64:(a + 1) * 64], Qn[:, h, qlo:qlo + 64], True, True))
                    kb.pe_group(fns, reads=[rk], writes=rsp)
                    rP, rsb = R(tag, "Pn", i), R(tag, "sbb", i)
                    if nk > 0:
                        tab = tabm[:, h, 0:512] if off == 0 else tabs[:, h, off - s0:off - s0 + nk * 64]
                        kb.op("dve", lambda e_: e_.scalar_tensor_tensor(sbb[i][:, 0:nk * 64], SPv[:, 0:nk * 64], SCALE, tab,
                                                                         ALU.mult, ALU.add), reads=rsp + [rk], writes=[rsb])
                        kb.op("act", lambda e_: e_.activation(Pn[i][:, 0:nk * 64], sbb[i][:, 0:nk * 64], AF.Exp),
                              reads=[rsb], writes=[rP])
                    kb.op("act", lambda e_: e_.activation(Pn[i][:, nk * 64:nb_ * 64], SPv[:, nk * 64:nb_ * 64], AF.Exp, scale=SCALE),
                          reads=rsp, writes=[rP])
                    ba, bs = 4 + i, 6 + i
                    fns = []
                    for a in range(nk):
                        fns.append(MM(PS[ba][:, 0:64], Vn[:, k0 + a - kmin, h * 128:(h + 1) * 128], Pn[i][:, a * 64:(a + 1) * 64], a == 0, False))
                    for a in range(4):
                        fns.append(MM(PS[ba][:, 0:64], Vc[:, a, h * 128:(h + 1) * 128], Pn[i][:, (nk + a) * 64:(nk + a + 1) * 64],
                                      nk == 0 and a == 0, a == 3))
                    kb.pe_group(fns, reads=[rP, rk], writes=[RPS[ba]])
                    fns = [MM(PS[bs][:, 0:64], lb.ones_bf[0:64, :], Pn[i][:, a * 64:(a + 1) * 64], a == 0, a == nb_ - 1) for a in range(nb_)]
                    kb.pe_group(fns, reads=[rP, rc], writes=[RPS[bs]])
                    rr = R(tag, "rsn", i)
                    kb.op("dve", lambda e_: e_.reciprocal(rsn[i][:], PS[bs][:, 0:64]), reads=[RPS[bs]], writes=[rr])
                    kb.op("dve", lambda e_: e_.tensor_tensor(On[:, h, qlo:qlo + 64], PS[ba][:, 0:64], rsn[i][:], ALU.mult),
                          reads=[RPS[ba], rr], writes=[ron])
            for (lo, tt, n) in t3:
                kb.dma("sp", ysT[8:12, :, tt:tt + n].rearrange("h p t -> p h t"), On[:, :, lo:lo + n], reads=[ron],
                       writes=[R(tag, "ys", "na", tt)])
            kb.barrier()

        for g2 in range(2):
            with contextlib.ExitStack() as st:
                KT = st.enter_context(lb.tmp("KT", [128, NKEY], BF16))
                Vg = st.enter_context(lb.tmp("Vg", [128, NCH, 128], BF16))
                Qg = st.enter_context(lb.tmp("Qg", [128, 4, GN], BF16))
                Og = st.enter_context(lb.tmp("Og", [128, 4, GN], BF16))
                Pb = [st.enter_context(lb.tmp("Pb%d" % i, [128, 512], BF16)) for i in range(3)]
                rsg = [st.enter_context(lb.tmp("rsg%d" % i, [128, 512], F32)) for i in range(2)]
                rk = R(tag, "gk", g2)
                for (o, s_) in split_cols(NKEY, 4096):
                    kb.dma("sp", KT[:, o:o + s_], kTall[g2, :, o:o + s_], writes=[rk])
                for c0 in range(0, NCH, 32):
                    c1 = min(NCH, c0 + 32)
                    kb.dma("sp", Vg[:, c0:c1, :], vgall[c0 * 128:c1 * 128, g2 * 128:(g2 + 1) * 128].rearrange("(c p) d -> p c d", p=128),
                           writes=[rk])
                for (lo, tt, n) in t3:
                    kb.dma("sp", Qg[:, :, lo:lo + n], qT[4 * g2:4 * g2 + 4, :, tt:tt + n].rearrange("h p t -> p h t"), writes=[rk])
                rog = R(tag, "Og", g2)
                it = 0
                sidx = 0
                for j in range(4):
                    for (lo, tt, n) in t3:
                        chunks = list(range(S // 128, NCH)) if isctx[(lo, tt)] else list(range(NCH))
                        ba, bs = 4 + it % 2, 6 + it % 2
                        it += 1

                        def emit_s(c, si):
                            bq = si % 2
                            kb.pe_group([MM(PS[bq][:, 0:n], KT[:, c * 128:(c + 1) * 128], Qg[:, j, lo:lo + n], True, True)],
                                        reads=[rk], writes=[RPS[bq]])

                        emit_s(chunks[0], sidx)
                        for ci, c in enumerate(chunks):
                            if ci + 1 < len(chunks):
                                emit_s(chunks[ci + 1], sidx + ci + 1)
                            bq = (sidx + ci) % 2
                            pi = (sidx + ci) % 3
                            rp = R(tag, "Pb", pi)
                            kb.op("act", lambda e_: e_.activation(Pb[pi][:, 0:n], PS[bq][:, 0:n], AF.Exp, scale=SCALE),
                                  reads=[RPS[bq]], writes=[rp])
                            first, last = ci == 0, ci == len(chunks) - 1
                            kb.pe_group([MM(PS[ba][:, 0:n], Vg[:, c, :], Pb[pi][:, 0:n], first, last),
                                         MM(PS[bs][:, 0:n], lb.ones_bf[:], Pb[pi][:, 0:n], first, last)],
                                        reads=[rp, rk, rc], writes=[RPS[ba], RPS[bs]])
                        sidx += len(chunks)
                        ri = it % 2
                        rr = R(tag, "rsg", ri)
                        kb.op("dve", lambda e_: e_.reciprocal(rsg[ri][:, 0:n], PS[bs][:, 0:n]), reads=[RPS[bs]], writes=[rr])
                        kb.op("dve", lambda e_: e_.tensor_tensor(Og[:, j, lo:lo + n], PS[ba][:, 0:n], rsg[ri][:, 0:n], ALU.mult),
                              reads=[RPS[ba], rr], writes=[rog])
                for (lo, tt, n) in t3:
                    kb.dma("sp", ysT[12 + 4 * g2:16 + 4 * g2, :, tt:tt + n].rearrange("h p t -> p h t"), Og[:, :, lo:lo + n],
                           reads=[rog], writes=[R(tag, "ys", "gqa", g2, tt)])
                kb.barrier()

        with contextlib.ExitStack() as st:
            ys = st.enter_context(lb.tmp("ys", [128, 20, GN], BF16))
            wm = [st.enter_context(lb.tmp("wm%d" % i, [128, 20, 512], BF16)) for i in range(2)]
            G = [st.enter_context(lb.tmp("G%d" % i, [128, 4, 512], BF16)) for i in range(2)]
            mm_ = [st.enter_context(lb.tmp("mm%d" % i, [128, 512], F32)) for i in range(4)]
            ss_ = [st.enter_context(lb.tmp("ss%d" % i, [128, 512], F32)) for i in range(2)]
            mo = [st.enter_context(lb.tmp("mo%d" % i, [128, 512], BF16)) for i in range(2)]
            load_res(ys, ysT, 20, "ys")
            gT4 = gT.rearrange("(b c) p t -> b c p t", b=4)
            kcs = [(0, 4), (4, 8), (8, 12), (12, 20)]
            it = 0
            def issue_wm(mb_):
                for b_, (ka, kb_) in enumerate(kcs):
                    kb.dma("sp", wm[mb_ % 2][:, ka:kb_, :], w_outs[b_][:, mb_ * 512:(mb_ + 1) * 512].rearrange("(k p) m -> p k m", p=128),
                           reads=r_wouts[b_], writes=[R(tag, "wm", mb_ % 2)])

            issue_wm(0)
            for mb in range(4):
                wi = mb % 2
                rw = R(tag, "wm", wi)
                if mb + 1 < 4:
                    issue_wm(mb + 1)
                for (lo, tt, n) in t3:
                    for j in range(4):
                        c = mb * 4 + j
                        i = it % 2
                        it += 1
                        rG = R(tag, "G", i)
                        kb.dma("sp", G[i][:, :, 0:n], gT4[:, c, :, tt:tt + n].rearrange("b p t -> p b t"), writes=[rG])
                        rm = [R(tag, "mm", b_) for b_ in range(4)]
                        for b_, (ka, kb_) in enumerate(kcs):
                            bank = b_ + 4 * i
                            kb.pe_group([MM(PS[bank][:, 0:n], wm[wi][:, k, j * 128:(j + 1) * 128], ys[:, k, lo:lo + n], k == ka, k == kb_ - 1)
                                         for k in range(ka, kb_)], reads=[rw, R(tag, "ys", lo)], writes=[RPS[bank]])
                            kb.op("dve", lambda e_: e_.tensor_tensor(mm_[b_][:, 0:n], PS[bank][:, 0:n], G[i][:, b_, 0:n], ALU.mult),
                                  reads=[RPS[bank], rG], writes=[rm[b_]])
                        rs0, rs1, rmo = R(tag, "ss", 0), R(tag, "ss", 1), R(tag, "mo", i)
                        kb.op("pool", lambda e_: e_.tensor_tensor(ss_[0][:, 0:n], mm_[0][:, 0:n], mm_[1][:, 0:n], ALU.add),
                              reads=[rm[0], rm[1]], writes=[rs0])
                        kb.op("pool", lambda e_: e_.tensor_tensor(ss_[1][:, 0:n], mm_[2][:, 0:n], mm_[3][:, 0:n], ALU.add),
                              reads=[rm[2], rm[3]], writes=[rs1])
                        kb.op("pool", lambda e_: e_.tensor_tensor(mo[i][:, 0:n], ss_[0][:, 0:n], ss_[1][:, 0:n], ALU.add),
                              reads=[rs0, rs1], writes=[rmo])
                        kb.dma("sp", mgT[c, :, tt:tt + n], mo[i][:, 0:n], reads=[rmo], writes=[R(tag, "mg", c, tt)])
            kb.barrier()

        with contextlib.ExitStack() as st:
            mg = st.enter_context(lb.tmp("mg", [128, KC, GN], BF16))
            hbt = [st.enter_context(lb.tmp("hbt%d" % i, [128, 512], F32)) for i in range(4)]
            hot = [st.enter_context(lb.tmp("hot%d" % i, [128, 512], F32)) for i in range(4)]
            load_res(mg, mgT, KC, "mg2")
            cnt = {"e": 0}

            def epi_o(c, lo, tt, n, b):
                e = cnt["e"] % 4
                cnt["e"] += 1
                v = 1 if isctx[(lo, tt)] else 0
                rh, ro = R(tag, "hbt", e), R(tag, "hot", e)
                kb.dma("sp", hbt[e][:, 0:n], hT[c, :, tt:tt + n], writes=[rh])
                kb.op("dve", lambda e_: e_.scalar_tensor_tensor(hot[e][:, 0:n], PS[b][:, 0:n], modsb[:, 32 + c, v:v + 1],
                                                                 hbt[e][:, 0:n], ALU.mult, ALU.add), reads=[RPS[b], rh, rc], writes=[ro])
                kb.dma("sp", hmT[c, :, tt:tt + n], hot[e][:, 0:n], reads=[ro], writes=[R(tag, "hm", c, tt)])

            sweep(lb, st, mg, lambda lo: R(tag, "mg2", lo), KC, w_o, r_wo, [(512 * i, [0, 1, 2, 3]) for i in range(4)], 512, t3, epi_o, "wO")
            kb.barrier()

        with contextlib.ExitStack() as st:
            xn2 = st.enter_context(lb.tmp("xn2", [128, KC, GN], BF16))
            norm_mod(lb, st, hmT, segs, xn2, A2, modsb, 48, xn2T, tag + "n2")
            kb.barrier()
    return lb.finish()


def na_tables(cfg, rpb_l):
    specs = na_row_specs(cfg)
    ROWS = cfg.ROWS
    col = np.arange(GRID_W)
    c0 = np.clip(col - 8, 0, GRID_W - 16)
    outs = []
    for c in range(NCORE):
        base = c * cfg.TLR
        tabs = []
        for (slot, lr, k0, nk) in specs["slots"]:
            tab = np.full((4, nk, 64, 64), NEG, np.float32)
            lrs = specs["mid_lr"] if lr is None else lr
            kk0 = lrs - 4 if lr is None else k0
            r = base + lrs
            r0 = min(max(r - 4, 0), ROWS - 8)
            for aa in range(nk):
                kr = base + kk0 + aa
                if kr < r0 or kr >= r0 + 8 or kr < 0 or kr >= ROWS:
                    continue
                dr = kr - r + 7
                for qc in range(64):
                    kcs = np.arange(c0[qc], c0[qc] + 16)
                    tab[:, aa, kcs, qc] = rpb_l[:, dr, kcs - qc + 15]
            tabs.append(tab.transpose(2, 0, 1, 3).reshape(64, 4, nk * 64))
        outs.append(np.ascontiguousarray(np.concatenate(tabs, -1)))
    return outs


def run_C(inp, cfg, l, hT, mods, hc, rA, fyT):
    f32 = np.float32
    TL, TT, S, NW = cfg.TL, cfg.TT, cfg.S, cfg.NW
    zc_lat = np.concatenate([np.asarray(rA[c]["zc"])[:, :, :TL] for c in range(NCORE)], 2)
    zpad = np.pad(zc_lat[[0, 1, 2, 3, 8, 9, 10, 11]], ((0, 0), (0, 0), (1, 1)))
    kT_lat = np.concatenate([np.asarray(rA[c]["kT"])[:, :, :TL] for c in range(NCORE)], 2)
    kTall = np.ascontiguousarray(np.concatenate([kT_lat, np.asarray(rA[0]["kT"])[:, :, TL:]], 2))
    vgall = np.ascontiguousarray(np.concatenate([np.asarray(rA[c]["vg"])[:TL] for c in range(NCORE)] + [np.asarray(rA[0]["vg"])[TL:]], 0))
    kn_lat = np.concatenate([np.asarray(rA[c]["knT"])[:, :, :TL] for c in range(NCORE)], 2)
    kn_pad = np.pad(kn_lat, ((0, 0), (0, 0), (256, 192)))
    vn_lat = np.concatenate([np.asarray(rA[c]["vn"])[:TL] for c in range(NCORE)], 0)
    vn_pad = np.pad(vn_lat, ((256, 192), (0, 0)))
    cnkT = np.ascontiguousarray(np.asarray(rA[0]["knT"])[:, :, TL:])
    cvn = np.ascontiguousarray(np.asarray(rA[0]["vn"])[TL:])
    tabs = na_tables(cfg, np.asarray(inp["na_rpb"], f32)[l])
    conv_w = np.ascontiguousarray(np.asarray(inp["conv_w"], f32)[l].reshape(3, 4, 128).transpose(2, 0, 1))
    wts = {k: np.ascontiguousarray(np.asarray(inp[k], f32)[l]) for k in
           ("w_gate", "w_conv_out", "w_fourier_out", "w_na_out", "w_gqa_out", "w_o")}
    n2 = pl(inp["norm2"][l], KC)
    bg = pl(inp["b_gate"][l], 64)
    maps = []
    for c in range(NCORE):
        m = {"hT": hT[c], "xnT": np.asarray(rA[c]["xnT"]), "zc": np.asarray(rA[c]["zc"]),
             "zce": np.ascontiguousarray(zpad[:, :, c * TL:c * TL + TL + 2]), "zfc": np.asarray(rA[c]["zfc"]),
             "qT": np.asarray(rA[c]["qT"]), "nqT": np.asarray(rA[c]["nqT"]), "kTall": kTall, "vgall": vgall,
             "knw": np.ascontiguousarray(kn_pad[:, :, c * TL:c * TL + NW]),
             "vnw": np.ascontiguousarray(vn_pad[c * TL:c * TL + NW]), "cnkT": cnkT, "cvn": cvn, "natab": tabs[c],
             "fyT": fyT[c], "mods": mods[l], "norm2": n2, "b_gate": bg, "conv_w": conv_w, "f_cs": hc["f_cs"],
             "f_ctx": hc["f_ctx"]}
        m.update(wts)
        maps.append(m)
    return launch(get_prog("C", cfg, build_C), maps)


def build_D(cfg):
    lb = LB()
    nc, kb, R, PS, RPS = lb.nc, lb.kb, lb.R, lb.PS, lb.RPS
    TL, TT, GL, GN = cfg.TL, cfg.TT, cfg.GL, cfg.GN
    NX0, NX1 = GL + 2 + CTX + 2, GL + 2
    hmT = lb.din("hmT", [KC, 128, TT])
    xg = [lb.din("xg0", [KC, 128, NX0], BF16), lb.din("xg1", [KC, 128, NX1], BF16)]
    fcw_in = lb.din("ffn_cw", [128, 3, 88])
    mods_in = lb.din("mods", [128, 96, 2])
    w_up = lb.din("w_up", [D, 11264])
    w_down = lb.din("w_down", [5632, D])
    hTo = lb.dout("hTo", [KC, 128, TT], F32)
    actT = lb.dscr("actT", [44, 128, TT], BF16)
    w_up, r_wu = conv_weight(lb, "w_up", w_up, D, 11264)
    w_down, r_wd = conv_weight(lb, "w_down", w_down, 5632, D)
    modsb = lb.const("modsb", mods_in, [128, 96, 2])
    fcw = lb.const("fcw", fcw_in, [128, 3, 88])
    rc = lb.rc
    for gi, segs in enumerate(groups(cfg)):
        tag = "g%d" % gi
        NX = NX0 if gi == 0 else NX1
        segD = []
        off = 0
        for sg in segs:
            segD.append((off, sg.n, sg.tt, sg.is_ctx))
            off += sg.n + 2
        ctiles = []
        for (o, n, tt, c_) in segD:
            for (a, s_) in split_cols(n + 2):
                ctiles.append((o + a, s_))
        with contextlib.ExitStack() as st:
            xin = st.enter_context(lb.tmp("xin", [128, KC, NX], BF16))
            ua = [st.enter_context(lb.tmp("ua%d" % i, [128, NX], F32)) for i in range(2)]
            ug = [st.enter_context(lb.tmp("ug%d" % i, [128, NX], F32)) for i in range(2)]
            ca = [st.enter_context(lb.tmp("ca%d" % i, [128, GL], F32)) for i in range(2)]
            cg = [st.enter_context(lb.tmp("cg%d" % i, [128, GL], F32)) for i in range(2)]
            sil = st.enter_context(lb.tmp("sil", [128, GL], F32))
            ptmp = st.enter_context(lb.tmp("ptmp", [128, GL], F32))
            ab = [st.enter_context(lb.tmp("ab%d" % i, [128, GL], BF16)) for i in range(2)]
            wblk = [st.enter_context(lb.tmp("wu%d" % i, [128, KC, 2, 512], BF16)) for i in range(2)]
            rx = R(tag, "xin")
            for (o, s_) in split_cols(NX, 512):
                kb.dma("sp", xin[:, :, o:o + s_], xg[gi][:, :, o:o + s_].rearrange("k p t -> p k t"), writes=[rx])
            ei = 0
            si = 0
            def issue_wu(pb_):
                for part in range(2):
                    kb.dma("sp", wblk[pb_ % 2][:, :, part, :],
                           w_up[:, part * 5632 + pb_ * 512:part * 5632 + (pb_ + 1) * 512].rearrange("(k p) m -> p k m", p=128),
                           reads=r_wu, writes=[R(tag, "wu", pb_ % 2)])

            issue_wu(0)
            for pb in range(11):
                wi = pb % 2
                rw = R(tag, "wu", wi)
                if pb + 1 < 11:
                    issue_wu(pb + 1)
                for jj in range(4):
                    j = 4 * pb + jj
                    ui = j % 2
                    rua, rug = R(tag, "ua", ui), R(tag, "ug", ui)
                    for (co, s_) in ctiles:
                        for part, (dstb, rd) in enumerate(((ua[ui], rua), (ug[ui], rug))):
                            b = lb.nb(0, 8)
                            kb.pe_group([MM(PS[b][:, 0:s_], wblk[wi][:, k, part, jj * 128:(jj + 1) * 128], xin[:, k, co:co + s_],
                                            k == 0, k == KC - 1) for k in range(KC)], reads=[rw, rx], writes=[RPS[b]])
                            evac(lb, b, s_, dstb[:, co:co + s_], rd, ei)
                            ei += 1
                    for (o, n, tt, c_) in segD:
                        i = si % 2
                        si += 1
                        rca, rcg, rsl, rab = R(tag, "ca", i), R(tag, "cg", i), R(tag, "sil"), R(tag, "ab", i)
                        src, dst, rs_, rd, ch = ua[ui], ca[i], rua, rca, j
                        kb.op("dve", lambda e_: e_.tensor_scalar(dst[:, 0:n], src[:, o:o + n], fcw[:, 0, ch:ch + 1], None, ALU.mult),
                              reads=[rs_, rc], writes=[rd])
                        kb.op("dve", lambda e_: e_.scalar_tensor_tensor(dst[:, 0:n], src[:, o + 1:o + 1 + n], fcw[:, 1, ch:ch + 1],
                                                                         dst[:, 0:n], ALU.mult, ALU.add), reads=[rs_, rc, rd], writes=[rd])
                        kb.op("dve", lambda e_: e_.scalar_tensor_tensor(dst[:, 0:n], src[:, o + 2:o + 2 + n], fcw[:, 2, ch:ch + 1],
                                                                         dst[:, 0:n], ALU.mult, ALU.add), reads=[rs_, rc, rd], writes=[rd])
                        src, dst, rs_, rd, ch = ug[ui], cg[i], rug, rcg, 44 + j
                        rtp = R(tag, "ptmp")
                        kb.op("pool", lambda e_: e_.tensor_scalar(dst[:, 0:n], src[:, o:o + n], fcw[:, 0, ch:ch + 1], None, ALU.mult),
                              reads=[rs_, rc], writes=[rd])
                        for tap in (1, 2):
                            kb.op("pool", lambda e_: e_.tensor_scalar(ptmp[:, 0:n], src[:, o + tap:o + tap + n], fcw[:, tap, ch:ch + 1], None, ALU.mult),
                                  reads=[rs_, rc], writes=[rtp])
                            kb.op("pool", lambda e_: e_.tensor_tensor(dst[:, 0:n], dst[:, 0:n], ptmp[:, 0:n], ALU.add),
                                  reads=[rtp, rd], writes=[rd])
                        kb.op("act", lambda e_: e_.activation(sil[:, 0:n], ca[i][:, 0:n], AF.Silu), reads=[rca], writes=[rsl])
                        kb.op("pool", lambda e_: e_.tensor_tensor(ab[i][:, 0:n], sil[:, 0:n], cg[i][:, 0:n], ALU.mult),
                              reads=[rsl, rcg], writes=[rab])
                        kb.dma("sp", actT[j, :, tt:tt + n], ab[i][:, 0:n], reads=[rab], writes=[R(tag, "act", j, tt)])
            kb.barrier()
        tiles = []
        for sg in segs:
            for t in sg.tiles():
                tiles.append(t + (sg.is_ctx,))
        t3 = [t[:3] for t in tiles]
        isctx = {(lo, tt): c for (lo, tt, n, c) in tiles}
        with contextlib.ExitStack() as st:
            act = st.enter_context(lb.tmp("act", [128, 44, GN], BF16))
            hbt = [st.enter_context(lb.tmp("hbt%d" % i, [128, 512], F32)) for i in range(4)]
            hot = [st.enter_context(lb.tmp("hot%d" % i, [128, 512], F32)) for i in range(4)]
            for (lo, tt, n) in t3:
                for k0 in range(0, 44, 11):
                    kb.dma("sp", act[:, k0:k0 + 11, lo:lo + n], actT[k0:k0 + 11, :, tt:tt + n].rearrange("k p t -> p k t"),
                           writes=[R(tag, "actr", lo)])
            cnt = {"e": 0}

            def epi_d(c, lo, tt, n, b):
                e = cnt["e"] % 4
                cnt["e"] += 1
                v = 1 if isctx[(lo, tt)] else 0
                rh, ro = R(tag, "hbt", e), R(tag, "hot", e)
                kb.dma("sp", hbt[e][:, 0:n], hmT[c, :, tt:tt + n], writes=[rh])
                kb.op("dve", lambda e_: e_.scalar_tensor_tensor(hot[e][:, 0:n], PS[b][:, 0:n], modsb[:, 80 + c, v:v + 1],
                                                                 hbt[e][:, 0:n], ALU.mult, ALU.add), reads=[RPS[b], rh, rc], writes=[ro])
                kb.dma("sp", hTo[c, :, tt:tt + n], hot[e][:, 0:n], reads=[ro], writes=[R(tag, "ho", c, tt)])

            sweep(lb, st, act, lambda lo: R(tag, "actr", lo), 44, w_down, r_wd, [(256 * i, [0, 1]) for i in range(8)], 256, t3, epi_d, "wD")
            kb.barrier()
    return lb.finish()


def run_D(inp, cfg, l, mods, rC):
    f32 = np.float32
    TL, GL = cfg.TL, cfg.GL
    x2_lat = np.concatenate([np.asarray(rC[c]["xn2T"])[:, :, :TL] for c in range(NCORE)], 2)
    x2p = np.pad(x2_lat, ((0, 0), (0, 0), (1, 1)))
    fcw = np.ascontiguousarray(np.asarray(inp["ffn_conv_w"], f32)[l].reshape(3, 88, 128).transpose(2, 0, 1))
    w_up = np.ascontiguousarray(np.asarray(inp["w_up"], f32)[l])
    w_down = np.ascontiguousarray(np.asarray(inp["w_down"], f32)[l])
    maps = []
    for c in range(NCORE):
        cx = np.pad(np.asarray(rC[c]["xn2T"])[:, :, TL:], ((0, 0), (0, 0), (1, 1)))
        xg0 = np.ascontiguousarray(np.concatenate([x2p[:, :, c * TL:c * TL + GL + 2], cx], 2))
        xg1 = np.ascontiguousarray(x2p[:, :, c * TL + GL:c * TL + 2 * GL + 2])
        maps.append({"hmT": np.asarray(rC[c]["hmT"]), "xg0": xg0, "xg1": xg1, "ffn_cw": fcw, "mods": mods[l],
                     "w_up": w_up, "w_down": w_down})
    return launch(get_prog("D", cfg, build_D), maps)


def run_model(inp, cfg, verbose=False):
    import time
    hc = host_consts(cfg)
    TL = cfg.TL
    t0 = time.time()
    mods = run_M(inp, cfg)
    x = np.asarray(inp["x"], np.float32)[0]
    ctx = np.asarray(inp["ctx"], np.float32)[0]
    ctx_fm = to_fm(ctx)
    hT = [np.ascontiguousarray(np.concatenate([to_fm(x[c * TL:(c + 1) * TL]), ctx_fm], 2)) for c in range(NCORE)]
    for l in range(cfg.L):
        rA = run_A(inp, cfg, l, hT, mods, hc)
        if verbose:
            print("layer", l, "A done", time.time() - t0, flush=True)
        fyT = run_B(cfg, rA, hc)
        rC = run_C(inp, cfg, l, hT, mods, hc, rA, fyT)
        if verbose:
            print("layer", l, "C done", time.time() - t0, flush=True)
        del rA
        rD = run_D(inp, cfg, l, mods, rC)
        del rC
        hT = [np.asarray(rD[c]["hTo"]) for c in range(NCORE)]
        if verbose:
            print("layer", l, "D done", time.time() - t0, flush=True)
    out = np.concatenate([from_fm(hT[c][:, :, :TL]) for c in range(NCORE)], 0)
    return np.ascontiguousarray(out[None].astype(np.float32))


def kernel(**inputs):
    cfg = Cfg(16384, 4)
    return run_model(inputs, cfg)
```

```python
import contextlib
import numpy as np
import math
import ml_dtypes
import concourse.bass as bass
import concourse.mybir as mybir
from concourse.bass_utils import run_bass_kernel_spmd

F32 = mybir.dt.float32
BF16 = mybir.dt.bfloat16
ALU = mybir.AluOpType
AF = mybir.ActivationFunctionType
AX = mybir.AxisListType


class Res:
    __slots__ = ("w", "r", "name")

    def __init__(self, name=""):
        self.w = None
        self.r = {}
        self.name = name


class KB:
    def __init__(self, nc, n_dma_sems=6, same_engine_sync=True):
        self.nc = nc
        self.es = contextlib.ExitStack()
        self.engs = {"pe": nc.tensor, "act": nc.scalar, "dve": nc.vector,
                     "pool": nc.gpsimd, "sp": nc.sync}
        self.semh = {}
        for k in self.engs:
            self.semh[k] = self.es.enter_context(nc.semaphore("s_" + k))
        self.cnt = {k: 0 for k in self.engs}
        self.seen = {k: {} for k in self.engs}
        self.same = same_engine_sync
        self.dq = {}
        for q in ("sp", "pool", "act"):
            sl = []
            for i in range(n_dma_sems):
                key = ("d", q, i)
                self.semh[key] = self.es.enter_context(nc.semaphore("d_%s%d" % (q, i)))
                sl.append(key)
            self.dq[q] = {"keys": sl, "uses": [0] * n_dma_sems, "next": 0}
        self.n_ins = 0

    def close(self):
        self.es.close()

    def sb(self, name, shape, dt):
        return self.es.enter_context(self.nc.sbuf_tensor("sb_" + name, list(shape), dt))

    def ps(self, name, shape, dt=F32):
        return self.es.enter_context(self.nc.psum_tensor("pp_" + name, list(shape), dt))

    def _wait(self, eng, deps):
        e = self.engs[eng]
        seen = self.seen[eng]
        for sk, v in deps:
            if sk == eng and (not self.same or eng == "pe"):
                continue
            if seen.get(sk, 0) >= v:
                continue
            e.wait_ge(self.semh[sk], v)
            seen[sk] = v

    def _deps(self, reads, writes):
        deps = {}
        for r in reads:
            if r.w is not None:
                sk, v = r.w
                if deps.get(sk, 0) < v:
                    deps[sk] = v
        for w in writes:
            if w.w is not None:
                sk, v = w.w
                if deps.get(sk, 0) < v:
                    deps[sk] = v
            for sk, v in w.r.items():
                if deps.get(sk, 0) < v:
                    deps[sk] = v
        return deps.items()

    def _mark(self, ev, reads, writes):
        sk, v = ev
        for r in reads:
            if r.r.get(sk, 0) < v:
                r.r[sk] = v
        for w in writes:
            w.w = ev
            w.r = {}

    def op(self, eng, fn, reads=(), writes=()):
        self._wait(eng, self._deps(reads, writes))
        ins = fn(self.engs[eng])
        self.cnt[eng] += 1
        n = self.cnt[eng]
        ins.then_inc(self.semh[eng], 1)
        self._mark((eng, n), reads, writes)
        self.n_ins += 1
        return ins

    def pe_group(self, fns, reads=(), writes=()):
        self._wait("pe", self._deps(reads, writes))
        ins = None
        for fn in fns:
            ins = fn(self.nc.tensor)
        self.cnt["pe"] += 1
        n = self.cnt["pe"]
        ins.then_inc(self.semh["pe"], 1)
        self._mark(("pe", n), reads, writes)
        self.n_ins += len(fns)

    def dma(self, q, out, in_, reads=(), writes=(), **kw):
        d = self.dq[q]
        i = d["next"]
        d["next"] = (i + 1) % len(d["keys"])
        key = d["keys"][i]
        deps = dict(self._deps(reads, writes))
        if d["uses"][i] > 0:
            deps[key] = 16 * d["uses"][i]
        self._wait(q, deps.items())
        ins = self.engs[q].dma_start(out=out, in_=in_, **kw)
        d["uses"][i] += 1
        v = 16 * d["uses"][i]
        ins.then_inc(self.semh[key], 16)
        self._mark((key, v), reads, writes)
        self.n_ins += 1
        return (key, v)

    def wait_all(self, eng, ress):
        deps = {}
        for r in ress:
            if r.w is not None:
                sk, v = r.w
                if deps.get(sk, 0) < v:
                    deps[sk] = v
        self._wait(eng, deps.items())


def _kb_collective(self, ins_ap, outs_ap, reads=(), writes=()):
    if "cc" not in self.semh:
        self.semh["cc"] = self.es.enter_context(self.nc.semaphore("s_cc"))
        self.ncc = 0
    self._wait("pool", self._deps(reads, writes))
    ins = self.nc.gpsimd.collective_compute(
        "AllGather", ALU.bypass, replica_groups=[list(range(8))], ins=[ins_ap], outs=[outs_ap])
    self.ncc += 1
    ins.then_inc(self.semh["cc"], 1)
    self._mark(("cc", self.ncc), reads, writes)


def _kb_barrier(self, skip_queues=()):
    targets = []
    for k in self.engs:
        if self.cnt[k] > 0:
            targets.append((k, self.cnt[k]))
    for q, d in self.dq.items():
        if q in skip_queues:
            continue
        for key, u in zip(d["keys"], d["uses"]):
            if u > 0:
                targets.append((key, 16 * u))
    if "cc" in self.semh and self.ncc > 0:
        targets.append(("cc", self.ncc))
    for e in self.engs:
        self._wait(e, [(sk, v) for sk, v in targets])


KB.collective = _kb_collective
KB.barrier = _kb_barrier


BF = ml_dtypes.bfloat16
NCORE = 8
D = 2048
KC = 16
CTX = 256
GRID_W = 64
EPS = 1e-6
SCALE = 128 ** -0.5
NEG = -30000.0


def split_cols(n, mx=512):
    k = (n + mx - 1) // mx
    base, rem = n // k, n % k
    out, o = [], 0
    for i in range(k):
        s = base + (1 if i < rem else 0)
        out.append((o, s))
        o += s
    return out


class Cfg:
    def __init__(self, S, DEPTH):
        self.S, self.L = S, DEPTH
        self.TL = S // NCORE
        self.TT = self.TL + CTX
        self.GL = self.TL // 2
        self.GN = self.GL + CTX
        self.TLR = self.TL // GRID_W
        self.GR = self.GL // GRID_W
        self.ROWS = S // GRID_W
        self.T1 = S // 128
        self.NKEY = S + CTX
        self.NCH = self.NKEY // 128
        self.NW = (self.TLR + 7) * 64


class Seg:
    def __init__(self, lo, tt, n, is_ctx):
        self.lo, self.tt, self.n, self.is_ctx = lo, tt, n, is_ctx

    def tiles(self):
        return [(self.lo + o, self.tt + o, s) for (o, s) in split_cols(self.n)]


def groups(cfg):
    return [[Seg(0, 0, cfg.GL, False), Seg(cfg.GL, cfg.TL, CTX, True)], [Seg(0, cfg.GL, cfg.GL, False)]]


def na_row_specs(cfg):
    TLR = cfg.TLR
    slots = [(0, None, None, 8)]
    rows = {}
    off = 8 * 64
    for lr in range(TLR):
        if lr < 4:
            k0, nk = lr - 4, 12 - lr
        elif lr >= TLR - 3:
            k0, nk = TLR - 8, lr + 4 - (TLR - 8)
        else:
            rows[lr] = (lr - 4, 8, 0)
            continue
        slots.append((len(slots), lr, k0, nk))
        rows[lr] = (k0, nk, off)
        off += nk * 64
    return {"slots": slots, "rows": rows, "ncols": off, "mid_lr": 4}


def MM(out, lhsT, rhs, start, stop):
    return lambda pe: pe.matmul(out, lhsT, rhs, start=start, stop=stop)


class LB:
    def __init__(self):
        self.nc = bass.Bass("TRN2", target_bir_lowering=False)
        self.kb = KB(self.nc)
        self.res = {}
        self.uid = 0
        self.outs = []
        self.bank = 0
        kb = self.kb
        self.ones_bf = kb.sb("ones_bf", [128, 128], BF16)
        self.ps_all = kb.ps("ps_all", [128, 4096], F32)
        self.PS = [self.ps_all[:, i * 512:(i + 1) * 512] for i in range(8)]
        self.RPS = [self.R("ps", i) for i in range(8)]
        self.rc = self.R("consts")
        kb.op("dve", lambda e: e.memset(self.ones_bf[:], 1.0), writes=[self.rc])
        self.epsb = kb.sb("epsb", [128, 1], F32)
        kb.op("dve", lambda e: e.memset(self.epsb[:], EPS), writes=[self.rc])

    def R(self, *key):
        r = self.res.get(key)
        if r is None:
            r = Res(str(key))
            self.res[key] = r
        return r

    def din(self, name, shape, dt=F32):
        return self.nc.dram_tensor(name, list(shape), dt, kind="ExternalInput").ap()

    def dout(self, name, shape, dt=F32):
        self.outs.append(name)
        return self.nc.dram_tensor(name, list(shape), dt, kind="ExternalOutput").ap()

    def dscr(self, name, shape, dt):
        return self.nc.dram_tensor(name, list(shape), dt, kind="Internal").ap()

    def tmp(self, name, shape, dt):
        self.uid += 1
        return self.nc.sbuf_tensor("t%d_%s" % (self.uid, name), list(shape), dt)

    def const(self, name, src, shape, dt=F32):
        t = self.kb.sb(name, shape, dt)
        self.kb.dma("sp", t[:], src, writes=[self.rc])
        return t

    def nb(self, lo=0, n=4):
        b = lo + self.bank % n
        self.bank += 1
        return b

    def finish(self):
        self.kb.barrier()
        self.kb.close()
        return self


def evac(lb, b, n, dst, rdst, idx, scale=None):
    kb = lb.kb
    if idx % 2 == 0:
        if scale is None:
            kb.op("act", lambda e: e.copy(dst, lb.PS[b][:, 0:n]), reads=[lb.RPS[b]], writes=[rdst])
        else:
            kb.op("act", lambda e: e.mul(dst, lb.PS[b][:, 0:n], scale), reads=[lb.RPS[b]], writes=[rdst])
    else:
        if scale is None:
            kb.op("dve", lambda e: e.tensor_copy(dst, lb.PS[b][:, 0:n]), reads=[lb.RPS[b]], writes=[rdst])
        else:
            kb.op("dve", lambda e: e.tensor_scalar_mul(dst, lb.PS[b][:, 0:n], scale), reads=[lb.RPS[b]], writes=[rdst])


class WConv:
    def __init__(self, lb, name, W_in, K, M, bw, col0s=None):
        self.lb, self.name, self.W_in, self.K, self.M, self.bw = lb, name, W_in, K, M, bw
        self.col0s = col0s if col0s is not None else list(range(0, M, bw))
        self.Wb = lb.dscr("wb_" + name, [len(self.col0s), K, bw], BF16)
        self.done = set()

    def res(self, bi):
        return self.lb.R("wcv", self.name, bi)

    def issue(self, bi):
        if bi in self.done or bi >= len(self.col0s):
            return
        self.done.add(bi)
        c0 = self.col0s[bi]
        self.lb.kb.dma("pool", self.Wb[bi], self.W_in[:, c0:c0 + self.bw], writes=[self.res(bi)])

    def issue_all(self):
        for bi in range(len(self.col0s)):
            self.issue(bi)

    def blk(self, bi):
        return self.Wb[bi]


LEAD = 3


def sweep(lb, st, xin, xres_fn, kcin, wc, blocks, tiles, epi, wname):
    kb = lb.kb
    bw = wc.bw
    wb = [st.enter_context(lb.tmp(wname + str(i), [128, kcin, bw], BF16)) for i in range(2)]

    def issue(bi):
        bb = blocks[bi][0]
        kb.dma("sp", wb[bi % 2][:], wc.blk(bb).rearrange("(k p) m -> p k m", p=128), reads=[wc.res(bb)],
               writes=[lb.R(wname, bi % 2)])

    for q in range(min(LEAD, len(blocks))):
        wc.issue(blocks[q][0])
    issue(0)
    for bi, (bb, js) in enumerate(blocks):
        i = bi % 2
        rw = lb.R(wname, i)
        if bi + LEAD < len(blocks):
            wc.issue(blocks[bi + LEAD][0])
        if bi + 1 < len(blocks):
            issue(bi + 1)
        for (lo, tt, n) in tiles:
            for j in js:
                b = lb.nb(0, 4)
                fns = [MM(lb.PS[b][:, 0:n], wb[i][:, k, j * 128:(j + 1) * 128], xin[:, k, lo:lo + n],
                          k == 0, k == kcin - 1) for k in range(kcin)]
                kb.pe_group(fns, reads=[rw, xres_fn(lo)], writes=[lb.RPS[b]])
                epi((wc.col0s[bb] + j * 128) // 128, lo, tt, n, b)


def norm_mod(lb, st, src, segs, xn, A, modsb, sh0, dstT, tag):
    kb, PS, RPS, R = lb.kb, lb.PS, lb.RPS, lb.R
    hb = [st.enter_context(lb.tmp("hb%d" % i, [128, KC, 512], F32)) for i in range(2)]
    sq = [st.enter_context(lb.tmp("sq%d" % i, [128, KC, 512], BF16)) for i in range(2)]
    rstd = [st.enter_context(lb.tmp("rstd%d" % i, [128, 512], F32)) for i in range(2)]
    tf = [st.enter_context(lb.tmp("tf%d" % i, [128, 512], F32)) for i in range(2)]
    ti = 0
    for sg in segs:
        v = 1 if sg.is_ctx else 0
        for (lo, tt, n) in sg.tiles():
            i = ti % 2
            ti += 1
            rh, rs, rr = R(tag, "hb", i), R(tag, "sq", i), R(tag, "rstd", i)
            kb.dma("sp", hb[i][:, :, 0:n], src[:, :, tt:tt + n].rearrange("k p t -> p k t"),
                   reads=[R(tag, "src", tt)], writes=[rh])
            kb.op("act", lambda e, i=i, n=n: e.activation(sq[i][:, :, 0:n], hb[i][:, :, 0:n], AF.Square),
                  reads=[rh], writes=[rs])
            b = 4 + i
            kb.pe_group([MM(PS[b][:, 0:n], lb.ones_bf[:], sq[i][:, k, 0:n], k == 0, k == KC - 1) for k in range(KC)],
                        reads=[rs, lb.rc], writes=[RPS[b]])
            kb.op("act", lambda e, i=i, n=n, b=b: e.activation(rstd[i][:, 0:n], PS[b][:, 0:n], AF.Sqrt, bias=lb.epsb[:, 0:1],
                                                               scale=1.0 / D), reads=[RPS[b], lb.rc], writes=[rr])
            kb.op("dve", lambda e, i=i, n=n: e.reciprocal(rstd[i][:, 0:n], rstd[i][:, 0:n]), reads=[rr], writes=[rr])
            rx = R(tag, "xn", lo)
            for k in range(KC):
                j = k % 2
                rt = R(tag, "tf", j)
                kb.op("dve", lambda e, i=i, n=n, k=k, j=j, v=v: e.scalar_tensor_tensor(
                    tf[j][:, 0:n], hb[i][:, k, 0:n], A[:, k, v:v + 1], rstd[i][:, 0:n], ALU.mult, ALU.mult),
                    reads=[rh, rr, lb.rc], writes=[rt])
                kb.op("act", lambda e, n=n, k=k, j=j, v=v, lo=lo: e.activation(
                    xn[:, k, lo:lo + n], tf[j][:, 0:n], AF.Identity, bias=modsb[:, sh0 + k, v:v + 1], scale=1.0),
                    reads=[rt, lb.rc], writes=[rx])
            if dstT is not None:
                kb.dma("sp", dstT[:, :, tt:tt + n].rearrange("k p t -> p k t"), xn[:, :, lo:lo + n],
                       reads=[rx], writes=[R(tag, "dst", tt)])


def load_mods(lb, mods_in, norm_in, gc_scale):
    kb = lb.kb
    modsb = lb.const("modsb", mods_in, [128, 96, 2])
    nrm = lb.const("nrm", norm_in, [128, KC])
    A = kb.sb("Amod", [128, KC, 2], F32)
    for v in range(2):
        kb.op("dve", lambda e, v=v: e.tensor_scalar(A[:, :, v], modsb[:, gc_scale:gc_scale + KC, v], 1.0, None, ALU.add),
              reads=[lb.rc], writes=[lb.rc])
        kb.op("dve", lambda e, v=v: e.tensor_tensor(A[:, :, v], A[:, :, v], nrm[:], ALU.mult),
              reads=[lb.rc], writes=[lb.rc])
    return modsb, A


def build_M(cfg):
    lb = LB()
    nc, kb, R, PS, RPS = lb.nc, lb.kb, lb.R, lb.PS, lb.RPS
    L = cfg.L
    cvec_in = lb.din("cvec", [128, KC, 2])
    wada_in = lb.din("w_ada", [L, D, 1536])
    bada_in = lb.din("b_ada", [128, L, 12])
    mout = lb.dout("modloc", [128, L * 24])
    with contextlib.ExitStack() as st:
        csil = st.enter_context(lb.tmp("csil", [128, KC, 2], F32))
        craw = st.enter_context(lb.tmp("craw", [128, KC, 2], F32))
        bada = st.enter_context(lb.tmp("bada", [128, L, 12], F32))
        mloc = st.enter_context(lb.tmp("mloc", [128, L, 12, 2], F32))
        wa = [st.enter_context(lb.tmp("wa%d" % i, [128, KC, 512], F32)) for i in range(2)]
        rcs = R("csil")
        kb.dma("sp", craw[:], cvec_in, writes=[rcs])
        kb.dma("sp", bada[:], bada_in, writes=[rcs])
        kb.op("act", lambda e: e.activation(csil[:], craw[:], AF.Silu), reads=[rcs], writes=[rcs])
        it = 0
        for l in range(L):
            for cb in range(3):
                i = it % 2
                it += 1
                rw = R("wa", i)
                kb.dma("sp", wa[i][:], wada_in[l, :, cb * 512:(cb + 1) * 512].rearrange("(k p) m -> p k m", p=128),
                       writes=[rw])
                b = 4 + i
                fns = []
                for j in range(4):
                    for k in range(KC):
                        fns.append(MM(PS[b][:, j * 2:(j + 1) * 2], wa[i][:, k, j * 128:(j + 1) * 128], csil[:, k, :],
                                      k == 0, k == KC - 1))
                kb.pe_group(fns, reads=[rw, rcs], writes=[RPS[b]])
                kb.op("dve", lambda e, l=l, cb=cb, b=b: e.tensor_tensor(
                    mloc[:, l, cb * 4:(cb + 1) * 4, :], PS[b][:, 0:8].rearrange("p (j v) -> p j v", v=2),
                    bada[:, l, cb * 4:(cb + 1) * 4].unsqueeze(2).to_broadcast([128, 4, 2]), ALU.add),
                    reads=[RPS[b], rcs], writes=[R("mloc")])
        kb.dma("sp", mout, mloc[:].rearrange("p l j v -> p (l j v)"), reads=[R("mloc")], writes=[R("mout")])
        kb.barrier(skip_queues=("pool",))
    return lb.finish()


def build_A(cfg):
    lb = LB()
    nc, kb, R, PS, RPS = lb.nc, lb.kb, lb.R, lb.PS, lb.RPS
    TL, TT, GL, GN = cfg.TL, cfg.TT, cfg.GL, cfg.GN
    hT = lb.din("hT", [KC, 128, TT])
    mods_in = lb.din("mods", [128, 96, 2])
    norm1_in = lb.din("norm1", [128, KC])
    w_in = lb.din("w_in", [D, 5120])
    gains_in = lb.din("gains", [128, 4])
    ropec_in = lb.din("ropec", [128, TL])
    ropes_in = lb.din("ropes", [128, TL])
    rotT_in = lb.din("rotT", [128, 128])
    fcs_in = lb.din("f_cs", [128, 256])
    xnT = lb.dout("xnT", [KC, 128, TT], BF16)
    zc = lb.dout("zc", [12, 128, TT], F32)
    zfc = lb.dout("zfc", [4, 128, CTX], F32)
    qT = lb.dout("qT", [8, 128, TT], BF16)
    nqT = lb.dout("nqT", [4, 128, TT], BF16)
    kT = lb.dout("kT", [2, 128, TT], BF16)
    knT = lb.dout("knT", [4, 128, TT], BF16)
    vg = lb.dout("vg", [TT, 256], BF16)
    vn = lb.dout("vn", [TT, 512], BF16)
    fx = lb.dout("fx", [TL // 128, 1024, 128], F32)
    wc_in = WConv(lb, "w_in", w_in, D, 5120, 512)

    modsb, A1 = load_mods(lb, mods_in, norm1_in, 16)
    gains = lb.const("gains", gains_in, [128, 4])
    rotT = lb.const("rotT", rotT_in, [128, 128])
    fcs = lb.const("fcs", fcs_in, [128, 256])
    cosF = lb.const("cosF", ropec_in, [128, TL])
    sinF = lb.const("sinF", ropes_in, [128, TL])
    rc = lb.rc

    for gi, segs in enumerate(groups(cfg)):
        with contextlib.ExitStack() as st:
            xn = st.enter_context(lb.tmp("xn", [128, KC, GN], BF16))
            tag = "g%d" % gi
            with contextlib.ExitStack() as st2:
                norm_mod(lb, st2, hT, segs, xn, A1, modsb, 0, xnT, tag)
                kb.barrier(skip_queues=("pool",))
            tiles = []
            for sg in segs:
                for t in sg.tiles():
                    tiles.append(t + (sg.is_ctx,))
            stg = [st.enter_context(lb.tmp("stg%d" % i, [128, 512], F32)) for i in range(4)]
            y0 = [st.enter_context(lb.tmp("y0%d" % i, [128, 512], F32)) for i in range(2)]
            sqh = [st.enter_context(lb.tmp("sqh%d" % i, [128, 512], BF16)) for i in range(2)]
            rsh = [st.enter_context(lb.tmp("rsh%d" % i, [128, 512], F32)) for i in range(2)]
            yy = [st.enter_context(lb.tmp("yy%d" % i, [128, 512], F32)) for i in range(2)]
            o1 = [st.enter_context(lb.tmp("o1%d" % i, [128, 512], F32)) for i in range(2)]
            o2 = [st.enter_context(lb.tmp("o2%d" % i, [128, 512], F32)) for i in range(2)]
            ob = [st.enter_context(lb.tmp("ob%d" % i, [128, 512], BF16)) for i in range(2)]
            cnt = {"e": 0, "h": 0}
            isctx = {(lo, tt): c for (lo, tt, n, c) in tiles}

            def epi(c, lo, tt, n, b):
                ctx_t = isctx[(lo, tt)]
                e = cnt["e"]
                cnt["e"] += 1
                if c < 12:
                    s = stg[e % 4]
                    rs_ = R(tag, "stg", e % 4)
                    evac(lb, b, n, s[:, 0:n], rs_, e)
                    kb.dma("sp", zc[c, :, tt:tt + n], s[:, 0:n], reads=[rs_], writes=[R(tag, "zc", c, tt)])
                elif c < 16:
                    g = c - 12
                    s = stg[e % 4]
                    rs_ = R(tag, "stg", e % 4)
                    evac(lb, b, n, s[:, 0:n], rs_, e)
                    if ctx_t:
                        kb.dma("sp", zfc[g, :, tt - TL:tt - TL + n], s[:, 0:n], reads=[rs_], writes=[R(tag, "zfc", g)])
                    else:
                        for part in range(2):
                            b2 = 4 + 2 * part + (e % 2)
                            kb.pe_group([MM(PS[b2][:, 0:n], fcs[:, part * 128:(part + 1) * 128], s[:, 0:n], True, True)],
                                        reads=[rs_, rc], writes=[RPS[b2]])
                            o = o1[e % 2] if part == 0 else o2[e % 2]
                            ro = R(tag, "ob%d" % (part + 1), e % 2)
                            evac(lb, b2, n, o[:, 0:n], ro, e + part)
                            col0 = part * 512 + g * 128
                            kb.dma("sp", fx[tt // 128:(tt + n) // 128, col0:col0 + 128, :].rearrange("b m t -> m b t"),
                                   o[:, 0:n].rearrange("m (b t) -> m b t", t=128), reads=[ro], writes=[R(tag, "fx", c, tt, part)])
                else:
                    if c < 20:
                        gi_, dst, rope = 0, nqT[c - 16], False
                    elif c < 24:
                        gi_, dst, rope = 1, knT[c - 20], False
                    elif c < 36:
                        gi_, dst, rope = 2, qT[c - 28], True
                    else:
                        gi_, dst, rope = 3, kT[c - 36], True
                    rope = rope and not ctx_t
                    i = cnt["h"] % 2
                    cnt["h"] += 1
                    ry, rq, rr, ryy, rob = R(tag, "y0", i), R(tag, "sqh", i), R(tag, "rsh", i), R(tag, "yy", i), R(tag, "ob", i)
                    kb.op("act", lambda e_: e_.copy(y0[i][:, 0:n], PS[b][:, 0:n]), reads=[RPS[b]], writes=[ry])
                    kb.op("act", lambda e_: e_.activation(sqh[i][:, 0:n], PS[b][:, 0:n], AF.Square), reads=[RPS[b]], writes=[rq])
                    b2 = 4 + i
                    kb.pe_group([MM(PS[b2][:, 0:n], lb.ones_bf[:], sqh[i][:, 0:n], True, True)], reads=[rq, rc], writes=[RPS[b2]])
                    kb.op("act", lambda e_: e_.activation(rsh[i][:, 0:n], PS[b2][:, 0:n], AF.Sqrt, bias=lb.epsb[:, 0:1],
                                                          scale=1.0 / 128), reads=[RPS[b2], rc], writes=[rr])
                    kb.op("dve", lambda e_: e_.reciprocal(rsh[i][:, 0:n], rsh[i][:, 0:n]), reads=[rr], writes=[rr])
                    kb.op("dve", lambda e_: e_.scalar_tensor_tensor(yy[i][:, 0:n], y0[i][:, 0:n], gains[:, gi_:gi_ + 1],
                                                                     rsh[i][:, 0:n], ALU.mult, ALU.mult),
                          reads=[ry, rr, rc], writes=[ryy])
                    if rope:
                        b3 = 6 + i
                        kb.pe_group([MM(PS[b3][:, 0:n], rotT[:], yy[i][:, 0:n], True, True)], reads=[ryy, rc], writes=[RPS[b3]])
                        r1, r2 = R(tag, "ob1", i), R(tag, "ob2", i)
                        kb.op("pool", lambda e_: e_.tensor_tensor(o1[i][:, 0:n], yy[i][:, 0:n], cosF[:, tt:tt + n], ALU.mult),
                              reads=[ryy, rc], writes=[r1])
                        kb.op("dve", lambda e_: e_.tensor_tensor(o2[i][:, 0:n], PS[b3][:, 0:n], sinF[:, tt:tt + n], ALU.mult),
                              reads=[RPS[b3], rc], writes=[r2])
                        kb.op("pool", lambda e_: e_.tensor_tensor(ob[i][:, 0:n], o1[i][:, 0:n], o2[i][:, 0:n], ALU.add),
                              reads=[r1, r2], writes=[rob])
                    else:
                        kb.op("pool", lambda e_: e_.tensor_copy(ob[i][:, 0:n], yy[i][:, 0:n]), reads=[ryy], writes=[rob])
                    kb.dma("sp", dst[:, tt:tt + n], ob[i][:, 0:n], reads=[rob], writes=[R(tag, "hd", c, tt)])

            blocks = [(0, [0, 1, 2, 3]), (1, [0, 1, 2, 3]), (2, [0, 1, 2, 3]), (3, [0, 1, 2, 3]),
                      (4, [0, 1, 2, 3]), (5, [0, 1, 2, 3]), (7, [0, 1, 2, 3]), (8, [0, 1, 2, 3]),
                      (9, [0, 1])]
            wc_in.issue(6)
            sweep(lb, st, xn, lambda lo: R(tag, "xn", lo), KC, wc_in, blocks, [t[:3] for t in tiles], epi, "wA")
            wv = st.enter_context(lb.tmp("wv", [128, KC, 256], BF16))
            wvn = st.enter_context(lb.tmp("wvn", [128, KC, 512], BF16))
            vst = [st.enter_context(lb.tmp("vst%d" % i, [128, 768], BF16)) for i in range(2)]
            rwv = R(tag, "wv")
            kb.dma("sp", wv[:], wc_in.blk(9)[:, 256:512].rearrange("(k p) m -> p k m", p=128), reads=[wc_in.res(9)], writes=[rwv])
            kb.dma("sp", wvn[:], wc_in.blk(6).rearrange("(k p) m -> p k m", p=128), reads=[wc_in.res(6)], writes=[R(tag, "wvn")])
            bi = 0
            for sg in segs:
                for t0 in range(0, sg.n, 128):
                    lo, tt = sg.lo + t0, sg.tt + t0
                    i = bi % 2
                    bi += 1
                    ba, bb = 4 + i, 6 + i
                    rx = R(tag, "xn", sg.lo + (t0 // 512) * 512 if False else [tl for (tl, _, n_) in sg.tiles() if tl <= lo < tl + n_][0])
                    kb.pe_group([MM(PS[ba][:, 0:256], xn[:, k, lo:lo + 128], wv[:, k, :], k == 0, k == KC - 1) for k in range(KC)],
                                reads=[rx, rwv], writes=[RPS[ba]])
                    kb.pe_group([MM(PS[bb][:, 0:512], xn[:, k, lo:lo + 128], wvn[:, k, :], k == 0, k == KC - 1) for k in range(KC)],
                                reads=[rx, R(tag, "wvn")], writes=[RPS[bb]])
                    rv = R(tag, "vst", i)
                    kb.op("act", lambda e_, i=i, ba=ba: e_.copy(vst[i][:, 0:256], PS[ba][:, 0:256]), reads=[RPS[ba]], writes=[rv])
                    kb.op("dve", lambda e_, i=i, bb=bb: e_.tensor_copy(vst[i][:, 256:768], PS[bb][:, 0:512]), reads=[RPS[bb]], writes=[rv])
                    kb.dma("sp", vg[tt:tt + 128, :], vst[i][:, 0:256], reads=[rv], writes=[R(tag, "vg", tt)])
                    kb.dma("sp", vn[tt:tt + 128, :], vst[i][:, 256:768], reads=[rv], writes=[R(tag, "vn", tt)])
            kb.barrier(skip_queues=("pool",))
    return lb.finish()


_PROG = {}


def get_prog(name, cfg, builder):
    key = (name, cfg.S, cfg.L)
    if key not in _PROG:
        _PROG[key] = builder(cfg)
    return _PROG[key]


def launch(lb, maps):
    res = run_bass_kernel_spmd(lb.nc, maps, core_ids=list(range(NCORE)))
    return res.results


def host_consts(cfg):
    f32 = np.float32
    S, T1, TL = cfg.S, cfg.T1, cfg.TL
    c = {}
    c["ident"] = np.eye(128, dtype=f32)
    rotT = np.zeros((128, 128), f32)
    for i in range(64):
        rotT[i + 64, i] = -1.0
        rotT[i, i + 64] = 1.0
    c["rotT"] = rotT
    a = np.arange(128)
    ang = 2 * np.pi * np.outer(a, a) / 128.0
    c["f_cs"] = np.concatenate([np.cos(ang), np.sin(ang)], 1).astype(f32)
    a1 = np.arange(T1)
    ang1 = 2 * np.pi * np.outer(a1, a1) / T1
    c["f_s1"] = np.ascontiguousarray(np.stack([np.cos(ang1), -np.sin(ang1), -np.cos(ang1)], 1).astype(f32))
    angt = 2 * np.pi * np.outer(a1, a) / float(S)
    c["f_tw"] = np.ascontiguousarray(np.stack([np.cos(angt), np.sin(angt)], 1).astype(f32))
    ac = np.arange(256)
    angc = 2 * np.pi * np.outer(ac, ac) / 256.0
    fc = np.stack([np.cos(angc), -np.sin(angc)], 1).astype(f32)
    c["f_ctx"] = np.ascontiguousarray(fc.reshape(2, 128, 2, 256).transpose(1, 0, 2, 3))
    inv_freq = (np.float32(10000.0) ** (-np.arange(32, dtype=f32) / np.float32(32))).astype(f32)
    c["ropec"], c["ropes"] = [], []
    for core in range(NCORE):
        t = np.arange(core * TL, (core + 1) * TL)
        row = (t // GRID_W).astype(f32)
        cl = (t % GRID_W).astype(f32)
        angr = np.concatenate([row[:, None] * inv_freq, cl[:, None] * inv_freq], -1).astype(f32)
        cosv = np.cos(angr).astype(f32).T
        sinv = np.sin(angr).astype(f32).T
        c["ropec"].append(np.ascontiguousarray(np.concatenate([cosv, cosv], 0)))
        c["ropes"].append(np.ascontiguousarray(np.concatenate([sinv, sinv], 0)))
    return c


def pl(a, nch):
    return np.ascontiguousarray(np.asarray(a, np.float32).reshape(nch, 128).T)


def to_fm(a):
    T, F = a.shape
    return np.ascontiguousarray(a.T.reshape(F // 128, 128, T))


def from_fm(a):
    return np.ascontiguousarray(a.reshape(-1, a.shape[2]).T)


def run_M(inp, cfg):
    L = cfg.L
    f32 = np.float32
    cv = np.stack([np.asarray(inp["c"], f32)[0], np.asarray(inp["c_ctx"], f32)], -1)
    cvec = np.ascontiguousarray(cv.reshape(KC, 128, 2).transpose(1, 0, 2))
    maps = []
    for c in range(NCORE):
        ba = np.asarray(inp["b_ada"], f32)[:L, c * 1536:(c + 1) * 1536]
        maps.append({"cvec": cvec,
                     "w_ada": np.ascontiguousarray(np.asarray(inp["w_ada"], f32)[:L, :, c * 1536:(c + 1) * 1536]),
                     "b_ada": np.ascontiguousarray(ba.reshape(L, 12, 128).transpose(2, 0, 1))})
    res = launch(get_prog("M", cfg, build_M), maps)
    allm = np.stack([res[c]["modloc"].reshape(128, L, 12, 2) for c in range(NCORE)], 0)
    mods = [np.ascontiguousarray(allm[:, :, l].transpose(1, 0, 2, 3).reshape(128, 96, 2)) for l in range(L)]
    return mods


def run_A(inp, cfg, l, hT, mods, hc):
    f32 = np.float32
    gains = np.ascontiguousarray(np.stack([np.asarray(inp[k], f32)[l] for k in
                                           ("na_q_gain", "na_k_gain", "gqa_q_gain", "gqa_k_gain")], -1))
    w_in = np.ascontiguousarray(np.asarray(inp["w_in"], f32)[l])
    n1 = pl(inp["norm1"][l], KC)
    maps = []
    for c in range(NCORE):
        maps.append({"hT": hT[c], "mods": mods[l], "norm1": n1, "w_in": w_in, "gains": gains,
                     "ropec": hc["ropec"][c], "ropes": hc["ropes"][c], "rotT": hc["rotT"], "f_cs": hc["f_cs"]})
    return launch(get_prog("A", cfg, build_A), maps)


def build_B(cfg):
    lb = LB()
    nc, kb, R, PS, RPS = lb.nc, lb.kb, lb.R, lb.PS, lb.RPS
    T1, S = cfg.T1, cfg.S
    xs_in = lb.din("xs", [T1, 2, 64, 128])
    fs1_in = lb.din("f_s1", [T1, 3, T1])
    ftw_in = lb.din("f_tw", [T1, 2, 128])
    fcs_in = lb.din("f_cs", [128, 256])
    ident_in = lb.din("ident", [128, 128])
    fy = lb.dout("fy", [128, 64, T1], F32)
    fs1 = lb.const("fs1", fs1_in, [T1, 3, T1])
    ftw = lb.const("ftw", ftw_in, [T1, 2, 128])
    fcs = lb.const("fcs", fcs_in, [128, 256])
    ident = lb.const("ident", ident_in, [128, 128])
    rc = lb.rc
    norm = 1.0 / math.sqrt(float(S) * 128.0)
    with contextlib.ExitStack() as st:
        xs = st.enter_context(lb.tmp("xs", [T1, 2, 64, 128], F32))
        V = st.enter_context(lb.tmp("V", [128, 2, 64, T1], F32))
        Y = st.enter_context(lb.tmp("Y", [128, 64, T1], F32))
        tt_ = [st.enter_context(lb.tmp("tw%d" % i, [T1, 4, 128], F32)) for i in range(4)]
        for blk in range(16):
            for part in range(2):
                kb.dma("sp", xs[:, part, 4 * blk:4 * blk + 4, :], xs_in[:, part, 4 * blk:4 * blk + 4, :],
                       writes=[R("xs", blk)])
        tcb = ftw[:, 0:1, :].to_broadcast([T1, 4, 128])
        tsb = ftw[:, 1:2, :].to_broadcast([T1, 4, 128])
        for blk in range(16):
            rx = R("xs", blk)
            Ab = xs[:, 0, 4 * blk:4 * blk + 4, :].rearrange("p c t -> p (c t)")
            Bb = xs[:, 1, 4 * blk:4 * blk + 4, :].rearrange("p c t -> p (c t)")
            b0, b1 = (blk % 2) * 2, (blk % 2) * 2 + 1
            kb.pe_group([MM(PS[b0][0:T1, :], fs1[:, 0, :], Ab, True, False), MM(PS[b0][0:T1, :], fs1[:, 1, :], Bb, False, True)],
                        reads=[rx, rc], writes=[RPS[b0]])
            kb.pe_group([MM(PS[b1][0:T1, :], fs1[:, 2, :], Bb, True, False), MM(PS[b1][0:T1, :], fs1[:, 1, :], Ab, False, True)],
                        reads=[rx, rc], writes=[RPS[b1]])
            ure = PS[b0][0:T1, :].rearrange("p (c t) -> p c t", c=4)
            uim = PS[b1][0:T1, :].rearrange("p (c t) -> p c t", c=4)
            rt = [R("tw", i) for i in range(4)]
            kb.op("dve", lambda e: e.tensor_tensor(tt_[0][:], ure, tcb, ALU.mult), reads=[RPS[b0], rc], writes=[rt[0]])
            kb.op("dve", lambda e: e.tensor_tensor(tt_[1][:], uim, tsb, ALU.mult), reads=[RPS[b1], rc], writes=[rt[1]])
            kb.op("dve", lambda e: e.tensor_tensor(tt_[2][:], uim, tcb, ALU.mult), reads=[RPS[b1], rc], writes=[rt[2]])
            kb.op("dve", lambda e: e.tensor_tensor(tt_[3][:], ure, tsb, ALU.mult), reads=[RPS[b0], rc], writes=[rt[3]])
            kb.op("pool", lambda e: e.tensor_tensor(xs[:, 0, 4 * blk:4 * blk + 4, :], tt_[0][:], tt_[1][:], ALU.add),
                  reads=[rt[0], rt[1]], writes=[rx])
            kb.op("pool", lambda e: e.tensor_tensor(xs[:, 1, 4 * blk:4 * blk + 4, :], tt_[2][:], tt_[3][:], ALU.subtract),
                  reads=[rt[2], rt[3]], writes=[rx])
        ei = 0
        for part in range(2):
            for cg in range(16):
                b = 4 + (ei % 4)
                fns = [lambda pe, j=j: pe.transpose(PS[b][:, j * T1:(j + 1) * T1], xs[:, part, 4 * cg + j, :], ident[0:T1, 0:T1])
                       for j in range(4)]
                kb.pe_group(fns, reads=[R("xs", cg), rc], writes=[RPS[b]])
                evac(lb, b, 4 * T1, V[:, part, 4 * cg:4 * cg + 4, :].rearrange("p c k -> p (c k)"), R("V", part, cg), ei)
                ei += 1
        cb = 512 // T1
        for blk in range(64 // cb):
            b = blk % 4
            reads = [R("V", p_, cg) for p_ in range(2) for cg in range((blk * cb) // 4, max((blk * cb) // 4 + 1, ((blk + 1) * cb + 3) // 4))]
            vre = V[:, 0, blk * cb:(blk + 1) * cb, :].rearrange("p c k -> p (c k)")
            vim = V[:, 1, blk * cb:(blk + 1) * cb, :].rearrange("p c k -> p (c k)")
            kb.pe_group([MM(PS[b][:, :], fcs[:, 0:128], vre, True, False), MM(PS[b][:, :], fcs[:, 128:256], vim, False, True)],
                        reads=reads + [rc], writes=[RPS[b]])
            evac(lb, b, 512, Y[:, blk * cb:(blk + 1) * cb, :].rearrange("p c k -> p (c k)"), R("Y"), blk, scale=norm)
        kb.dma("sp", fy, Y[:], reads=[R("Y")], writes=[R("fy")])
        kb.barrier(skip_queues=("pool",))
    return lb.finish()


def run_B(cfg, rA, hc):
    T1 = cfg.T1
    fx_all = np.concatenate([np.asarray(rA[c]["fx"]) for c in range(NCORE)], 0)
    maps = []
    for c in range(NCORE):
        xs = np.stack([fx_all[:, 64 * c:64 * c + 64, :], fx_all[:, 512 + 64 * c:512 + 64 * c + 64, :]], 1)
        maps.append({"xs": np.ascontiguousarray(xs), "f_s1": hc["f_s1"], "f_tw": hc["f_tw"], "f_cs": hc["f_cs"],
                     "ident": hc["ident"]})
    res = launch(get_prog("B", cfg, build_B), maps)
    yall = np.stack([np.asarray(res[c]["fy"]) for c in range(NCORE)], 0)
    ycols = yall.transpose(0, 2, 1, 3).reshape(512, cfg.S)
    return [np.ascontiguousarray(ycols[:, r * cfg.TL:(r + 1) * cfg.TL].reshape(4, 128, cfg.TL)) for r in range(NCORE)]


def build_C(cfg):
    lb = LB()
    nc, kb, R, PS, RPS = lb.nc, lb.kb, lb.R, lb.PS, lb.RPS
    TL, TT, GL, GN, S = cfg.TL, cfg.TT, cfg.GL, cfg.GN, cfg.S
    TLR, GR, NKEY, NCH, NW = cfg.TLR, cfg.GR, cfg.NKEY, cfg.NCH, cfg.NW
    specs = na_row_specs(cfg)
    NTAB = specs["ncols"]
    hT = lb.din("hT", [KC, 128, TT])
    xnT = lb.din("xnT", [KC, 128, TT], BF16)
    zc = lb.din("zc", [12, 128, TT])
    zce = lb.din("zce", [8, 128, TL + 2])
    zfc = lb.din("zfc", [4, 128, CTX])
    qT = lb.din("qT", [8, 128, TT], BF16)
    nqT = lb.din("nqT", [4, 128, TT], BF16)
    kTall = lb.din("kTall", [2, 128, NKEY], BF16)
    vgall = lb.din("vgall", [NKEY, 256], BF16)
    knw = lb.din("knw", [4, 128, NW], BF16)
    vnw = lb.din("vnw", [NW, 512], BF16)
    cnkT = lb.din("cnkT", [4, 128, CTX], BF16)
    cvn = lb.din("cvn", [CTX, 512], BF16)
    natab = lb.din("natab", [64, 4, NTAB])
    fyT = lb.din("fyT", [4, 128, TL])
    mods_in = lb.din("mods", [128, 96, 2])
    norm2_in = lb.din("norm2", [128, KC])
    bgate_in = lb.din("b_gate", [128, 64])
    convw_in = lb.din("conv_w", [128, 3, 4])
    fcs_in = lb.din("f_cs", [128, 256])
    fctx_in = lb.din("f_ctx", [128, 2, 2, 256])
    w_gate = lb.din("w_gate", [D, 8192])
    w_outs = [lb.din("w_conv_out", [512, D]), lb.din("w_fourier_out", [512, D]),
              lb.din("w_na_out", [512, D]), lb.din("w_gqa_out", [1024, D])]
    w_o = lb.din("w_o", [D, D])
    hmT = lb.dout("hmT", [KC, 128, TT], F32)
    xn2T = lb.dout("xn2T", [KC, 128, TT], BF16)
    gT = lb.dscr("gT", [64, 128, TT], BF16)
    ysT = lb.dscr("ysT", [20, 128, TT], BF16)
    mgT = lb.dscr("mgT", [KC, 128, TT], BF16)
    wc_gate = WConv(lb, "w_gate", w_gate, D, 8192, 512)
    wc_outs = [WConv(lb, nm, w_outs[b_], kk, D, 512) for b_, (nm, kk) in
               enumerate((("w_conv_out", 512), ("w_fourier_out", 512), ("w_na_out", 512), ("w_gqa_out", 1024)))]
    wc_o = WConv(lb, "w_o", w_o, D, D, 512)
    wc_gate.issue_all()
    for w_ in wc_outs:
        w_.issue_all()
    wc_o.issue_all()

    modsb, A2 = load_mods(lb, mods_in, norm2_in, 64)
    bgate = lb.const("bgate", bgate_in, [128, 64])
    convw = lb.const("convw", convw_in, [128, 3, 4])
    fcs = lb.const("fcs", fcs_in, [128, 256])
    fctx = lb.const("fctx", fctx_in, [128, 2, 2, 256])
    rc = lb.rc

    for gi, segs in enumerate(groups(cfg)):
        tag = "g%d" % gi
        tiles = []
        for sg in segs:
            for t in sg.tiles():
                tiles.append(t + (sg.is_ctx,))
        t3 = [t[:3] for t in tiles]
        isctx = {(lo, tt): c for (lo, tt, n, c) in tiles}

        def load_res(dst, src, nch, key):
            for (lo, tt, n) in t3:
                kb.dma("sp", dst[:, 0:nch, lo:lo + n], src[0:nch, :, tt:tt + n].rearrange("k p t -> p k t"),
                       writes=[R(tag, key, lo)])

        with contextlib.ExitStack() as st:
            NB = GL + 2
            xa = [st.enter_context(lb.tmp("xa%d" % i, [128, NB], F32)) for i in range(2)]
            cg = [st.enter_context(lb.tmp("cg%d" % i, [128, NB], F32)) for i in range(2)]
            bg = [st.enter_context(lb.tmp("bg%d" % i, [128, NB], F32)) for i in range(2)]
            uu = [st.enter_context(lb.tmp("uu%d" % i, [128, NB], F32)) for i in range(2)]
            tc_ = [st.enter_context(lb.tmp("tc%d" % i, [128, NB], F32)) for i in range(2)]
            yb = [st.enter_context(lb.tmp("yb%d" % i, [128, NB], BF16)) for i in range(2)]
            it = 0
            for sg in segs:
                n = sg.n
                for j in range(4):
                    i = it % 2
                    it += 1
                    rin, ru, rt_, ry = R(tag, "cin", i), R(tag, "cu", i), R(tag, "ct", i), R(tag, "cy", i)
                    if not sg.is_ctx:
                        kb.dma("sp", xa[i][:, 0:n + 2], zce[j, :, sg.tt:sg.tt + n + 2], writes=[rin])
                        kb.dma("sp", cg[i][:, 0:n + 2], zce[4 + j, :, sg.tt:sg.tt + n + 2], writes=[rin])
                    else:
                        kb.op("dve", lambda e_: e_.memset(xa[i][:, 0:n + 2], 0.0), writes=[rin])
                        kb.op("dve", lambda e_: e_.memset(cg[i][:, 0:n + 2], 0.0), writes=[rin])
                        kb.dma("sp", xa[i][:, 1:n + 1], zc[j, :, sg.tt:sg.tt + n], writes=[rin])
                        kb.dma("sp", cg[i][:, 1:n + 1], zc[8 + j, :, sg.tt:sg.tt + n], writes=[rin])
                    kb.dma("sp", bg[i][:, 0:n], zc[4 + j, :, sg.tt:sg.tt + n], writes=[rin])
                    kb.op("dve", lambda e_: e_.tensor_tensor(uu[i][:, 0:n + 2], xa[i][:, 0:n + 2], cg[i][:, 0:n + 2], ALU.mult),
                          reads=[rin], writes=[ru])
                    kb.op("dve", lambda e_: e_.tensor_scalar(tc_[i][:, 0:n], uu[i][:, 0:n], convw[:, 0, j:j + 1], None, ALU.mult),
                          reads=[ru, rc], writes=[rt_])
                    kb.op("dve", lambda e_: e_.scalar_tensor_tensor(tc_[i][:, 0:n], uu[i][:, 1:n + 1], convw[:, 1, j:j + 1],
                                                                     tc_[i][:, 0:n], ALU.mult, ALU.add), reads=[ru, rc, rt_], writes=[rt_])
                    kb.op("dve", lambda e_: e_.scalar_tensor_tensor(tc_[i][:, 0:n], uu[i][:, 2:n + 2], convw[:, 2, j:j + 1],
                                                                     tc_[i][:, 0:n], ALU.mult, ALU.add), reads=[ru, rc, rt_], writes=[rt_])
                    kb.op("dve", lambda e_: e_.tensor_tensor(yb[i][:, 0:n], tc_[i][:, 0:n], bg[i][:, 0:n], ALU.mult),
                          reads=[rt_, rin], writes=[ry])
                    kb.dma("sp", ysT[j, :, sg.tt:sg.tt + n], yb[i][:, 0:n], reads=[ry], writes=[R(tag, "ys", j, sg.tt)])
            kb.barrier(skip_queues=("pool",))

        with contextlib.ExitStack() as st:
            for sg in segs:
                if not sg.is_ctx:
                    fyb = [st.enter_context(lb.tmp("fyb%d" % i, [128, GL], F32)) for i in range(2)]
                    fyo = [st.enter_context(lb.tmp("fyo%d" % i, [128, GL], BF16)) for i in range(2)]
                    for g in range(4):
                        i = g % 2
                        kb.dma("sp", fyb[i][:, 0:sg.n], fyT[g, :, sg.tt:sg.tt + sg.n], writes=[R(tag, "fyb", i)])
                        kb.op("act", lambda e_: e_.copy(fyo[i][:, 0:sg.n], fyb[i][:, 0:sg.n]), reads=[R(tag, "fyb", i)],
                              writes=[R(tag, "fyo", i)])
                        kb.dma("sp", ysT[4 + g, :, sg.tt:sg.tt + sg.n], fyo[i][:, 0:sg.n], reads=[R(tag, "fyo", i)],
                               writes=[R(tag, "ys", 4 + g, sg.tt)])
                else:
                    zf = st.enter_context(lb.tmp("zf", [128, 4, CTX], F32))
                    xtm = st.enter_context(lb.tmp("xtm", [128, 2, 4, 256], F32))
                    yo = st.enter_context(lb.tmp("yo", [128, 4, CTX], BF16))
                    kb.dma("sp", zf[:], zfc.rearrange("g p t -> p g t"), writes=[R(tag, "zf")])
                    e = 0
                    for blk in range(2):
                        for g in range(4):
                            b = e % 4
                            kb.pe_group([MM(PS[b][:, 0:256], zf[:, g, blk * 128:(blk + 1) * 128], fcs[:, 0:256], True, True)],
                                        reads=[R(tag, "zf"), rc], writes=[RPS[b]])
                            evac(lb, b, 256, xtm[:, blk, g, :], R(tag, "xtm"), e)
                            e += 1
                    for g in range(4):
                        b = 4 + g
                        fns = []
                        for blk in range(2):
                            fns.append(MM(PS[b][:, 0:256], xtm[:, blk, g, 0:128], fctx[:, blk, 0, :], blk == 0, False))
                            fns.append(MM(PS[b][:, 0:256], xtm[:, blk, g, 128:256], fctx[:, blk, 1, :], False, blk == 1))
                        kb.pe_group(fns, reads=[R(tag, "xtm"), rc], writes=[RPS[b]])
                        evac(lb, b, 256, yo[:, g, :], R(tag, "yo"), g, scale=1.0 / math.sqrt(256.0 * 128.0))
                    kb.dma("sp", ysT[4:8, :, TL:TT].rearrange("g p t -> p g t"), yo[:], reads=[R(tag, "yo")],
                           writes=[R(tag, "ys", "fctx")])
            kb.barrier(skip_queues=("pool",))

        with contextlib.ExitStack() as st:
            lr0 = gi * GR
            kmin = lr0 - 4
            kmax = max([specs["rows"][lr][0] + specs["rows"][lr][1] for lr in range(lr0, lr0 + GR)])
            nrw = kmax - kmin
            sp_rows = [lr for lr in range(lr0, lr0 + GR) if specs["rows"][lr][2] != 0]
            s0 = min([specs["rows"][lr][2] for lr in sp_rows])
            s1 = max([specs["rows"][lr][2] + specs["rows"][lr][1] * 64 for lr in sp_rows])
            KnT = st.enter_context(lb.tmp("KnT", [128, 4, nrw * 64], BF16))
            Vn = st.enter_context(lb.tmp("Vn", [64, nrw, 512], BF16))
            KcT = st.enter_context(lb.tmp("KcT", [128, 4, CTX], BF16))
            Vc = st.enter_context(lb.tmp("Vc", [64, 4, 512], BF16))
            Qn = st.enter_context(lb.tmp("Qn", [128, 4, GN], BF16))
            On = st.enter_context(lb.tmp("On", [128, 4, GN], BF16))
            tabm = st.enter_context(lb.tmp("tabm", [64, 4, 512], F32))
            tabs = st.enter_context(lb.tmp("tabs", [64, 4, s1 - s0], F32))
            sbb = [st.enter_context(lb.tmp("sbb%d" % i, [64, 768], F32)) for i in range(2)]
            Pn = [st.enter_context(lb.tmp("Pn%d" % i, [64, 1024], BF16)) for i in range(2)]
            rsn = [st.enter_context(lb.tmp("rsn%d" % i, [128, 64], F32)) for i in range(2)]
            rk = R(tag, "nak")
            kb.dma("sp", KnT[:], knw[:, :, (kmin + 4) * 64:(kmax + 4) * 64].rearrange("h p t -> p h t"), writes=[rk])
            kb.dma("sp", Vn[:], vnw[(kmin + 4) * 64:(kmax + 4) * 64, :].rearrange("(r c) f -> c r f", c=64), writes=[rk])
            kb.dma("sp", KcT[:], cnkT.rearrange("h p t -> p h t"), writes=[rk])
            kb.dma("sp", Vc[:], cvn.rearrange("(r c) f -> c r f", c=64), writes=[rk])
            kb.dma("sp", tabm[:], natab[:, :, 0:512], writes=[rk])
            kb.dma("sp", tabs[:], natab[:, :, s0:s1], writes=[rk])
            for (lo, tt, n) in t3:
                kb.dma("sp", Qn[:, :, lo:lo + n], nqT[:, :, tt:tt + n].rearrange("h p t -> p h t"), writes=[rk])
            qblocks = []
            for sg in segs:
                for q0 in range(0, sg.n, 64):
                    if sg.is_ctx:
                        qblocks.append((sg.lo + q0, 0, 0, None))
                    else:
                        lr = (sg.tt + q0) // 64
                        k0, nk, off = specs["rows"][lr]
                        qblocks.append((sg.lo + q0, k0, nk, off))
            it = 0
            ron = R(tag, "On")
            for (qlo, k0, nk, off) in qblocks:
                for h in range(4):
                    i = it % 2
                    it += 1
                    nb_ = nk + 4
                    SPv = lb.ps_all[0:64, i * 1024:i * 1024 + nb_ * 64]
                    rsp = [RPS[2 * i], RPS[2 * i + 1]]
                    fns = []
                    for a in range(nk):
                        kk = (k0 + a - kmin) * 64
                        fns.append(MM(SPv[:, a * 64:(a + 1) * 64], KnT[:, h, kk:kk + 64], Qn[:, h, qlo:qlo + 64], True, True))
                    for a in range(4):
                        fns.append(MM(SPv[:, (nk + a) * 64:(nk + a + 1) * 64], KcT[:, h, a * 64:(a + 1) * 64], Qn[:, h, qlo:qlo + 64], True, True))
                    kb.pe_group(fns, reads=[rk], writes=rsp)
                    rP, rsb = R(tag, "Pn", i), R(tag, "sbb", i)
                    if nk > 0:
                        tab = tabm[:, h, 0:512] if off == 0 else tabs[:, h, off - s0:off - s0 + nk * 64]
                        kb.op("dve", lambda e_: e_.scalar_tensor_tensor(sbb[i][:, 0:nk * 64], SPv[:, 0:nk * 64], SCALE, tab,
                                                                         ALU.mult, ALU.add), reads=rsp + [rk], writes=[rsb])
                        kb.op("act", lambda e_: e_.activation(Pn[i][:, 0:nk * 64], sbb[i][:, 0:nk * 64], AF.Exp),
                              reads=[rsb], writes=[rP])
                    kb.op("act", lambda e_: e_.activation(Pn[i][:, nk * 64:nb_ * 64], SPv[:, nk * 64:nb_ * 64], AF.Exp, scale=SCALE),
                          reads=rsp, writes=[rP])
                    ba, bs = 4 + i, 6 + i
                    fns = []
                    for a in range(nk):
                        fns.append(MM(PS[ba][:, 0:64], Vn[:, k0 + a - kmin, h * 128:(h + 1) * 128], Pn[i][:, a * 64:(a + 1) * 64], a == 0, False))
                    for a in range(4):
                        fns.append(MM(PS[ba][:, 0:64], Vc[:, a, h * 128:(h + 1) * 128], Pn[i][:, (nk + a) * 64:(nk + a + 1) * 64],
                                      nk == 0 and a == 0, a == 3))
                    kb.pe_group(fns, reads=[rP, rk], writes=[RPS[ba]])
                    fns = [MM(PS[bs][:, 0:64], lb.ones_bf[0:64, :], Pn[i][:, a * 64:(a + 1) * 64], a == 0, a == nb_ - 1) for a in range(nb_)]
                    kb.pe_group(fns, reads=[rP, rc], writes=[RPS[bs]])
                    rr = R(tag, "rsn", i)
                    kb.op("dve", lambda e_: e_.reciprocal(rsn[i][:], PS[bs][:, 0:64]), reads=[RPS[bs]], writes=[rr])
                    kb.op("dve", lambda e_: e_.tensor_tensor(On[:, h, qlo:qlo + 64], PS[ba][:, 0:64], rsn[i][:], ALU.mult),
                          reads=[RPS[ba], rr], writes=[ron])
            for (lo, tt, n) in t3:
                kb.dma("sp", ysT[8:12, :, tt:tt + n].rearrange("h p t -> p h t"), On[:, :, lo:lo + n], reads=[ron],
                       writes=[R(tag, "ys", "na", tt)])
            kb.barrier(skip_queues=("pool",))

        for g2 in range(2):
            with contextlib.ExitStack() as st:
                KT = st.enter_context(lb.tmp("KT", [128, NKEY], BF16))
                Vg = st.enter_context(lb.tmp("Vg", [128, NCH, 128], BF16))
                Qg = st.enter_context(lb.tmp("Qg", [128, 4, GN], BF16))
                Og = st.enter_context(lb.tmp("Og", [128, 4, GN], BF16))
                Pb = [st.enter_context(lb.tmp("Pb%d" % i, [128, 512], BF16)) for i in range(3)]
                rsg = [st.enter_context(lb.tmp("rsg%d" % i, [128, 512], F32)) for i in range(2)]
                rk = R(tag, "gk", g2)
                for (o, s_) in split_cols(NKEY, 4096):
                    kb.dma("sp", KT[:, o:o + s_], kTall[g2, :, o:o + s_], writes=[rk])
                for c0 in range(0, NCH, 32):
                    c1 = min(NCH, c0 + 32)
                    kb.dma("sp", Vg[:, c0:c1, :], vgall[c0 * 128:c1 * 128, g2 * 128:(g2 + 1) * 128].rearrange("(c p) d -> p c d", p=128),
                           writes=[rk])
                for (lo, tt, n) in t3:
                    kb.dma("sp", Qg[:, :, lo:lo + n], qT[4 * g2:4 * g2 + 4, :, tt:tt + n].rearrange("h p t -> p h t"), writes=[rk])
                rog = R(tag, "Og", g2)
                it = 0
                sidx = 0
                for j in range(4):
                    for (lo, tt, n) in t3:
                        chunks = list(range(S // 128, NCH)) if isctx[(lo, tt)] else list(range(NCH))
                        ba, bs = 4 + it % 2, 6 + it % 2
                        it += 1

                        def emit_s(c, si):
                            bq = si % 2
                            kb.pe_group([MM(PS[bq][:, 0:n], KT[:, c * 128:(c + 1) * 128], Qg[:, j, lo:lo + n], True, True)],
                                        reads=[rk], writes=[RPS[bq]])

                        emit_s(chunks[0], sidx)
                        for ci, c in enumerate(chunks):
                            if ci + 1 < len(chunks):
                                emit_s(chunks[ci + 1], sidx + ci + 1)
                            bq = (sidx + ci) % 2
                            pi = (sidx + ci) % 3
                            rp = R(tag, "Pb", pi)
                            kb.op("act", lambda e_: e_.activation(Pb[pi][:, 0:n], PS[bq][:, 0:n], AF.Exp, scale=SCALE),
                                  reads=[RPS[bq]], writes=[rp])
                            first, last = ci == 0, ci == len(chunks) - 1
                            kb.pe_group([MM(PS[ba][:, 0:n], Vg[:, c, :], Pb[pi][:, 0:n], first, last),
                                         MM(PS[bs][:, 0:n], lb.ones_bf[:], Pb[pi][:, 0:n], first, last)],
                                        reads=[rp, rk, rc], writes=[RPS[ba], RPS[bs]])
                        sidx += len(chunks)
                        ri = it % 2
                        rr = R(tag, "rsg", ri)
                        kb.op("dve", lambda e_: e_.reciprocal(rsg[ri][:, 0:n], PS[bs][:, 0:n]), reads=[RPS[bs]], writes=[rr])
                        kb.op("dve", lambda e_: e_.tensor_tensor(Og[:, j, lo:lo + n], PS[ba][:, 0:n], rsg[ri][:, 0:n], ALU.mult),
                              reads=[RPS[ba], rr], writes=[rog])
                for (lo, tt, n) in t3:
                    kb.dma("sp", ysT[12 + 4 * g2:16 + 4 * g2, :, tt:tt + n].rearrange("h p t -> p h t"), Og[:, :, lo:lo + n],
                           reads=[rog], writes=[R(tag, "ys", "gqa", g2, tt)])
                kb.barrier(skip_queues=("pool",))

        with contextlib.ExitStack() as st:
            xn = st.enter_context(lb.tmp("xn", [128, KC, GN], BF16))
            gst = [st.enter_context(lb.tmp("gst%d" % i, [128, 512], BF16)) for i in range(4)]
            load_res(xn, xnT, KC, "xn")
            cnt = {"e": 0}

            def epi_g(c, lo, tt, n, b):
                e = cnt["e"] % 4
                cnt["e"] += 1
                rg = R(tag, "gst", e)
                kb.op("act", lambda e_: e_.activation(gst[e][:, 0:n], PS[b][:, 0:n], AF.Sigmoid, bias=bgate[:, c:c + 1], scale=1.0),
                      reads=[RPS[b], rc], writes=[rg])
                kb.dma("sp", gT[c, :, tt:tt + n], gst[e][:, 0:n], reads=[rg], writes=[R(tag, "gT", c, tt)])

            sweep(lb, st, xn, lambda lo: R(tag, "xn", lo), KC, wc_gate, [(i, [0, 1, 2, 3]) for i in range(16)],
                  t3, epi_g, "wG")
            kb.barrier(skip_queues=("pool",))

        with contextlib.ExitStack() as st:
            ys = st.enter_context(lb.tmp("ys", [128, 20, GN], BF16))
            wm = [st.enter_context(lb.tmp("wm%d" % i, [128, 20, 512], BF16)) for i in range(2)]
            G = [st.enter_context(lb.tmp("G%d" % i, [128, 4, 512], BF16)) for i in range(2)]
            mm_ = [st.enter_context(lb.tmp("mm%d" % i, [128, 512], F32)) for i in range(4)]
            ss_ = [st.enter_context(lb.tmp("ss%d" % i, [128, 512], F32)) for i in range(2)]
            mo = [st.enter_context(lb.tmp("mo%d" % i, [128, 512], BF16)) for i in range(2)]
            load_res(ys, ysT, 20, "ys")
            gT4 = gT.rearrange("(b c) p t -> b c p t", b=4)
            kcs = [(0, 4), (4, 8), (8, 12), (12, 20)]
            it = 0
            def issue_wm(mb_):
                for b_, (ka, kb_) in enumerate(kcs):
                    kb.dma("sp", wm[mb_ % 2][:, ka:kb_, :], wc_outs[b_].blk(mb_).rearrange("(k p) m -> p k m", p=128),
                           reads=[wc_outs[b_].res(mb_)], writes=[R(tag, "wm", mb_ % 2)])

            issue_wm(0)
            for mb in range(4):
                wi = mb % 2
                rw = R(tag, "wm", wi)
                if mb + 1 < 4:
                    issue_wm(mb + 1)
                for (lo, tt, n) in t3:
                    for j in range(4):
                        c = mb * 4 + j
                        i = it % 2
                        it += 1
                        rG = R(tag, "G", i)
                        kb.dma("sp", G[i][:, :, 0:n], gT4[:, c, :, tt:tt + n].rearrange("b p t -> p b t"), writes=[rG])
                        rm = [R(tag, "mm", b_) for b_ in range(4)]
                        for b_, (ka, kb_) in enumerate(kcs):
                            bank = b_ + 4 * i
                            kb.pe_group([MM(PS[bank][:, 0:n], wm[wi][:, k, j * 128:(j + 1) * 128], ys[:, k, lo:lo + n], k == ka, k == kb_ - 1)
                                         for k in range(ka, kb_)], reads=[rw, R(tag, "ys", lo)], writes=[RPS[bank]])
                            kb.op("dve", lambda e_: e_.tensor_tensor(mm_[b_][:, 0:n], PS[bank][:, 0:n], G[i][:, b_, 0:n], ALU.mult),
                                  reads=[RPS[bank], rG], writes=[rm[b_]])
                        rs0, rs1, rmo = R(tag, "ss", 0), R(tag, "ss", 1), R(tag, "mo", i)
                        kb.op("pool", lambda e_: e_.tensor_tensor(ss_[0][:, 0:n], mm_[0][:, 0:n], mm_[1][:, 0:n], ALU.add),
                              reads=[rm[0], rm[1]], writes=[rs0])
                        kb.op("pool", lambda e_: e_.tensor_tensor(ss_[1][:, 0:n], mm_[2][:, 0:n], mm_[3][:, 0:n], ALU.add),
                              reads=[rm[2], rm[3]], writes=[rs1])
                        kb.op("pool", lambda e_: e_.tensor_tensor(mo[i][:, 0:n], ss_[0][:, 0:n], ss_[1][:, 0:n], ALU.add),
                              reads=[rs0, rs1], writes=[rmo])
                        kb.dma("sp", mgT[c, :, tt:tt + n], mo[i][:, 0:n], reads=[rmo], writes=[R(tag, "mg", c, tt)])
            kb.barrier(skip_queues=("pool",))

        with contextlib.ExitStack() as st:
            mg = st.enter_context(lb.tmp("mg", [128, KC, GN], BF16))
            hbt = [st.enter_context(lb.tmp("hbt%d" % i, [128, 512], F32)) for i in range(4)]
            hot = [st.enter_context(lb.tmp("hot%d" % i, [128, 512], F32)) for i in range(4)]
            load_res(mg, mgT, KC, "mg2")
            cnt = {"e": 0}

            def epi_o(c, lo, tt, n, b):
                e = cnt["e"] % 4
                cnt["e"] += 1
                v = 1 if isctx[(lo, tt)] else 0
                rh, ro = R(tag, "hbt", e), R(tag, "hot", e)
                kb.dma("sp", hbt[e][:, 0:n], hT[c, :, tt:tt + n], writes=[rh])
                kb.op("dve", lambda e_: e_.scalar_tensor_tensor(hot[e][:, 0:n], PS[b][:, 0:n], modsb[:, 32 + c, v:v + 1],
                                                                 hbt[e][:, 0:n], ALU.mult, ALU.add), reads=[RPS[b], rh, rc], writes=[ro])
                kb.dma("sp", hmT[c, :, tt:tt + n], hot[e][:, 0:n], reads=[ro], writes=[R(tag, "hm", c, tt)])

            sweep(lb, st, mg, lambda lo: R(tag, "mg2", lo), KC, wc_o, [(i, [0, 1, 2, 3]) for i in range(4)], t3, epi_o, "wO")
            kb.barrier(skip_queues=("pool",))

        with contextlib.ExitStack() as st:
            xn2 = st.enter_context(lb.tmp("xn2", [128, KC, GN], BF16))
            norm_mod(lb, st, hmT, segs, xn2, A2, modsb, 48, xn2T, tag + "n2")
            kb.barrier(skip_queues=("pool",))
    return lb.finish()


def na_tables(cfg, rpb_l):
    specs = na_row_specs(cfg)
    ROWS = cfg.ROWS
    col = np.arange(GRID_W)
    c0 = np.clip(col - 8, 0, GRID_W - 16)
    outs = []
    for c in range(NCORE):
        base = c * cfg.TLR
        tabs = []
        for (slot, lr, k0, nk) in specs["slots"]:
            tab = np.full((4, nk, 64, 64), NEG, np.float32)
            lrs = specs["mid_lr"] if lr is None else lr
            kk0 = lrs - 4 if lr is None else k0
            r = base + lrs
            r0 = min(max(r - 4, 0), ROWS - 8)
            for aa in range(nk):
                kr = base + kk0 + aa
                if kr < r0 or kr >= r0 + 8 or kr < 0 or kr >= ROWS:
                    continue
                dr = kr - r + 7
                for qc in range(64):
                    kcs = np.arange(c0[qc], c0[qc] + 16)
                    tab[:, aa, kcs, qc] = rpb_l[:, dr, kcs - qc + 15]
            tabs.append(tab.transpose(2, 0, 1, 3).reshape(64, 4, nk * 64))
        outs.append(np.ascontiguousarray(np.concatenate(tabs, -1)))
    return outs


def run_C(inp, cfg, l, hT, mods, hc, rA, fyT):
    f32 = np.float32
    TL, TT, S, NW = cfg.TL, cfg.TT, cfg.S, cfg.NW
    zc_lat = np.concatenate([np.asarray(rA[c]["zc"])[:, :, :TL] for c in range(NCORE)], 2)
    zpad = np.pad(zc_lat[[0, 1, 2, 3, 8, 9, 10, 11]], ((0, 0), (0, 0), (1, 1)))
    kT_lat = np.concatenate([np.asarray(rA[c]["kT"])[:, :, :TL] for c in range(NCORE)], 2)
    kTall = np.ascontiguousarray(np.concatenate([kT_lat, np.asarray(rA[0]["kT"])[:, :, TL:]], 2))
    vgall = np.ascontiguousarray(np.concatenate([np.asarray(rA[c]["vg"])[:TL] for c in range(NCORE)] + [np.asarray(rA[0]["vg"])[TL:]], 0))
    kn_lat = np.concatenate([np.asarray(rA[c]["knT"])[:, :, :TL] for c in range(NCORE)], 2)
    kn_pad = np.pad(kn_lat, ((0, 0), (0, 0), (256, 192)))
    vn_lat = np.concatenate([np.asarray(rA[c]["vn"])[:TL] for c in range(NCORE)], 0)
    vn_pad = np.pad(vn_lat, ((256, 192), (0, 0)))
    cnkT = np.ascontiguousarray(np.asarray(rA[0]["knT"])[:, :, TL:])
    cvn = np.ascontiguousarray(np.asarray(rA[0]["vn"])[TL:])
    tabs = na_tables(cfg, np.asarray(inp["na_rpb"], f32)[l])
    conv_w = np.ascontiguousarray(np.asarray(inp["conv_w"], f32)[l].reshape(3, 4, 128).transpose(2, 0, 1))
    wts = {k: np.ascontiguousarray(np.asarray(inp[k], f32)[l]) for k in
           ("w_gate", "w_conv_out", "w_fourier_out", "w_na_out", "w_gqa_out", "w_o")}
    n2 = pl(inp["norm2"][l], KC)
    bg = pl(inp["b_gate"][l], 64)
    maps = []
    for c in range(NCORE):
        m = {"hT": hT[c], "xnT": np.asarray(rA[c]["xnT"]), "zc": np.asarray(rA[c]["zc"]),
             "zce": np.ascontiguousarray(zpad[:, :, c * TL:c * TL + TL + 2]), "zfc": np.asarray(rA[c]["zfc"]),
             "qT": np.asarray(rA[c]["qT"]), "nqT": np.asarray(rA[c]["nqT"]), "kTall": kTall, "vgall": vgall,
             "knw": np.ascontiguousarray(kn_pad[:, :, c * TL:c * TL + NW]),
             "vnw": np.ascontiguousarray(vn_pad[c * TL:c * TL + NW]), "cnkT": cnkT, "cvn": cvn, "natab": tabs[c],
             "fyT": fyT[c], "mods": mods[l], "norm2": n2, "b_gate": bg, "conv_w": conv_w, "f_cs": hc["f_cs"],
             "f_ctx": hc["f_ctx"]}
        m.update(wts)
        maps.append(m)
    return launch(get_prog("C", cfg, build_C), maps)


def build_D(cfg):
    lb = LB()
    nc, kb, R, PS, RPS = lb.nc, lb.kb, lb.R, lb.PS, lb.RPS
    TL, TT, GL, GN = cfg.TL, cfg.TT, cfg.GL, cfg.GN
    NX0, NX1 = GL + 2 + CTX + 2, GL + 2
    hmT = lb.din("hmT", [KC, 128, TT])
    xg = [lb.din("xg0", [KC, 128, NX0], BF16), lb.din("xg1", [KC, 128, NX1], BF16)]
    fcw_in = lb.din("ffn_cw", [128, 3, 88])
    mods_in = lb.din("mods", [128, 96, 2])
    w_up = lb.din("w_up", [D, 11264])
    w_down = lb.din("w_down", [5632, D])
    hTo = lb.dout("hTo", [KC, 128, TT], F32)
    actT = lb.dscr("actT", [44, 128, TT], BF16)
    wc_up = WConv(lb, "w_up", w_up, D, 11264, 512)
    wc_down = WConv(lb, "w_down", w_down, 5632, D, 256)
    modsb = lb.const("modsb", mods_in, [128, 96, 2])
    fcw = lb.const("fcw", fcw_in, [128, 3, 88])
    rc = lb.rc
    for gi, segs in enumerate(groups(cfg)):
        tag = "g%d" % gi
        NX = NX0 if gi == 0 else NX1
        segD = []
        off = 0
        for sg in segs:
            segD.append((off, sg.n, sg.tt, sg.is_ctx))
            off += sg.n + 2
        ctiles = []
        for (o, n, tt, c_) in segD:
            for (a, s_) in split_cols(n + 2):
                ctiles.append((o + a, s_))
        with contextlib.ExitStack() as st:
            xin = st.enter_context(lb.tmp("xin", [128, KC, NX], BF16))
            ua = [st.enter_context(lb.tmp("ua%d" % i, [128, NX], F32)) for i in range(2)]
            ug = [st.enter_context(lb.tmp("ug%d" % i, [128, NX], F32)) for i in range(2)]
            ca = [st.enter_context(lb.tmp("ca%d" % i, [128, GL], F32)) for i in range(2)]
            cg = [st.enter_context(lb.tmp("cg%d" % i, [128, GL], F32)) for i in range(2)]
            sil = st.enter_context(lb.tmp("sil", [128, GL], F32))
            ptmp = st.enter_context(lb.tmp("ptmp", [128, GL], F32))
            ab = [st.enter_context(lb.tmp("ab%d" % i, [128, GL], BF16)) for i in range(2)]
            wblk = [st.enter_context(lb.tmp("wu%d" % i, [128, KC, 2, 512], BF16)) for i in range(2)]
            rx = R(tag, "xin")
            for (o, s_) in split_cols(NX, 512):
                kb.dma("sp", xin[:, :, o:o + s_], xg[gi][:, :, o:o + s_].rearrange("k p t -> p k t"), writes=[rx])
            ei = 0
            si = 0
            def issue_wu(pb_):
                for part in range(2):
                    kb.dma("sp", wblk[pb_ % 2][:, :, part, :], wc_up.blk(part * 11 + pb_).rearrange("(k p) m -> p k m", p=128),
                           reads=[wc_up.res(part * 11 + pb_)], writes=[R(tag, "wu", pb_ % 2)])

            for q in range(2):
                wc_up.issue(q)
                wc_up.issue(11 + q)
            issue_wu(0)
            for pb in range(11):
                wi = pb % 2
                rw = R(tag, "wu", wi)
                if pb + 2 < 11:
                    wc_up.issue(pb + 2)
                    wc_up.issue(11 + pb + 2)
                else:
                    for q in range(4 * (pb - 9), 4 * (pb - 9) + 4):
                        wc_down.issue(q)
                if pb + 1 < 11:
                    issue_wu(pb + 1)
                for jj in range(4):
                    j = 4 * pb + jj
                    ui = j % 2
                    rua, rug = R(tag, "ua", ui), R(tag, "ug", ui)
                    for (co, s_) in ctiles:
                        for part, (dstb, rd) in enumerate(((ua[ui], rua), (ug[ui], rug))):
                            b = lb.nb(0, 8)
                            kb.pe_group([MM(PS[b][:, 0:s_], wblk[wi][:, k, part, jj * 128:(jj + 1) * 128], xin[:, k, co:co + s_],
                                            k == 0, k == KC - 1) for k in range(KC)], reads=[rw, rx], writes=[RPS[b]])
                            evac(lb, b, s_, dstb[:, co:co + s_], rd, ei)
                            ei += 1
                    for (o, n, tt, c_) in segD:
                        i = si % 2
                        si += 1
                        rca, rcg, rsl, rab = R(tag, "ca", i), R(tag, "cg", i), R(tag, "sil"), R(tag, "ab", i)
                        src, dst, rs_, rd, ch = ua[ui], ca[i], rua, rca, j
                        kb.op("dve", lambda e_: e_.tensor_scalar(dst[:, 0:n], src[:, o:o + n], fcw[:, 0, ch:ch + 1], None, ALU.mult),
                              reads=[rs_, rc], writes=[rd])
                        kb.op("dve", lambda e_: e_.scalar_tensor_tensor(dst[:, 0:n], src[:, o + 1:o + 1 + n], fcw[:, 1, ch:ch + 1],
                                                                         dst[:, 0:n], ALU.mult, ALU.add), reads=[rs_, rc, rd], writes=[rd])
                        kb.op("dve", lambda e_: e_.scalar_tensor_tensor(dst[:, 0:n], src[:, o + 2:o + 2 + n], fcw[:, 2, ch:ch + 1],
                                                                         dst[:, 0:n], ALU.mult, ALU.add), reads=[rs_, rc, rd], writes=[rd])
                        src, dst, rs_, rd, ch = ug[ui], cg[i], rug, rcg, 44 + j
                        rtp = R(tag, "ptmp")
                        kb.op("pool", lambda e_: e_.tensor_scalar(dst[:, 0:n], src[:, o:o + n], fcw[:, 0, ch:ch + 1], None, ALU.mult),
                              reads=[rs_, rc], writes=[rd])
                        for tap in (1, 2):
                            kb.op("pool", lambda e_: e_.tensor_scalar(ptmp[:, 0:n], src[:, o + tap:o + tap + n], fcw[:, tap, ch:ch + 1], None, ALU.mult),
                                  reads=[rs_, rc], writes=[rtp])
                            kb.op("pool", lambda e_: e_.tensor_tensor(dst[:, 0:n], dst[:, 0:n], ptmp[:, 0:n], ALU.add),
                                  reads=[rtp, rd], writes=[rd])
                        kb.op("act", lambda e_: e_.activation(sil[:, 0:n], ca[i][:, 0:n], AF.Silu), reads=[rca], writes=[rsl])
                        kb.op("pool", lambda e_: e_.tensor_tensor(ab[i][:, 0:n], sil[:, 0:n], cg[i][:, 0:n], ALU.mult),
                              reads=[rsl, rcg], writes=[rab])
                        kb.dma("sp", actT[j, :, tt:tt + n], ab[i][:, 0:n], reads=[rab], writes=[R(tag, "act", j, tt)])
            kb.barrier(skip_queues=("pool",))
        tiles = []
        for sg in segs:
            for t in sg.tiles():
                tiles.append(t + (sg.is_ctx,))
        t3 = [t[:3] for t in tiles]
        isctx = {(lo, tt): c for (lo, tt, n, c) in tiles}
        with contextlib.ExitStack() as st:
            act = st.enter_context(lb.tmp("act", [128, 44, GN], BF16))
            hbt = [st.enter_context(lb.tmp("hbt%d" % i, [128, 512], F32)) for i in range(4)]
            hot = [st.enter_context(lb.tmp("hot%d" % i, [128, 512], F32)) for i in range(4)]
            for (lo, tt, n) in t3:
                for k0 in range(0, 44, 11):
                    kb.dma("sp", act[:, k0:k0 + 11, lo:lo + n], actT[k0:k0 + 11, :, tt:tt + n].rearrange("k p t -> p k t"),
                           writes=[R(tag, "actr", lo)])
            cnt = {"e": 0}

            def epi_d(c, lo, tt, n, b):
                e = cnt["e"] % 4
                cnt["e"] += 1
                v = 1 if isctx[(lo, tt)] else 0
                rh, ro = R(tag, "hbt", e), R(tag, "hot", e)
                kb.dma("sp", hbt[e][:, 0:n], hmT[c, :, tt:tt + n], writes=[rh])
                kb.op("dve", lambda e_: e_.scalar_tensor_tensor(hot[e][:, 0:n], PS[b][:, 0:n], modsb[:, 80 + c, v:v + 1],
                                                                 hbt[e][:, 0:n], ALU.mult, ALU.add), reads=[RPS[b], rh, rc], writes=[ro])
                kb.dma("sp", hTo[c, :, tt:tt + n], hot[e][:, 0:n], reads=[ro], writes=[R(tag, "ho", c, tt)])

            sweep(lb, st, act, lambda lo: R(tag, "actr", lo), 44, wc_down, [(i, [0, 1]) for i in range(8)], t3, epi_d, "wD")
            kb.barrier(skip_queues=("pool",))
    return lb.finish()


def run_D(inp, cfg, l, mods, rC):
    f32 = np.float32
    TL, GL = cfg.TL, cfg.GL
    x2_lat = np.concatenate([np.asarray(rC[c]["xn2T"])[:, :, :TL] for c in range(NCORE)], 2)
    x2p = np.pad(x2_lat, ((0, 0), (0, 0), (1, 1)))
    fcw = np.ascontiguousarray(np.asarray(inp["ffn_conv_w"], f32)[l].reshape(3, 88, 128).transpose(2, 0, 1))
    w_up = np.ascontiguousarray(np.asarray(inp["w_up"], f32)[l])
    w_down = np.ascontiguousarray(np.asarray(inp["w_down"], f32)[l])
    maps = []
    for c in range(NCORE):
        cx = np.pad(np.asarray(rC[c]["xn2T"])[:, :, TL:], ((0, 0), (0, 0), (1, 1)))
        xg0 = np.ascontiguousarray(np.concatenate([x2p[:, :, c * TL:c * TL + GL + 2], cx], 2))
        xg1 = np.ascontiguousarray(x2p[:, :, c * TL + GL:c * TL + 2 * GL + 2])
        maps.append({"hmT": np.asarray(rC[c]["hmT"]), "xg0": xg0, "xg1": xg1, "ffn_cw": fcw, "mods": mods[l],
                     "w_up": w_up, "w_down": w_down})
    return launch(get_prog("D", cfg, build_D), maps)


def run_model(inp, cfg, verbose=False):
    import time
    hc = host_consts(cfg)
    TL = cfg.TL
    t0 = time.time()
    mods = run_M(inp, cfg)
    x = np.asarray(inp["x"], np.float32)[0]
    ctx = np.asarray(inp["ctx"], np.float32)[0]
    ctx_fm = to_fm(ctx)
    hT = [np.ascontiguousarray(np.concatenate([to_fm(x[c * TL:(c + 1) * TL]), ctx_fm], 2)) for c in range(NCORE)]
    for l in range(cfg.L):
        rA = run_A(inp, cfg, l, hT, mods, hc)
        if verbose:
            print("layer", l, "A done", time.time() - t0, flush=True)
        fyT = run_B(cfg, rA, hc)
        rC = run_C(inp, cfg, l, hT, mods, hc, rA, fyT)
        if verbose:
            print("layer", l, "C done", time.time() - t0, flush=True)
        del rA
        rD = run_D(inp, cfg, l, mods, rC)
        del rC
        hT = [np.asarray(rD[c]["hTo"]) for c in range(NCORE)]
        if verbose:
            print("layer", l, "D done", time.time() - t0, flush=True)
    out = np.concatenate([from_fm(hT[c][:, :, :TL]) for c in range(NCORE)], 0)
    return np.ascontiguousarray(out[None].astype(np.float32))


def kernel(**inputs):
    cfg = Cfg(16384, 4)
    return run_model(inputs, cfg)
```
